# Optimizing a Trainium2 kernel written in Bass

```python
import jax, jax.numpy as jnp
from jax import lax
import numpy as np

D_MODEL = 1024
BATCH = 8
SEQ = 4096
DEPTH = 2

HEAD_DIM = 64
MIX_WIDTH = D_MODEL
FOX_HEADS = MIX_WIDTH // (4 * HEAD_DIM)
MOBA_HEADS = MIX_WIDTH // (4 * HEAD_DIM)
RWKV_HEADS = MIX_WIDTH // (2 * HEAD_DIM)
FOX_WIDTH = FOX_HEADS * HEAD_DIM
MOBA_WIDTH = MOBA_HEADS * HEAD_DIM
RWKV_WIDTH = RWKV_HEADS * HEAD_DIM
RWKV_DECAY_LORA = 64
RWKV_ICLR_LORA = 64
RWKV_GATE_LORA = 160
RWKV_GN_EPS = 64e-5
FOX_IN = 3 * FOX_WIDTH + FOX_HEADS
MOBA_IN = 3 * MOBA_WIDTH
RWKV_IN = 3 * RWKV_WIDTH + RWKV_DECAY_LORA + RWKV_ICLR_LORA + RWKV_GATE_LORA
IN_WIDTH = FOX_IN + MOBA_IN + RWKV_IN
FOX_QBLOCK = 128
MOBA_BLOCK = 256
MOBA_TOPK = 3
MOBA_QCHUNK = 32
ROPE_THETA = 500000.0
ROPE_DIM = HEAD_DIM // 4
FFN_DIM = 2816
MEM_LEN = 256
XATTN_HEADS = 4
XATTN_HEAD_DIM = 128
XATTN_WIDTH = XATTN_HEADS * XATTN_HEAD_DIM
NORM_EPS = 1e-6

kernel_name = 'hybrid_fox_rwkv7_moba_macaron'


def rms_norm(t, g):
    tf = t.astype(jnp.float32)
    y = tf * lax.rsqrt(jnp.mean(tf * tf, axis=-1, keepdims=True) + NORM_EPS)
    return (y * g).astype(t.dtype)


def swiglu(h, w_in, w_out):
    gate, up = jnp.split(h @ w_in, 2, axis=-1)
    return (jax.nn.silu(gate) * up) @ w_out


def split_heads(t, n_heads):
    B, S, _ = t.shape
    return t.reshape(B, S, n_heads, HEAD_DIM).transpose(0, 2, 1, 3)


def merge_heads(t):
    B, H, S, d = t.shape
    return t.transpose(0, 2, 1, 3).reshape(B, S, H * d)


def rope_tables(positions):
    inv_freq = ROPE_THETA ** (-jnp.arange(0, ROPE_DIM, 2, dtype=jnp.float32) / ROPE_DIM)
    ang = positions.astype(jnp.float32)[..., None] * inv_freq
    return jnp.cos(ang), jnp.sin(ang)


def apply_partial_rope(t, cos, sin):
    half = ROPE_DIM // 2
    x1, x2, rest = t[..., :half], t[..., half:ROPE_DIM], t[..., ROPE_DIM:]
    c, s = cos[:, None], sin[:, None]
    out = jnp.concatenate([x1 * c - x2 * s, x2 * c + x1 * s, rest.astype(jnp.float32)], axis=-1)
    return out.astype(t.dtype)


def forgetting_attention(q, k, v, log_f):
    B, H, S, d = q.shape
    c = jnp.cumsum(log_f, axis=-1)
    n_qb = S // FOX_QBLOCK
    q_blocks = q.reshape(B, H, n_qb, FOX_QBLOCK, d).transpose(2, 0, 1, 3, 4)
    c_blocks = c.reshape(B, H, n_qb, FOX_QBLOCK).transpose(2, 0, 1, 3)
    k_pos = jnp.arange(S)
    scale = d ** -0.5

    def block(args):
        i, q_i, c_i = args
        s = jnp.einsum('bhqd,bhkd->bhqk', q_i, k).astype(jnp.float32) * scale
        s = s + c_i[..., None] - c[:, :, None, :]
        q_pos = i * FOX_QBLOCK + jnp.arange(FOX_QBLOCK)
        s = jnp.where(k_pos[None, :] <= q_pos[:, None], s, -jnp.inf)
        p = jax.nn.softmax(s, axis=-1).astype(v.dtype)
        return jnp.einsum('bhqk,bhkd->bhqd', p, v)

    out = lax.map(block, (jnp.arange(n_qb), q_blocks, c_blocks))
    return out.transpose(1, 2, 0, 3, 4).reshape(B, H, S, d)


def fox_mixer(z, f_bias, q_gain, k_gain):
    q, k, v, f = jnp.split(z, [FOX_WIDTH, 2 * FOX_WIDTH, 3 * FOX_WIDTH], axis=-1)
    q = rms_norm(split_heads(q, FOX_HEADS), q_gain)
    k = rms_norm(split_heads(k, FOX_HEADS), k_gain)
    v = split_heads(v, FOX_HEADS)
    log_f = jax.nn.log_sigmoid(f.astype(jnp.float32) + f_bias).transpose(0, 2, 1)
    return merge_heads(forgetting_attention(q, k, v, log_f))


def moba_attention(q, k, v):
    B, H, S, d = q.shape
    s_pad = -(-S // MOBA_BLOCK) * MOBA_BLOCK
    pad = ((0, 0), (0, 0), (0, s_pad - S), (0, 0))
    k_pad = jnp.pad(k, pad)
    v_pad = jnp.pad(v, pad)
    n_blk = s_pad // MOBA_BLOCK
    top_k = min(MOBA_TOPK, n_blk)
    k_blk = k_pad.reshape(B, H, n_blk, MOBA_BLOCK, d)
    v_blk = v_pad.reshape(B, H, n_blk, MOBA_BLOCK, d)
    k_mean = jnp.mean(k_blk.astype(jnp.float32), axis=3)
    n_chunk = S // MOBA_QCHUNK
    q_chunks = q.reshape(B, H, n_chunk, MOBA_QCHUNK, d).transpose(2, 0, 1, 3, 4)
    b_idx = jnp.arange(B)[:, None, None, None]
    h_idx = jnp.arange(H)[None, :, None, None]
    blk_ids = jnp.arange(n_blk)
    scale = d ** -0.5

    def chunk(args):
        i, q_i = args
        q_start = i * MOBA_QCHUNK
        q_blk = q_start // MOBA_BLOCK
        gate = jnp.einsum('bhqd,bhnd->bhqn', q_i.astype(jnp.float32), k_mean)
        gate = jnp.where(blk_ids < q_blk, gate, -jnp.inf)
        _, sel = lax.top_k(gate, top_k)
        valid = sel < q_blk
        k_sel = k_blk[b_idx, h_idx, sel]
        v_sel = v_blk[b_idx, h_idx, sel]
        s_sel = jnp.einsum('bhqd,bhqnkd->bhqnk', q_i, k_sel).astype(jnp.float32) * scale
        s_sel = jnp.where(valid[..., None], s_sel, -jnp.inf)
        s_sel = s_sel.reshape(B, H, MOBA_QCHUNK, top_k * MOBA_BLOCK)
        blk_start = q_blk * MOBA_BLOCK
        k_own = lax.dynamic_slice_in_dim(k_pad, blk_start, MOBA_BLOCK, axis=2)
        v_own = lax.dynamic_slice_in_dim(v_pad, blk_start, MOBA_BLOCK, axis=2)
        s_own = jnp.einsum('bhqd,bhkd->bhqk', q_i, k_own).astype(jnp.float32) * scale
        q_pos = q_start + jnp.arange(MOBA_QCHUNK)
        k_pos = blk_start + jnp.arange(MOBA_BLOCK)
        s_own = jnp.where(k_pos[None, :] <= q_pos[:, None], s_own, -jnp.inf)
        p = jax.nn.softmax(jnp.concatenate([s_sel, s_own], axis=-1), axis=-1).astype(v.dtype)
        p_sel = p[..., :top_k * MOBA_BLOCK].reshape(B, H, MOBA_QCHUNK, top_k, MOBA_BLOCK)
        p_own = p[..., top_k * MOBA_BLOCK:]
        return (jnp.einsum('bhqnk,bhqnkd->bhqd', p_sel, v_sel)
                + jnp.einsum('bhqk,bhkd->bhqd', p_own, v_own))

    out = lax.map(chunk, (jnp.arange(n_chunk), q_chunks))
    return out.transpose(1, 2, 0, 3, 4).reshape(B, H, S, d)


def moba_mixer(z, cos, sin, q_gain, k_gain):
    q, k, v = jnp.split(z, [MOBA_WIDTH, 2 * MOBA_WIDTH], axis=-1)
    q = apply_partial_rope(rms_norm(split_heads(q, MOBA_HEADS), q_gain), cos, sin)
    k = apply_partial_rope(rms_norm(split_heads(k, MOBA_HEADS), k_gain), cos, sin)
    v = split_heads(v, MOBA_HEADS)
    return merge_heads(moba_attention(q, k, v))


def rwkv7_mixer(z, mu, w0, w2, a0, a2, g2, k_k, k_a, r_k, ln_w, ln_b):
    B, S, _ = z.shape
    zf = z.astype(jnp.float32)
    z_prev = jnp.pad(zf, ((0, 0), (1, 0), (0, 0)))[:, :S]
    zf = zf + (z_prev - zf) * mu
    W = RWKV_WIDTH
    r, k, v, w_lo, a_lo, g_lo = jnp.split(
        zf, [W, 2 * W, 3 * W, 3 * W + RWKV_DECAY_LORA, 3 * W + RWKV_DECAY_LORA + RWKV_ICLR_LORA], axis=-1)
    w_raw = w0 + jnp.tanh(w_lo) @ w2
    decay = jnp.exp(-jnp.exp(-jax.nn.softplus(-w_raw) - 0.5))
    a = jax.nn.sigmoid(a0 + a_lo @ a2)
    g = jax.nn.sigmoid(g_lo) @ g2
    kk = k * k_k
    k = k * (1.0 + (a - 1.0) * k_a)

    def hd(t):
        return t.reshape(B, S, RWKV_HEADS, HEAD_DIM)

    r, k, v, decay, a, kk = hd(r), hd(k), hd(v), hd(decay), hd(a), hd(kk)
    kk = kk / jnp.maximum(jnp.sqrt(jnp.sum(kk * kk, axis=-1, keepdims=True)), 1e-12)

    def step(state, inp):
        r_t, w_t, k_t, v_t, kk_t, a_t = inp
        sa = jnp.einsum('bhij,bhj->bhi', state, -kk_t)
        state = (state * w_t[:, :, None, :] + sa[..., None] * (kk_t * a_t)[:, :, None, :]
                 + v_t[..., None] * k_t[:, :, None, :])
        return state, jnp.einsum('bhij,bhj->bhi', state, r_t)

    xs = tuple(t.transpose(1, 0, 2, 3) for t in (r, decay, k, v, kk, a))
    state0 = jnp.zeros((B, RWKV_HEADS, HEAD_DIM, HEAD_DIM), jnp.float32)
    _, ys = lax.scan(step, state0, xs)
    y = ys.transpose(1, 0, 2, 3)
    mean = jnp.mean(y, axis=-1, keepdims=True)
    var = jnp.mean(jnp.square(y - mean), axis=-1, keepdims=True)
    y = ((y - mean) * lax.rsqrt(var + RWKV_GN_EPS)).reshape(B, S, W) * ln_w + ln_b
    bonus = (jnp.sum(r * k * r_k, axis=-1, keepdims=True) * v).reshape(B, S, W)
    return ((y + bonus) * g).astype(z.dtype)


def memory_cross_attention(h, m, w_q, w_kv, q_gain, k_gain, w_o):
    B, S, _ = h.shape
    M = m.shape[1]
    q = rms_norm((h @ w_q).reshape(B, S, XATTN_HEADS, XATTN_HEAD_DIM), q_gain)
    kv = (m @ w_kv).reshape(B, M, 2, XATTN_HEADS, XATTN_HEAD_DIM)
    k = rms_norm(kv[:, :, 0], k_gain)
    v = kv[:, :, 1]
    s = jnp.einsum('bqhd,bkhd->bhqk', q, k).astype(jnp.float32) * (XATTN_HEAD_DIM ** -0.5)
    p = jax.nn.softmax(s, axis=-1).astype(v.dtype)
    o = jnp.einsum('bhqk,bkhd->bqhd', p, v).reshape(B, S, XATTN_WIDTH)
    return o @ w_o


def setup_inputs(seed: int = 0) -> dict:
    key = jax.random.key(seed)
    ks = iter(jax.random.split(key, 64))
    L, D = DEPTH, D_MODEL

    def nrm(shape, scale):
        return jax.random.normal(next(ks), shape, jnp.float32) * scale

    def gain(shape):
        return 1.0 + nrm(shape, 0.02)

    x = nrm((BATCH, SEQ, D), 1.0)
    mem = nrm((BATCH, MEM_LEN, D), 1.0)
    offset = jax.random.randint(next(ks), (BATCH, 1), 0, SEQ, dtype=jnp.int32)
    positions = (offset + jnp.arange(SEQ, dtype=jnp.int32)[None, :]).astype(jnp.int32)
    return {
        'x': x, 'mem': mem, 'positions': positions,
        'ffn1_norm': gain((L, D)),
        'ffn1_w_in': nrm((L, D, 2 * FFN_DIM), D ** -0.5),
        'ffn1_w_out': nrm((L, FFN_DIM, D), FFN_DIM ** -0.5),
        'mix_norm': gain((L, D)),
        'mix_w_in': nrm((L, D, IN_WIDTH), D ** -0.5),
        'mix_w_out': nrm((L, MIX_WIDTH, D), MIX_WIDTH ** -0.5),
        'fox_f_bias': 3.0 + nrm((L, FOX_HEADS), 0.5),
        'fox_q_gain': gain((L, HEAD_DIM)),
        'fox_k_gain': gain((L, HEAD_DIM)),
        'moba_q_gain': gain((L, HEAD_DIM)),
        'moba_k_gain': gain((L, HEAD_DIM)),
        'rwkv_mu': jax.random.uniform(next(ks), (L, RWKV_IN), jnp.float32, 0.0, 1.0),
        'rwkv_w0': nrm((L, RWKV_WIDTH), 1.0) - 1.0,
        'rwkv_w2': nrm((L, RWKV_DECAY_LORA, RWKV_WIDTH), RWKV_DECAY_LORA ** -0.5),
        'rwkv_a0': nrm((L, RWKV_WIDTH), 0.5),
        'rwkv_a2': nrm((L, RWKV_ICLR_LORA, RWKV_WIDTH), RWKV_ICLR_LORA ** -0.5),
        'rwkv_g2': nrm((L, RWKV_GATE_LORA, RWKV_WIDTH), RWKV_GATE_LORA ** -0.5),
        'rwkv_k_k': 0.85 + nrm((L, RWKV_WIDTH), 0.05),
        'rwkv_k_a': 1.0 + nrm((L, RWKV_WIDTH), 0.05),
        'rwkv_r_k': nrm((L, RWKV_HEADS, HEAD_DIM), 0.1),
        'rwkv_ln_w': gain((L, RWKV_WIDTH)),
        'rwkv_ln_b': nrm((L, RWKV_WIDTH), 0.02),
        'xattn_norm': gain((L, D)),
        'xattn_mem_norm': gain((L, D)),
        'xattn_w_q': nrm((L, D, XATTN_WIDTH), D ** -0.5),
        'xattn_w_kv': nrm((L, D, 2 * XATTN_WIDTH), D ** -0.5),
        'xattn_q_gain': gain((L, XATTN_HEAD_DIM)),
        'xattn_k_gain': gain((L, XATTN_HEAD_DIM)),
        'xattn_w_out': nrm((L, XATTN_WIDTH, D), XATTN_WIDTH ** -0.5),
        'ffn2_norm': gain((L, D)),
        'ffn2_w_in': nrm((L, D, 2 * FFN_DIM), D ** -0.5),
        'ffn2_w_out': nrm((L, FFN_DIM, D), FFN_DIM ** -0.5),
    }


def reference(x, mem, positions, ffn1_norm, ffn1_w_in, ffn1_w_out, mix_norm, mix_w_in, mix_w_out,
              fox_f_bias, fox_q_gain, fox_k_gain, moba_q_gain, moba_k_gain,
              rwkv_mu, rwkv_w0, rwkv_w2, rwkv_a0, rwkv_a2, rwkv_g2, rwkv_k_k, rwkv_k_a, rwkv_r_k,
              rwkv_ln_w, rwkv_ln_b, xattn_norm, xattn_mem_norm, xattn_w_q, xattn_w_kv,
              xattn_q_gain, xattn_k_gain, xattn_w_out, ffn2_norm, ffn2_w_in, ffn2_w_out):
    cos, sin = rope_tables(positions)
    for l in range(DEPTH):
        x = x + 0.5 * swiglu(rms_norm(x, ffn1_norm[l]), ffn1_w_in[l], ffn1_w_out[l]).astype(x.dtype)
        z = rms_norm(x, mix_norm[l]) @ mix_w_in[l]
        z_fox, z_moba, z_rwkv = jnp.split(z, [FOX_IN, FOX_IN + MOBA_IN], axis=-1)
        y_fox = fox_mixer(z_fox, fox_f_bias[l], fox_q_gain[l], fox_k_gain[l])
        y_rwkv = rwkv7_mixer(z_rwkv, rwkv_mu[l], rwkv_w0[l], rwkv_w2[l], rwkv_a0[l], rwkv_a2[l],
                             rwkv_g2[l], rwkv_k_k[l], rwkv_k_a[l], rwkv_r_k[l], rwkv_ln_w[l], rwkv_ln_b[l])
        y_moba = moba_mixer(z_moba, cos, sin, moba_q_gain[l], moba_k_gain[l])
        y = jnp.concatenate([y_fox, y_rwkv, y_moba], axis=-1) @ mix_w_out[l]
        x = x + y.astype(x.dtype)
        x = x + memory_cross_attention(rms_norm(x, xattn_norm[l]), rms_norm(mem, xattn_mem_norm[l]),
                                       xattn_w_q[l], xattn_w_kv[l], xattn_q_gain[l], xattn_k_gain[l],
                                       xattn_w_out[l]).astype(x.dtype)
        x = x + 0.5 * swiglu(rms_norm(x, ffn2_norm[l]), ffn2_w_in[l], ffn2_w_out[l]).astype(x.dtype)
    return x
```

```python
import numpy as np
from contextlib import ExitStack
import concourse.bass as bass
import concourse.mybir as mybir
from concourse.bass_utils import run_bass_kernel_spmd

F32 = mybir.dt.float32
BF16 = mybir.dt.bfloat16
I32 = mybir.dt.int32
AF = mybir.ActivationFunctionType
ALU = mybir.AluOpType
AX = mybir.AxisListType

D = 1024
S = 4096
NT = S // 128
L = 2
FFN = 2816
NF = FFN // 128
HD = 64
MEM = 256
EPS = 1e-6
IN_W = 3364
FOX_IN = 772
MOBA_IN = 768
RW0 = FOX_IN + MOBA_IN
NEG = -30000.0


class Tk:
    __slots__ = ("name", "lw", "rd", "excl")

    def __init__(self, name="", excl=False):
        self.name = name
        self.lw = None
        self.rd = []
        self.excl = excl


class Ins:
    __slots__ = ("eng", "fn", "deps", "idx", "mark", "cnt", "dma", "dsem", "dval", "waits")


class Prog:
    ENGS = ("pe", "dve", "act", "pool", "sp")
    NDMA = 6

    def __init__(self, nc):
        self.nc = nc
        self.ins = {e: [] for e in self.ENGS}
        self.order = []
        self.dma_slot_last = {}
        self.dma_cnt = {e: 0 for e in self.ENGS}
        self.dma_uses = {}
        self.pending_barrier = {e: None for e in self.ENGS}
        self.pe_mode = None

    def add(self, eng, fn, reads=(), writes=(), dma=False, mode=None):
        if eng == "pe":
            if mode is None:
                mode = (128, 128)
            mode = tuple(32 if v <= 32 else (64 if v <= 64 else 128) for v in mode)
            if self.pe_mode is not None and mode != self.pe_mode:
                self.pe_mode = mode
                self.add("pe", lambda e: e.drain(), (), (), mode=mode)
            self.pe_mode = mode
        I = Ins()
        I.eng = eng
        I.fn = fn
        I.dma = dma
        I.mark = False
        I.cnt = 0
        I.waits = []
        deps = []
        if any(t.excl for t in reads):
            writes = tuple(writes) + tuple(t for t in reads if t.excl and t not in writes)
            reads = tuple(t for t in reads if not t.excl)
        for t in reads:
            if t.lw is not None:
                deps.append(t.lw)
        for t in writes:
            if t.lw is not None:
                deps.append(t.lw)
            deps.extend(t.rd)
        if self.pending_barrier[eng] is not None:
            deps.extend(self.pending_barrier[eng])
            self.pending_barrier[eng] = None
        if dma:
            k = self.dma_cnt[eng] % self.NDMA
            self.dma_cnt[eng] += 1
            key = (eng, k)
            prev = self.dma_slot_last.get(key)
            if prev is not None:
                deps.append(prev)
            self.dma_slot_last[key] = I
            self.dma_uses[key] = self.dma_uses.get(key, 0) + 1
            I.dsem = key
            I.dval = 16 * self.dma_uses[key]
        I.deps = deps
        I.idx = len(self.ins[eng])
        self.ins[eng].append(I)
        self.order.append(I)
        for t in reads:
            t.rd.append(I)
        for t in writes:
            t.lw = I
            t.rd = []
        return I

    def barrier(self):
        last = []
        for e in self.ENGS:
            if self.ins[e]:
                last.append(self.ins[e][-1])
        for key, I in self.dma_slot_last.items():
            last.append(I)
        for e in self.ENGS:
            self.pending_barrier[e] = list(last)

    def finish(self, es):
        nc = self.nc
        self.barrier()
        self.add("sp", lambda e: e.nop(), ())
        waited = {e: {} for e in self.ENGS}
        for I in self.order:
            E = I.eng
            w = waited[E]
            for d in I.deps:
                if d.dma:
                    key = ("dma",) + d.dsem
                    if w.get(key, 0) >= d.dval:
                        continue
                    w[key] = d.dval
                    I.waits.append(d)
                else:
                    if d.eng == E and E in ("pe", "sp"):
                        continue
                    key = d.eng
                    if w.get(key, -1) >= d.idx:
                        continue
                    w[key] = d.idx
                    d.mark = True
                    I.waits.append(d)
        sems = {}
        for e in self.ENGS:
            sems[e] = es.enter_context(nc.semaphore("sem_" + e))
            c = 0
            for I in self.ins[e]:
                if I.mark and not I.dma:
                    c += 1
                    I.cnt = c
        dsems = {}
        for key in self.dma_uses:
            dsems[key] = es.enter_context(nc.semaphore("dsem_%s_%d" % key))

        def emit(eng_name, e):
            for I in self.ins[eng_name]:
                for d in I.waits:
                    if d.dma:
                        e.wait_ge(dsems[d.dsem], d.dval)
                    else:
                        e.wait_ge(sems[d.eng], d.cnt)
                r = I.fn(e)
                if I.dma:
                    r.then_inc(dsems[I.dsem], 16)
                elif I.mark:
                    r.then_inc(sems[eng_name], 1)

        with nc.Block() as block:
            @block.sync
            def _(e):
                emit("sp", e)

            @block.tensor
            def _(e):
                emit("pe", e)

            @block.vector
            def _(e):
                emit("dve", e)

            @block.scalar
            def _(e):
                emit("act", e)

            @block.gpsimd
            def _(e):
                emit("pool", e)


class Ctx:
    pass


def build(nc, cfg):
    P = Prog(nc)
    C = Ctx()
    C.nc = nc
    C.P = P
    C.cfg = cfg
    es_top = ExitStack()
    C.es = es_top
    C.uid = 0

    def sb(name, shape, dt):
        C.uid += 1
        return nc.sbuf_tensor("%s_%d" % (name, C.uid), shape, dt)
    C.sb = sb

    def din(name, shape, dt=F32):
        return nc.dram_tensor(name, list(shape), dt, kind="ExternalInput").ap()

    C.x_in = din("x", [S, D])
    C.mem = din("mem", [MEM, D])
    C.pos = din("positions", [1, S], I32)
    names = {
        "ffn1_norm": [L, D], "ffn1_w_in": [L, D, 2 * FFN], "ffn1_w_out": [L, FFN, D],
        "mix_norm": [L, D], "mix_w_in": [L, D, IN_W], "mix_w_out": [L, D, D],
        "fox_f_bias": [L, 4], "fox_q_gain": [L, HD], "fox_k_gain": [L, HD],
        "moba_q_gain": [L, HD], "moba_k_gain": [L, HD],
        "rwkv_mu": [L, 1824], "rwkv_w0": [L, 512], "rwkv_w2": [L, 64, 512], "rwkv_a0": [L, 512],
        "rwkv_a2": [L, 64, 512], "rwkv_g2": [L, 160, 512], "rwkv_k_k": [L, 512], "rwkv_k_a": [L, 512],
        "rwkv_r_k": [L, 512], "rwkv_ln_w": [L, 512], "rwkv_ln_b": [L, 512],
        "xattn_norm": [L, D], "xattn_mem_norm": [L, D], "xattn_w_q": [L, D, 512], "xattn_w_kv": [L, D, 1024],
        "xattn_q_gain": [L, 128], "xattn_k_gain": [L, 128], "xattn_w_out": [L, 512, D],
        "ffn2_norm": [L, D], "ffn2_w_in": [L, D, 2 * FFN], "ffn2_w_out": [L, FFN, D],
    }
    LW = cfg.get("LW", L)
    C.w = {k: din(k, [LW] + list(v[1:])) for k, v in names.items()}
    C.out = nc.dram_tensor("out", [S, D], F32, kind="ExternalOutput").ap()
    C.xtk = [Tk("x%d" % i) for i in range(NT)]

    es = es_top
    C.ident = es.enter_context(C.sb("ident", [128, 128], BF16))
    C.identf = es.enter_context(C.sb("identf", [128, 128], F32))
    C.t_const = Tk("const")
    C.banks = [es.enter_context(nc.psum_tensor("bank%d" % i, [128, 512], F32)) for i in range(8)]
    C.bk = [Tk("bank%d" % i, excl=True) for i in range(8)]
    P.add("pool", lambda e: e.memset(C.identf[:], 1.0), (), (C.t_const,))
    P.add("pool", lambda e: e.affine_select(out=C.identf[:], in_=C.identf[:], pattern=[[-1, 128]],
                                            compare_op=ALU.is_equal, fill=0.0, base=0, channel_multiplier=1),
          (), (C.t_const,))
    P.add("pool", lambda e: e.tensor_copy(out=C.ident[:], in_=C.identf[:]), (), (C.t_const,))

    setup_globals(C)
    src = C.x_in
    plan = cfg.get("plan")
    if plan is None:
        plan = []
        stop = cfg.get("stop_after")
        skip = cfg.get("skip", ())
        for l in range(L):
            for ph in ("ffn1", "mix", "xattn", "ffn2"):
                if ph not in skip:
                    plan.append((ph, l))
                if stop == (ph, l):
                    break
            else:
                continue
            break
    for ph, l in plan:
        if ph in ("ffn1", "ffn2"):
            ffn_phase(C, l, ph, src)
            src = C.out
        elif ph == "mix":
            mixer_phase(C, l, src)
            src = C.out
        elif ph == "rwkv":
            rwkv_phase(C, l, src, C.yr, C.yr_tk)
        elif ph == "attn":
            mixer_phase(C, l, src, do_rwkv=False)
            src = C.out
        elif ph == "xattn":
            xattn_phase(C, l, src)
            src = C.out
    P.finish(es_top)
    return nc


def load_cast(C, dst, dst_tk, w2d, K, N, stg, stg_tk, piece=2048, engs=("dve", "act")):
    P = C.P
    nchunk = K // 128
    i = 0
    for c in range(nchunk):
        for n0 in range(0, N, piece):
            n1 = min(N, n0 + piece)
            sb = i % len(stg)
            s_ap = stg[sb][:, 0:n1 - n0]
            P.add("sp", (lambda s_ap=s_ap, c=c, n0=n0, n1=n1: lambda e: e.dma_start(
                out=s_ap, in_=w2d[c * 128:(c + 1) * 128, n0:n1]))(), (), (stg_tk[sb],), dma=True)
            eng = engs[i % len(engs)]
            if eng == "act":
                P.add(eng, (lambda s_ap=s_ap, c=c, n0=n0, n1=n1: lambda e: e.copy(
                    out=dst[:, c, n0:n1], in_=s_ap))(), (stg_tk[sb],), (dst_tk,))
            else:
                P.add(eng, (lambda s_ap=s_ap, c=c, n0=n0, n1=n1: lambda e: e.tensor_copy(
                    out=dst[:, c, n0:n1], in_=s_ap))(), (stg_tk[sb],), (dst_tk,))
            i += 1


def bcast_load(C, dst, dst_tk, row_ap, n):
    C.P.add("sp", lambda e: e.dma_start(out=dst, in_=row_ap.partition_broadcast(128)), (), (dst_tk,), dma=True)


def norm_transpose(C, xt, xt_tk, gain_bc, gain_tk, hT, hT_tk, col0, scr, bank, nchunks=8):
    P = C.P
    W = 128 * nchunks
    P.add("act", lambda e: e.activation(out=scr["junk"][:, 0:W], in_=xt, func=AF.Square, accum_out=scr["ss"][:]),
          (xt_tk,), (scr["junk_tk"], scr["ss_tk"]))
    P.add("act", lambda e: e.activation(out=scr["std"][:], in_=scr["ss"][:], func=AF.Sqrt, scale=1.0 / W, bias=scr["eps"][:]),
          (scr["ss_tk"], C.t_const), (scr["std_tk"],))
    P.add("dve", lambda e: e.reciprocal(out=scr["rstd"][:], in_=scr["std"][:]), (scr["std_tk"],), (scr["rstd_tk"],))
    P.add("dve", lambda e: e.scalar_tensor_tensor(out=scr["hb"][:, 0:W], in0=xt, scalar=scr["rstd"][:], in1=gain_bc,
                                                  op0=ALU.mult, op1=ALU.mult),
          (xt_tk, scr["rstd_tk"], gain_tk), (scr["hb_tk"],))
    pb = C.banks[bank].bitcast(BF16)
    for c in range(nchunks):
        P.add("pe", (lambda c=c: lambda e: e.transpose(out=pb[:, c * 128:(c + 1) * 128], in_=scr["hb"][:, c * 128:(c + 1) * 128],
                                                       identity=C.ident[:]))(), (scr["hb_tk"], C.t_const), (C.bk[bank],))
    P.add("act", lambda e: e.copy(out=hT[:, 0:nchunks, col0:col0 + 128],
                                  in_=pb[:, 0:W].rearrange("p (c t) -> p c t", c=nchunks)),
          (C.bk[bank],), (hT_tk,))


def make_norm_scratch(C, es, tag):
    nc = C.nc
    scr = {}
    scr["junk"] = es.enter_context(C.sb(tag + "junk", [128, D], BF16))
    scr["hb"] = es.enter_context(C.sb(tag + "hb", [128, D], BF16))
    for k in ("ss", "std", "rstd", "eps"):
        scr[k] = es.enter_context(C.sb(tag + k, [128, 1], F32))
    for k in ("junk", "hb", "ss", "std", "rstd"):
        scr[k + "_tk"] = Tk(tag + k)
    C.P.add("pool", lambda e: e.memset(scr["eps"][:], EPS), (), (C.t_const,))
    return scr


def ffn_phase(C, l, name, src):
    nc, P = C.nc, C.P
    TB = 512
    NTB = TB // 128
    NB = S // TB
    with ExitStack() as es:
        win = es.enter_context(C.sb(name + "win", [128, 8, 2 * FFN], BF16))
        wout = es.enter_context(C.sb(name + "wout", [128, NF, D], BF16))
        win_tk, wout_tk, gbc_tk, hT_tk, aT_tk = Tk("win"), Tk("wout"), Tk("gbc"), Tk("hT"), Tk("aT")
        with ExitStack() as es2:
            stg = [es2.enter_context(C.sb(name + "stg%d" % i, [128, 2816], F32)) for i in range(3)]
            stg_tk = [Tk("stg0"), Tk("stg1"), Tk("stg2")]
            load_cast(C, win, win_tk, C.w[name + "_w_in"][l], D, 2 * FFN, stg, stg_tk, piece=2816)
            load_cast(C, wout, wout_tk, C.w[name + "_w_out"][l], FFN, D, stg, stg_tk, piece=1024)
            P.barrier()
        gbc = es.enter_context(C.sb(name + "gbc", [128, D], F32))
        xt = [es.enter_context(C.sb(name + "xt%d" % i, [128, D], F32)) for i in range(NTB)]
        hT = es.enter_context(C.sb(name + "hT", [128, 8, TB], BF16))
        aT = es.enter_context(C.sb(name + "aT", [128, NF, TB], BF16))
        sg = [es.enter_context(C.sb(name + "sg%d" % i, [128, TB], F32)) for i in range(2)]
        scr = make_norm_scratch(C, es, name)
        xt_tk = [Tk("xt%d" % i) for i in range(NTB)]
        sg_tk = [Tk("sg0"), Tk("sg1")]
        bcast_load(C, gbc[:], gbc_tk, C.w[name + "_norm"][l:l + 1, :], D)
        for b in range(NB):
            for j in range(NTB):
                ti = b * NTB + j
                P.add("sp", (lambda j=j, ti=ti: lambda e: e.dma_start(out=xt[j][:], in_=src[ti * 128:(ti + 1) * 128, :]))(),
                      (C.xtk[ti],), (xt_tk[j],), dma=True)
                norm_transpose(C, xt[j][:], xt_tk[j], gbc[:], gbc_tk, hT, hT_tk, j * 128, scr, 0)
            for f in range(NF):
                bg, bu = 1 + (f % 2), 3 + (f % 2)
                for (bank, col) in ((bg, f * 128), (bu, FFN + f * 128)):
                    for c in range(8):
                        P.add("pe", (lambda bank=bank, col=col, c=c: lambda e: e.matmul(
                            C.banks[bank][:, 0:TB], lhsT=win[:, c, col:col + 128], rhs=hT[:, c, :],
                            start=(c == 0), stop=(c == 7)))(), (win_tk, hT_tk), (C.bk[bank],))
                k = f % 2
                P.add("act", (lambda bg=bg, k=k: lambda e: e.activation(out=sg[k][:], in_=C.banks[bg][:, 0:TB], func=AF.Silu))(),
                      (C.bk[bg],), (sg_tk[k],))
                P.add("dve", (lambda bu=bu, k=k, f=f: lambda e: e.tensor_tensor(out=aT[:, f, :], in0=C.banks[bu][:, 0:TB],
                                                                              in1=sg[k][:], op=ALU.mult))(),
                      (C.bk[bu], sg_tk[k]), (aT_tk,))
            for j in range(NTB):
                ti = b * NTB + j
                for half in range(2):
                    bank = 5 + half
                    for f in range(NF):
                        P.add("pe", (lambda bank=bank, f=f, j=j, half=half: lambda e: e.matmul(
                            C.banks[bank][:, :], lhsT=aT[:, f, j * 128:(j + 1) * 128], rhs=wout[:, f, half * 512:(half + 1) * 512],
                            start=(f == 0), stop=(f == NF - 1)))(), (aT_tk, wout_tk), (C.bk[bank],))
                    P.add("dve", (lambda bank=bank, j=j, half=half: lambda e: e.scalar_tensor_tensor(
                        out=xt[j][:, half * 512:(half + 1) * 512], in0=C.banks[bank][:, :], scalar=0.5,
                        in1=xt[j][:, half * 512:(half + 1) * 512], op0=ALU.mult, op1=ALU.add))(),
                        (C.bk[bank], xt_tk[j]), (xt_tk[j],))
                P.add("sp", (lambda j=j, ti=ti: lambda e: e.dma_start(out=C.out[ti * 128:(ti + 1) * 128, :], in_=xt[j][:]))(),
                      (xt_tk[j],), (C.xtk[ti],), dma=True)
        P.barrier()


TWO_PI = 6.283185307179586
C1 = 6.28125
C2 = TWO_PI - C1
MAGIC = 12582912.0
INVF = [float(np.float32(500000.0) ** (-np.float32(2 * i) / np.float32(16.0))) for i in range(8)]


def cconst(C, val):
    key = float(val)
    if key not in C.consts:
        t = C.es.enter_context(C.sb("c%d" % len(C.consts), [128, 1], F32))
        C.P.add("pool", lambda e: e.memset(t[:], key), (), (C.t_const,))
        C.consts[key] = t
    return C.consts[key]


def setup_globals(C):
    nc, P, es = C.nc, C.P, C.es
    C.consts = {}
    C.tri = es.enter_context(C.sb("tri", [128, 128], BF16))
    C.trif = es.enter_context(C.sb("trif", [128, 128], F32))
    C.onesf = es.enter_context(C.sb("onesf", [128, 128], F32))
    C.onesb = es.enter_context(C.sb("onesb", [128, 512], BF16))
    C.cos = es.enter_context(C.sb("cos", [128, NT, 8], F32))
    C.sin = es.enter_context(C.sb("sin", [128, NT, 8], F32))
    tc_ = (C.t_const,)
    P.add("pool", lambda e: e.memset(C.onesf[:], 1.0), (), tc_)
    P.add("pool", lambda e: e.memset(C.onesb[:], 1.0), (), tc_)
    P.add("pool", lambda e: e.affine_select(out=C.trif[:], in_=C.onesf[:], pattern=[[1, 128]], compare_op=ALU.is_ge,
                                            fill=0.0, base=0, channel_multiplier=-1), tc_, tc_)
    P.add("pool", lambda e: e.tensor_copy(out=C.tri[:], in_=C.trif[:]), tc_, tc_)
    for v in (EPS, 1.0, 0.0, np.pi / 2, 64e-5):
        cconst(C, v)
    C.qaf = [nc.dram_tensor("qaf%d" % h, [66, S], BF16).ap() for h in range(4)]
    C.kaf = [nc.dram_tensor("kaf%d" % h, [66, S], BF16).ap() for h in range(4)]
    C.qam = [nc.dram_tensor("qam%d" % h, [80, S], BF16).ap() for h in range(4)]
    C.kam = [nc.dram_tensor("kam%d" % h, [80, S], BF16).ap() for h in range(4)]
    yk = C.cfg.get("yr_kind")
    C.yr = (nc.dram_tensor("yr", [S, 512], BF16, kind=yk) if yk else nc.dram_tensor("yr", [S, 512], BF16)).ap()
    C.yr_tk = Tk("yr")
    C.qa_tk = {("f", h): Tk() for h in range(4)}
    C.qa_tk.update({("m", h): Tk() for h in range(4)})
    C.ka_tk = {("f", h): Tk() for h in range(4)}
    C.ka_tk.update({("m", h): Tk() for h in range(4)})
    with ExitStack() as s3:
        ohf = s3.enter_context(C.sb("oh_full", [16, S], BF16))
        onf = s3.enter_context(C.sb("ones_full", [16, S], BF16))
        tko = Tk("ohfull")
        P.add("pool", lambda e: e.memset(onf[:], 1.0), (), (tko,))
        P.add("pool", lambda e: e.affine_select(out=ohf[:], in_=onf[:], pattern=[[1, S]], compare_op=ALU.is_ge, fill=0.0,
                                                base=0, channel_multiplier=-256), (tko,), (tko,))
        P.add("pool", lambda e: e.affine_select(out=ohf[:], in_=ohf[:], pattern=[[-1, S]], compare_op=ALU.is_ge, fill=0.0,
                                                base=255, channel_multiplier=256), (tko,), (tko,))
        for h in range(4):
            P.add("sp", (lambda h=h: lambda e: e.dma_start(out=C.kam[h][64:80, :], in_=ohf[:]))(), (tko,), (C.ka_tk[("m", h)],), dma=True)
            P.add("sp", (lambda h=h: lambda e: e.dma_start(out=C.kaf[h][64:66, :], in_=onf[0:2, :]))(), (tko,), (C.ka_tk[("f", h)],), dma=True)
        P.barrier()
    with ExitStack() as s2:
        posi = s2.enter_context(C.sb("posi", [32, 128], I32))
        posf = s2.enter_context(C.sb("posf", [32, 128], F32))
        posT = s2.enter_context(C.sb("posT", [128, NT], F32))
        ang = s2.enter_context(C.sb("ang", [128, NT, 8], F32))
        t1 = s2.enter_context(C.sb("rp1", [128, NT * 8], F32))
        t2 = s2.enter_context(C.sb("rp2", [128, NT * 8], F32))
        r = s2.enter_context(C.sb("rpr", [128, NT * 8], F32))
        tk = Tk("rope")
        P.add("sp", lambda e: e.dma_start(out=posi[:], in_=C.pos.rearrange("o (j p) -> (o j) p", p=128)), (), (tk,), dma=True)
        P.add("dve", lambda e: e.tensor_copy(out=posf[:], in_=posi[:]), (tk,), (tk,))
        P.add("pe", lambda e: e.transpose(out=C.banks[0][:, 0:32], in_=posf[:], identity=C.identf[0:32, 0:32]),
              (tk, C.t_const), (C.bk[0],), mode=(32, 128))
        P.add("dve", lambda e: e.tensor_copy(out=posT[:], in_=C.banks[0][:, 0:32]), (C.bk[0],), (tk,))
        for i in range(8):
            P.add("dve", (lambda i=i: lambda e: e.tensor_scalar(out=ang[:, :, i], in0=posT[:], scalar1=INVF[i], scalar2=None,
                                                                op0=ALU.mult))(), (tk,), (tk,))
        af = ang[:].rearrange("p j i -> p (j i)")
        P.add("dve", lambda e: e.tensor_scalar(out=t1[:], in0=af, scalar1=1.0 / TWO_PI, scalar2=MAGIC, op0=ALU.mult, op1=ALU.add), (tk,), (tk,))
        P.add("dve", lambda e: e.tensor_scalar(out=t2[:], in0=t1[:], scalar1=-MAGIC, scalar2=None, op0=ALU.add), (tk,), (tk,))
        P.add("dve", lambda e: e.scalar_tensor_tensor(out=r[:], in0=t2[:], scalar=-C1, in1=af, op0=ALU.mult, op1=ALU.add), (tk,), (tk,))
        P.add("dve", lambda e: e.scalar_tensor_tensor(out=r[:], in0=t2[:], scalar=-C2, in1=r[:], op0=ALU.mult, op1=ALU.add), (tk,), (tk,))
        P.add("dve", lambda e: e.tensor_scalar(out=t1[:], in0=r[:], scalar1=float(np.pi), scalar2=-TWO_PI, op0=ALU.is_gt, op1=ALU.mult), (tk,), (tk,))
        P.add("dve", lambda e: e.tensor_tensor(out=r[:], in0=r[:], in1=t1[:], op=ALU.add), (tk,), (tk,))
        P.add("dve", lambda e: e.tensor_scalar(out=t1[:], in0=r[:], scalar1=-float(np.pi), scalar2=TWO_PI, op0=ALU.is_lt, op1=ALU.mult), (tk,), (tk,))
        P.add("dve", lambda e: e.tensor_tensor(out=r[:], in0=r[:], in1=t1[:], op=ALU.add), (tk,), (tk,))
        P.add("dve", lambda e: e.tensor_scalar(out=r[:], in0=r[:], scalar1=3.14159, scalar2=-3.14159, op0=ALU.min, op1=ALU.max), (tk,), (tk,))
        P.add("act", lambda e: e.activation(out=C.sin[:].rearrange("p j i -> p (j i)"), in_=r[:], func=AF.Sin), (tk,), tc_)
        P.add("act", lambda e: e.activation(out=t1[:], in_=r[:], func=AF.Abs), (tk,), (tk,))
        P.add("act", lambda e: e.activation(out=C.cos[:].rearrange("p j i -> p (j i)"), in_=t1[:], func=AF.Sin, scale=-1.0,
                                            bias=cconst(C, np.pi / 2)[:]), (tk, C.t_const), tc_)
        P.barrier()


def qk_norm(C, src_psum, ngrp, gains, sq, ssq, std, rstd, dst, tk, eps_ap):
    P = C.P
    W = ngrp * 64
    src_tk, sq_tk, st_tk, dst_tk, g_tk = tk
    P.add("act", lambda e: e.activation(out=sq[:, 0:W], in_=src_psum, func=AF.Square), (src_tk,), (sq_tk,))
    P.add("dve", lambda e: e.tensor_reduce(out=ssq[:, 0:ngrp], in_=sq[:, 0:W].rearrange("p (g d) -> p g d", d=64), axis=AX.X, op=ALU.add),
          (sq_tk,), (st_tk,))
    P.add("act", lambda e: e.activation(out=std[:, 0:ngrp], in_=ssq[:, 0:ngrp], func=AF.Sqrt, scale=1.0 / 64, bias=eps_ap),
          (st_tk, C.t_const), (st_tk,))
    P.add("dve", lambda e: e.reciprocal(out=rstd[:, 0:ngrp], in_=std[:, 0:ngrp]), (st_tk,), (st_tk,))
    P.add("dve", lambda e: e.tensor_tensor(out=dst[:, 0:W].rearrange("p (g d) -> p g d", d=64),
                                           in0=src_psum.rearrange("p (g d) -> p g d", d=64),
                                           in1=rstd[:, 0:ngrp].unsqueeze(2).broadcast_to([128, ngrp, 64]), op=ALU.mult),
          (src_tk, st_tk), (dst_tk,))
    P.add("dve", lambda e: e.tensor_tensor(out=dst[:, 0:W], in0=dst[:, 0:W], in1=gains[:, 0:W], op=ALU.mult), (dst_tk, g_tk), (dst_tk,))


def mixer_phase(C, l, src, do_rwkv=True):
    nc, P = C.nc, C.P
    if do_rwkv and "rwkv" not in C.cfg.get("skip", ()):
        rwkv_phase(C, l, src, C.yr, C.yr_tk)
    with ExitStack() as es:
        ymix = es.enter_context(C.sb("ymix", [128, NT, D], BF16))
        ymix_tk = [Tk("ymix%d" % i) for i in range(NT)]
        if "attn" not in C.cfg.get("skip", ()):
            with ExitStack() as es2:
                vaug = {k: es2.enter_context(C.sb("vaug" + k, [128, NT, 4, 65], BF16)) for k in ("f", "m")}
                vaug_tk = {k: Tk("vaug" + k) for k in ("f", "m")}
                cneg = es2.enter_context(C.sb("cneg", [128, NT, 4], F32))
                cneg_tk = Tk("cneg")
                mixer_prep_attn(C, l, src, vaug, vaug_tk, cneg, cneg_tk)
                P.barrier()
                attn_heads(C, l, vaug, vaug_tk, cneg, cneg_tk, ymix, ymix_tk)
                P.barrier()
        else:
            P.add("pool", lambda e: e.memset(ymix[:, :, 0:256], 0.0), (), tuple(ymix_tk))
            P.add("pool", lambda e: e.memset(ymix[:, :, 768:1024], 0.0), (), tuple(ymix_tk))
        if C.cfg.get("no_outproj"):
            return
        if "rwkv" not in C.cfg.get("skip", ()):
            for i in range(NT):
                P.add("sp", (lambda i=i: lambda e: e.dma_start(out=ymix[:, i, 256:768], in_=C.yr[i * 128:(i + 1) * 128, :]))(),
                      (C.yr_tk,), (ymix_tk[i],), dma=True)
        else:
            P.add("pool", lambda e: e.memset(ymix[:, :, 256:768], 0.0), (), tuple(ymix_tk))
        P.barrier()
        outproj_phase(C, l, src, ymix, ymix_tk)
        P.barrier()


def mixer_prep_attn(C, l, src, vaug, vaug_tk, cneg, cneg_tk):
    nc, P = C.nc, C.P
    NW = RW0
    with ExitStack() as es:
        A = lambda name, shape, dt: es.enter_context(C.sb("mp_" + name, shape, dt))
        win = A("win", [128, 8, NW], BF16)
        stg = [A("stg%d" % i, [128, NW], F32) for i in range(2)]
        gbc = A("gbc", [128, D], F32)
        xt = [A("xt%d" % i, [128, D], F32) for i in range(2)]
        gains = {k: A("g" + k, [128, 512], F32) for k in ("f", "m")}
        fbias = A("fbias", [128, 4], F32)
        stT = A("stT", [128, 8, 512], BF16)
        stM = A("stM", [64, 512], BF16)
        kmT = [A("kmT%d" % i, [128, 16], BF16) for i in range(2)]
        kms = A("kms", [128, 2, 2], F32)
        carry = A("carry", [128, 4], F32)
        cT = A("cT", [4, 512], F32)
        chi = A("chi", [4, 512], BF16)
        chf = A("chf", [4, 512], F32)
        clo = A("clo", [4, 512], BF16)
        oh = A("oh", [16, 512], BF16)
        tks = {n: Tk(n) for n in ("win", "gbc", "hT", "gf", "gm", "fbias", "sq", "st", "qknf", "qknm", "qkb", "rtmp", "stT", "stM",
                                  "km", "gt", "sel", "mbt", "f", "carry", "cT", "chi", "oh")}
        stg_tk = [Tk(), Tk()]
        xt_tk = [Tk(), Tk()]
        eps_ap = cconst(C, EPS)[:, 0:1]
        one_ap = cconst(C, 1.0)[:, 0:1]
        bcast_load(C, gbc[:], tks["gbc"], C.w["mix_norm"][l:l + 1, :], D)
        load_cast(C, win, tks["win"], C.w["mix_w_in"][l][:, 0:NW], D, NW, stg, stg_tk, piece=NW)
        for k, qn, kn in (("f", "fox_q_gain", "fox_k_gain"), ("m", "moba_q_gain", "moba_k_gain")):
            for g0, nm in ((0, qn), (4, kn)):
                P.add("sp", (lambda k=k, g0=g0, nm=nm: lambda e: e.dma_start(
                    out=gains[k][:, g0 * 64:(g0 + 4) * 64].rearrange("p (g d) -> p g d", d=64),
                    in_=C.w[nm][l:l + 1, :].partition_broadcast(128).broadcast_to([128, 4, 64])))(),
                    (), (tks["g" + k],), dma=True)
            P.add("dve", (lambda k=k: lambda e: e.tensor_scalar(out=gains[k][:, 0:256], in0=gains[k][:, 0:256], scalar1=0.125,
                                                                scalar2=None, op0=ALU.mult))(), (tks["g" + k],), (tks["g" + k],))
        P.add("sp", lambda e: e.dma_start(out=fbias[:], in_=C.w["fox_f_bias"][l:l + 1, :].partition_broadcast(128)), (), (tks["fbias"],), dma=True)
        P.add("pool", lambda e: e.memset(carry[:], 0.0), (), (tks["carry"],))
        for k in ("f", "m"):
            P.add("pool", (lambda k=k: lambda e: e.memset(vaug[k][:, :, :, 64:65], 1.0))(), (), (vaug_tk[k],))
        for i in range(2):
            P.add("pool", (lambda i=i: lambda e: e.memset(kmT[i][:], 0.0))(), (), (tks["km"],))
        P.barrier()
        O = Ops(C)
        dup = {}
        for par in range(2):
            d_ = {}
            d_["hT"] = A("hT_%d" % par, [128, 8, 128], BF16)
            d_["sq"] = A("sq_%d" % par, [128, 512], F32)
            for k_ in ("ssq", "std", "rstd"):
                d_[k_] = A("%s_%d" % (k_, par), [128, 8], F32)
            d_["qknf"] = A("qknf_%d" % par, [128, 512], F32)
            d_["qknm"] = A("qknm_%d" % par, [128, 512], F32)
            d_["qkb"] = A("qkb_%d" % par, [128, 1024], BF16)
            d_["rtmp"] = A("rtmp_%d" % par, [128, 4, 8, 8], F32)
            d_["gt"] = A("gt_%d" % par, [128, 4, 16], F32)
            d_["top8"] = A("top8_%d" % par, [128, 4, 8], F32)
            d_["selt"] = A("selt_%d" % par, [128, 4, 16], F32)
            d_["mbt"] = A("mbt_%d" % par, [128, 4, 16], BF16)
            d_["fb"] = A("fb_%d" % par, [128, 4], F32)
            d_["lf"] = A("lf_%d" % par, [128, 4], F32)
            d_["scr"] = make_norm_scratch(C, es, "mp%d" % par)
            d_["tk"] = {}
            dup[par] = d_

        def tile_gen(i):
            par = i % 2
            d_ = dup[par]
            B = 4 * par
            j = par
            g4 = i % 4
            qblk = i // 2

            def t(n):
                if n not in d_["tk"]:
                    d_["tk"][n] = Tk("%s_p%d" % (n, par))
                return d_["tk"][n]
            hT_, sq_, ssq_, std_, rstd_ = d_["hT"], d_["sq"], d_["ssq"], d_["std"], d_["rstd"]
            qkb_, rtmp_, gt_, top8_, selt_, mbt_, fb_, lf_ = d_["qkb"], d_["rtmp"], d_["gt"], d_["top8"], d_["selt"], d_["mbt"], d_["fb"], d_["lf"]
            O.dma(xt[j][:], src[i * 128:(i + 1) * 128, :], (C.xtk[i],), (xt_tk[j],))
            norm_transpose(C, xt[j][:], xt_tk[j], gbc[:], tks["gbc"], hT_, t("hT"), 0, d_["scr"], B + 0)
            yield
            for bank, c0, c1 in ((B + 1, 0, 512), (B + 2, 512, 772), (B + 3, 772, 1284), (B + 0, 1284, 1540)):
                for c in range(8):
                    O.mm(C.banks[bank][:, 0:c1 - c0], hT_[:, c, :], win[:, c, c0:c1], c == 0, c == 7, (t("hT"), tks["win"]), (C.bk[bank],))
                yield
            P.add("act", lambda e: e.copy(out=vaug["f"][:, i, :, 0:64], in_=C.banks[B + 2][:, 0:256].rearrange("p (h d) -> p h d", d=64)),
                  (C.bk[B + 2],), (vaug_tk["f"],))
            P.add("act", lambda e: e.copy(out=vaug["m"][:, i, :, 0:64], in_=C.banks[B + 0][:, 0:256].rearrange("p (h d) -> p h d", d=64)),
                  (C.bk[B + 0],), (vaug_tk["m"],))
            O.tt("dve", fb_[:], C.banks[B + 2][:, 256:260], fbias[:], ALU.add, (C.bk[B + 2], tks["fbias"]), (t("f"),))
            yield
            O.act(lf_[:], fb_[:], AF.Exp, (t("f"),), (t("f"),), scale=-1.0)
            O.act(lf_[:], lf_[:], AF.Ln, (t("f"), C.t_const), (t("f"),), bias=one_ap)
            yield
            for kk_, bnk in (("f", B + 1), ("m", B + 3)):
                qk_norm(C, C.banks[bnk][:, 0:512], 8, gains[kk_], sq_, ssq_, std_, rstd_, d_["qkn" + kk_],
                        (C.bk[bnk], t("sq"), t("st"), t("qkn" + kk_), tks["g" + kk_]), eps_ap)
                yield
            P.add("act", lambda e: e.copy(out=qkb_[:, 0:512], in_=d_["qknf"][:]), (t("qknf"),), (t("qkb"),))
            O.mm(C.banks[B + 0][:, 0:4], C.trif[:], lf_[:], True, True, (t("f"), C.t_const), (C.bk[B + 0],))
            O.mm(C.banks[B + 0][:, 4:8], C.onesf[:], lf_[:], True, True, (t("f"), C.t_const), (C.bk[B + 0],))
            O.tt("dve", cneg[:, i, :], C.banks[B + 0][:, 0:4], carry[:], ALU.add, (C.bk[B + 0], tks["carry"]), (cneg_tk,))
            O.tt("dve", carry[:], C.banks[B + 0][:, 4:8], carry[:], ALU.add, (C.bk[B + 0], tks["carry"]), (tks["carry"],))
            yield
            P.add("pe", lambda e: e.transpose(out=C.banks[B + 0][0:4, 128:256], in_=cneg[:, i, :], identity=C.identf[:]),
                  (cneg_tk, C.t_const), (C.bk[B + 0],), mode=(128, 4))
            O.act(cT[:, g4 * 128:(g4 + 1) * 128], C.banks[B + 0][0:4, 128:256], AF.Copy, (C.bk[B + 0],), (tks["cT"],), scale=-1.0)
            yield
            qv = d_["qknm"][:].rearrange("p (g d) -> p g d", d=64)
            x1, x2 = qv[:, :, 0:8], qv[:, :, 8:16]
            cb = C.cos[:, i:i + 1, :].broadcast_to([128, 8, 8])
            sb = C.sin[:, i:i + 1, :].broadcast_to([128, 8, 8])
            rt, qm = t("rtmp"), t("qknm")
            O.tt("dve", rtmp_[:, 0], x1, cb, ALU.mult, (qm, C.t_const), (rt,))
            O.tt("dve", rtmp_[:, 1], x2, sb, ALU.mult, (qm, C.t_const), (rt,))
            O.tt("dve", rtmp_[:, 2], x2, cb, ALU.mult, (qm, C.t_const), (rt,))
            O.tt("dve", rtmp_[:, 3], x1, sb, ALU.mult, (qm, C.t_const), (rt,))
            yield
            O.tt("dve", x1, rtmp_[:, 0], rtmp_[:, 1], ALU.subtract, (rt,), (qm,))
            O.tt("dve", x2, rtmp_[:, 2], rtmp_[:, 3], ALU.add, (rt,), (qm,))
            P.add("act", lambda e: e.copy(out=qkb_[:, 512:1024], in_=d_["qknm"][:]), (qm,), (t("qkb"),))
            yield
            pbq = C.banks[B + 1].bitcast(BF16)
            for blk in range(8):
                O.tr(pbq[:, blk * 128:(blk + 1) * 128], qkb_[:, blk * 128:(blk + 1) * 128], C.ident[:], (t("qkb"), C.t_const), (C.bk[B + 1],))
            P.add("act", lambda e: e.copy(out=stT[:, :, g4 * 128:(g4 + 1) * 128], in_=pbq[:, :].rearrange("p (b t) -> p b t", b=8)),
                  (C.bk[B + 1],), (tks["stT"],))
            yield
            if qblk > 0:
                for h in range(4):
                    pr = (h % 2) * 64
                    gb = B + 2 + (h % 2)
                    P.add("pe", (lambda h=h, pr=pr, gb=gb: lambda e: e.matmul(
                        C.banks[gb][:, h * 16:(h + 1) * 16], lhsT=stT[pr:pr + 64, 4 + h // 2, g4 * 128:(g4 + 1) * 128],
                        rhs=kmT[h // 2][pr:pr + 64, :], start=True, stop=True))(), (tks["stT"], tks["km"]), (C.bk[gb],), mode=(64, 128))
                for p2 in range(2):
                    P.add("dve", (lambda p2=p2: lambda e: e.tensor_copy(
                        out=gt_[:, p2:4:2, :], in_=C.banks[B + 2 + p2][:, 0:64].rearrange("p (h n) -> p h n", n=16)[:, p2:4:2, :]))(),
                        (C.bk[B + 2 + p2],), (t("gt"),))
                P.add("dve", lambda e: e.memset(gt_[:, :, qblk:16], -1e30), (), (t("gt"),))
                yield
                for h in range(4):
                    P.add("dve", (lambda h=h: lambda e: e.max(out=top8_[:, h, :], in_=gt_[:, h, :]))(), (t("gt"),), (t("sel"),))
                yield
                for h in range(4):
                    P.add("dve", (lambda h=h: lambda e: e.tensor_scalar(out=selt_[:, h, :], in0=gt_[:, h, :], scalar1=top8_[:, h, 2:3], scalar2=None,
                                                                        op0=ALU.is_ge))(), (t("gt"), t("sel")), (t("sel"),))
                yield
                O.ts("dve", mbt_[:], selt_[:], -NEG, ALU.mult, (t("sel"),), (t("mbt"),), s2=NEG, op1=ALU.add)
                if qblk < 15:
                    P.add("dve", lambda e: e.memset(mbt_[:, :, qblk + 1:16], NEG), (), (t("mbt"),))
                P.add("dve", lambda e: e.memset(mbt_[:, :, qblk:qblk + 1], 0.0), (), (t("mbt"),))
            else:
                P.add("dve", lambda e: e.memset(mbt_[:], NEG), (), (t("mbt"),))
                P.add("dve", lambda e: e.memset(mbt_[:, :, 0:1], 0.0), (), (t("mbt"),))
            yield
            pbm = C.banks[B + 3].bitcast(BF16)
            P.add("pe", lambda e: e.transpose(out=pbm[0:64, 0:128], in_=mbt_[:].rearrange("p h n -> p (h n)"), identity=C.ident[:]),
                  (t("mbt"), C.t_const), (C.bk[B + 3],), mode=(128, 64))
            P.add("act", lambda e: e.copy(out=stM[:, g4 * 128:(g4 + 1) * 128], in_=pbm[0:64, 0:128]), (C.bk[B + 3],), (tks["stM"],))
            yield

        for m in range(C.cfg.get("prep_tiles", NT) // 2):
            gens = [tile_gen(2 * m), tile_gen(2 * m + 1)]
            while gens:
                for g_ in list(gens):
                    try:
                        next(g_)
                    except StopIteration:
                        gens.remove(g_)
            i = 2 * m + 1
            g4 = i % 4
            qblk = m
            c0 = (g4 - 1) * 128
            for hp in range(2):
                P.add("dve", (lambda hp=hp, c0=c0: lambda e: e.tensor_reduce(out=kms[:, hp, 0:1], in_=stT[:, 6 + hp, c0:c0 + 256], axis=AX.X, op=ALU.add))(),
                      (tks["stT"],), (tks["km"],))
                P.add("dve", (lambda hp=hp, qblk=qblk: lambda e: e.tensor_scalar(out=kmT[hp][:, qblk:qblk + 1], in0=kms[:, hp, 0:1], scalar1=1.0 / 256,
                                                                                  scalar2=None, op0=ALU.mult))(), (tks["km"],), (tks["km"],))
            if g4 == 3:
                t0 = (i - 3) * 128
                P.add("dve", lambda e: e.tensor_copy(out=chi[:], in_=cT[:]), (tks["cT"],), (tks["chi"],))
                P.add("dve", lambda e: e.tensor_copy(out=chf[:], in_=chi[:]), (tks["chi"],), (tks["chi"],))
                P.add("dve", lambda e: e.tensor_tensor(out=clo[:], in0=cT[:], in1=chf[:], op=ALU.subtract), (tks["cT"], tks["chi"]), (tks["chi"],))
                for h in range(4):
                    P.add("sp", (lambda h=h, t0=t0: lambda e: e.dma_start(out=C.qaf[h][64:65, t0:t0 + 512], in_=chi[h:h + 1, :]))(),
                          (tks["chi"],), (C.qa_tk[("f", h)],), dma=True)
                    P.add("sp", (lambda h=h, t0=t0: lambda e: e.dma_start(out=C.qaf[h][65:66, t0:t0 + 512], in_=clo[h:h + 1, :]))(),
                          (tks["chi"],), (C.qa_tk[("f", h)],), dma=True)
                for h in range(4):
                    pr = (h % 2) * 64
                    for (dst, dtk, blk) in ((C.qaf[h], C.qa_tk[("f", h)], 0 + h // 2), (C.kaf[h], C.ka_tk[("f", h)], 2 + h // 2),
                                            (C.qam[h], C.qa_tk[("m", h)], 4 + h // 2), (C.kam[h], C.ka_tk[("m", h)], 6 + h // 2)):
                        P.add("sp", (lambda dst=dst, blk=blk, pr=pr, t0=t0: lambda e: e.dma_start(
                            out=dst[0:64, t0:t0 + 512], in_=stT[pr:pr + 64, blk, :]))(), (tks["stT"],), (dtk,), dma=True)
                    P.add("sp", (lambda h=h, t0=t0: lambda e: e.dma_start(out=C.qam[h][64:80, t0:t0 + 512], in_=stM[h * 16:(h + 1) * 16, :]))(),
                          (tks["stM"],), (C.qa_tk[("m", h)],), dma=True)


def attn_heads(C, l, vaug, vaug_tk, cneg, cneg_tk, ymix, ymix_tk):
    nc, P = C.nc, C.P
    with ExitStack() as es:
        A = lambda name, shape, dt: es.enter_context(C.sb("at_" + name, shape, dt))
        qa = [A("qa%d" % i, [80, S], BF16) for i in range(2)]
        ka = [A("ka%d" % i, [80, S], BF16) for i in range(2)]
        pt = [A("pt%d" % i, [128, 512], BF16) for i in range(4)]
        rec = A("rec", [128, 4], F32)
        qa_tk = [Tk(), Tk()]
        ka_tk = [Tk(), Tk()]
        pt_tk = [Tk(), Tk(), Tk(), Tk()]
        rec_tk = Tk()
        zero_ap = cconst(C, 0.0)[:, 0:1]
        hi = 0
        pti = 0
        blk = 0
        pending = []
        for kind, KA, ycol in (("f", 66, 0), ("m", 80, 768)):
            for h in range(C.cfg.get("attn_heads", 4)):
                b = hi % 2
                hi += 1
                qsrc = (C.qaf if kind == "f" else C.qam)[h]
                ksrc = (C.kaf if kind == "f" else C.kam)[h]
                P.add("sp", (lambda b=b, qsrc=qsrc, KA=KA: lambda e: e.dma_start(out=qa[b][0:KA, :], in_=qsrc[:, :]))(),
                      (C.qa_tk[(kind, h)],), (qa_tk[b],), dma=True)
                P.add("sp", (lambda b=b, ksrc=ksrc, KA=KA: lambda e: e.dma_start(out=ka[b][0:KA, :], in_=ksrc[:, :]))(),
                      (C.ka_tk[(kind, h)],), (ka_tk[b],), dma=True)
                for qb in range(8):
                    ob = 2 + (blk % 2)
                    blk += 1
                    first = True
                    for j in range(4 * qb + 4):
                        jj = j - 4 * qb
                        c0 = 0 if jj < 0 else jj * 128
                        sbk = (0, 1, 4)[pti % 3]
                        p = pti % 4
                        pti += 1
                        P.add("pe", (lambda sbk=sbk, b=b, j=j, qb=qb, c0=c0, KA=KA: lambda e: e.matmul(
                            C.banks[sbk][:, c0:512], lhsT=ka[b][0:KA, j * 128:(j + 1) * 128], rhs=qa[b][0:KA, qb * 512 + c0:(qb + 1) * 512],
                            start=True, stop=True))(), (ka_tk[b], qa_tk[b]), (C.bk[sbk],))
                        if kind == "f":
                            bias_ap = cneg[:, j, h:h + 1]
                            rd = (C.bk[sbk], cneg_tk)
                        else:
                            bias_ap = zero_ap
                            rd = (C.bk[sbk], C.t_const)
                        P.add("act", (lambda sbk=sbk, p=p, c0=c0, bias_ap=bias_ap: lambda e: e.activation(
                            out=pt[p][:, c0:512], in_=C.banks[sbk][:, c0:512], func=AF.Exp, bias=bias_ap))(), rd, (pt_tk[p],))
                        if jj >= 0:
                            P.add("pool", (lambda p=p, c0=c0: lambda e: e.tensor_tensor(out=pt[p][:, c0:c0 + 128], in0=pt[p][:, c0:c0 + 128],
                                                                                        in1=C.tri[:], op=ALU.mult))(), (pt_tk[p], C.t_const), (pt_tk[p],))
                        if len(pending) >= 2:
                            pending.pop(0)()

                        def pv_step(ob=ob, p=p, j=j, h=h, kind=kind, first=first, qb=qb, jj=jj, ycol=ycol, last=(j == 4 * qb + 3)):
                            fst = first
                            for ii in range(max(jj, 0), 4):
                                qt = 4 * qb + ii
                                P.add("pe", (lambda ii=ii, fst=fst, qt=qt: lambda e: e.matmul(
                                    C.banks[ob][:, ii * 128:ii * 128 + 65], lhsT=pt[p][:, ii * 128:(ii + 1) * 128], rhs=vaug[kind][:, j, h, :],
                                    start=fst, stop=(j == qt), skip_group_check=True))(), (pt_tk[p], vaug_tk[kind]), (C.bk[ob],))
                                fst = False
                            if last:
                                ov = C.banks[ob][:, :].rearrange("p (i c) -> p i c", c=128)
                                P.add("dve", lambda e: e.reciprocal(out=rec[:], in_=ov[:, :, 64]), (C.bk[ob],), (rec_tk,))
                                for ii in range(4):
                                    qt = 4 * qb + ii
                                    P.add("dve", (lambda ii=ii, qt=qt: lambda e: e.tensor_scalar(
                                        out=ymix[:, qt, ycol + h * 64:ycol + (h + 1) * 64], in0=ov[:, ii, 0:64], scalar1=rec[:, ii:ii + 1], scalar2=None,
                                        op0=ALU.mult))(), (C.bk[ob], rec_tk), (ymix_tk[qt],))
                        pending.append(pv_step)
                        first = False
        while pending:
            pending.pop(0)()


def outproj_phase(C, l, src, ymix, ymix_tk):
    nc, P = C.nc, C.P
    with ExitStack() as es:
        A = lambda name, shape, dt: es.enter_context(C.sb("op_" + name, shape, dt))
        wo = A("wo", [128, 8, D], BF16)
        stg = [A("stg%d" % i, [128, D], F32) for i in range(2)]
        xt = [A("xt%d" % i, [128, D], F32) for i in range(2)]
        yT = [A("yT%d" % i, [128, 8, 128], BF16) for i in range(2)]
        wo_tk, stg_tk, xt_tk, yT_tk = Tk(), [Tk(), Tk()], [Tk(), Tk()], [Tk(), Tk()]
        load_cast(C, wo, wo_tk, C.w["mix_w_out"][l], D, D, stg, stg_tk, piece=D)
        pend = []
        for i in range(NT):
            j = i % 2
            P.add("sp", (lambda j=j, i=i: lambda e: e.dma_start(out=xt[j][:], in_=src[i * 128:(i + 1) * 128, :]))(),
                  (C.xtk[i],), (xt_tk[j],), dma=True)
            pb = C.banks[j].bitcast(BF16)
            for c in range(8):
                P.add("pe", (lambda pb=pb, c=c, i=i: lambda e: e.transpose(out=pb[:, c * 128:(c + 1) * 128], in_=ymix[:, i, c * 128:(c + 1) * 128],
                                                                          identity=C.ident[:]))(), (ymix_tk[i], C.t_const), (C.bk[j],))
            P.add("act", (lambda pb=pb, j=j: lambda e: e.copy(out=yT[j][:], in_=pb[:, :].rearrange("p (c t) -> p c t", c=8)))(),
                  (C.bk[j],), (yT_tk[j],))

            def mm_step(i=i, j=j):
                for half in range(2):
                    bank = 2 + 2 * j + half
                    for c in range(8):
                        P.add("pe", (lambda bank=bank, c=c, half=half: lambda e: e.matmul(
                            C.banks[bank][:, :], lhsT=yT[j][:, c, :], rhs=wo[:, c, half * 512:(half + 1) * 512], start=(c == 0), stop=(c == 7)))(),
                            (yT_tk[j], wo_tk), (C.bk[bank],))
                    P.add("dve", (lambda bank=bank, half=half: lambda e: e.tensor_tensor(
                        out=xt[j][:, half * 512:(half + 1) * 512], in0=C.banks[bank][:, :], in1=xt[j][:, half * 512:(half + 1) * 512], op=ALU.add))(),
                        (C.bk[bank], xt_tk[j]), (xt_tk[j],))
                P.add("sp", lambda e: e.dma_start(out=C.out[i * 128:(i + 1) * 128, :], in_=xt[j][:]), (xt_tk[j],), (C.xtk[i],), dma=True)
            if pend:
                pend.pop(0)()
            pend.append(mm_step)
        while pend:
            pend.pop(0)()


class Ops:
    def __init__(self, C):
        self.P = C.P
        self.C = C

    def tt(self, eng, out, in0, in1, op, rd, wr):
        self.P.add(eng, lambda e: e.tensor_tensor(out=out, in0=in0, in1=in1, op=op), rd, wr)

    def ts(self, eng, out, in0, s1, op0, rd, wr, s2=None, op1=None):
        if op1 is None:
            self.P.add(eng, lambda e: e.tensor_scalar(out=out, in0=in0, scalar1=s1, scalar2=None, op0=op0), rd, wr)
        else:
            self.P.add(eng, lambda e: e.tensor_scalar(out=out, in0=in0, scalar1=s1, scalar2=s2, op0=op0, op1=op1), rd, wr)

    def stt(self, out, in0, scalar, in1, op0, op1, rd, wr):
        self.P.add("dve", lambda e: e.scalar_tensor_tensor(out=out, in0=in0, scalar=scalar, in1=in1, op0=op0, op1=op1), rd, wr)

    def act(self, out, in_, func, rd, wr, scale=1.0, bias=None):
        if bias is None:
            self.P.add("act", lambda e: e.activation(out=out, in_=in_, func=func, scale=scale), rd, wr)
        else:
            self.P.add("act", lambda e: e.activation(out=out, in_=in_, func=func, scale=scale, bias=bias), rd, tuple(wr))

    def red(self, out, in_, rd, wr):
        self.P.add("dve", lambda e: e.tensor_reduce(out=out, in_=in_, axis=AX.X, op=ALU.add), rd, wr)

    def rcp(self, out, in_, rd, wr):
        self.P.add("dve", lambda e: e.reciprocal(out=out, in_=in_), rd, wr)

    def mm(self, out, lhsT, rhs, start, stop, rd, wr):
        mode = (lhsT.shape[0], int(np.prod(lhsT.shape[1:])))
        self.P.add("pe", lambda e: e.matmul(out, lhsT=lhsT, rhs=rhs, start=start, stop=stop, skip_group_check=True), rd, wr, mode=mode)

    def tr(self, out, in_, ident, rd, wr):
        mode = (in_.shape[0], int(np.prod(in_.shape[1:])))
        self.P.add("pe", lambda e: e.transpose(out=out, in_=in_, identity=ident), rd, wr, mode=mode)

    def dma(self, out, in_, rd, wr, eng="sp"):
        self.P.add(eng, lambda e: e.dma_start(out=out, in_=in_), rd, wr, dma=True)

    def memset(self, eng, ap, val, wr):
        self.P.add(eng, lambda e: e.memset(ap, val), (), wr)


EW = 0.6065306597126334
RWN = 1824


def rwkv_phase(C, l, src, yr, yr_tk):
    nc, P = C.nc, C.P
    O = Ops(C)
    with ExitStack() as es:
        A = lambda name, shape, dt: es.enter_context(C.sb("rw_" + name, shape, dt))
        wa = A("wa", [128, 8, RWN], BF16)
        wb = A("wb", [128, 8, RWN], BF16)
        w2b, a2b = A("w2b", [128, 512], BF16), A("a2b", [128, 512], BF16)
        g2b, g2b2 = A("g2b", [128, 512], BF16), A("g2b2", [128, 512], BF16)
        T = {}

        def tk(n):
            if n not in T:
                T[n] = Tk(n)
            return T[n]
        tcn = C.t_const
        xt_tk = [Tk(), Tk()]
        eps_ap = cconst(C, EPS)[:, 0:1]
        gneps_ap = cconst(C, 64e-5)[:, 0:1]
        with ExitStack() as es3:
            mub = es3.enter_context(C.sb("rw_mub", [128, RWN], F32))
            omm = es3.enter_context(C.sb("rw_omm", [128, RWN], F32))
            stg = [es3.enter_context(C.sb("rw_stg%d" % i, [128, RWN], F32)) for i in range(2)]
            stg_tk = [Tk(), Tk()]
            O.dma(mub[:], C.w["rwkv_mu"][l:l + 1, :].partition_broadcast(128), (), (tk("mub"),))
            O.ts("dve", omm[:], mub[:], -1.0, ALU.mult, (tk("mub"),), (tk("mub"),), s2=1.0, op1=ALU.add)
            for c in range(8):
                sb = c % 2
                O.dma(stg[sb][:], C.w["mix_w_in"][l][c * 128:(c + 1) * 128, RW0:RW0 + RWN], (), (stg_tk[sb],))
                O.tt("dve", wa[:, c, :], stg[sb][:], omm[:], ALU.mult, (stg_tk[sb], tk("mub")), (tk("wa"),))
                O.tt("pool", wb[:, c, :], stg[sb][:], mub[:], ALU.mult, (stg_tk[sb], tk("mub")), (tk("wa"),))
            for dst in (w2b, a2b, g2b2):
                O.memset("pool", dst[:], 0.0, (tk("lw2"),))
            for (dst, nm, r0, r1) in ((w2b, "rwkv_w2", 0, 64), (a2b, "rwkv_a2", 0, 64), (g2b, "rwkv_g2", 0, 128), (g2b2, "rwkv_g2", 128, 160)):
                sb = 0
                O.dma(stg[sb][0:r1 - r0, 0:512], C.w[nm][l][r0:r1, :], (), (stg_tk[sb],))
                O.P.add("dve", (lambda dst=dst, n=r1 - r0: lambda e: e.tensor_copy(out=dst[0:n, :], in_=stg[0][0:n, 0:512]))(), (stg_tk[sb],), (tk("lw2"),))
            P.barrier()
        gbc = A("gbc", [128, D], F32)
        xt = [A("xt%d" % i, [128, D], F32) for i in range(2)]
        hTx = A("hTx", [128, 8, 129], BF16)
        hT = A("hT", [128, 8, 128], BF16)
        bc = {k: A("bc_" + k, [128, 512], F32) for k in ("w0", "a0", "kk", "ka", "rk", "lnw", "lnb")}
        tw, al = A("tw", [128, 128], BF16), A("al", [128, 128], BF16)
        sg1, sg2 = A("sg1", [128, 128], BF16), A("sg2", [128, 128], BF16)
        BTm, BOm = A("BTm", [128, 128], F32), A("BOm", [128, 128], F32)
        Msu, Miu, Msl = A("Msu", [128, 8, 64], F32), A("Miu", [128, 8, 64], F32), A("Msl", [128, 8, 64], F32)
        W = {k: A("w_" + k, [128, 512], F32) for k in (
            "sw", "a", "kkn", "kmod", "b", "lc", "t1", "At", "Bt", "Kt", "Rt", "Gt", "Bh", "Kh", "Vt", "g",
            "AtT", "BtT", "KtT", "RtT", "GtT", "Q0", "P0", "Q1", "P1", "Nak", "Nrb", "Nrk", "X", "AKV", "RKV", "W0", "U", "Y", "yc", "rr", "kr")}
        H = A("H", [128, 256], F32)
        sm = {k: A("s_" + k, [128, 8], F32) for k in ("n1", "n2", "rks", "m1", "m2", "m3")}
        outb = A("outb", [128, 512], BF16)
        scr = make_norm_scratch(C, es, "rw")
        bcast_load(C, gbc[:], tk("gbc"), C.w["mix_norm"][l:l + 1, :], D)
        for k, nm in (("w0", "rwkv_w0"), ("a0", "rwkv_a0"), ("kk", "rwkv_k_k"), ("ka", "rwkv_k_a"), ("rk", "rwkv_r_k"),
                      ("lnw", "rwkv_ln_w"), ("lnb", "rwkv_ln_b")):
            O.dma(bc[k][:], C.w[nm][l:l + 1, :].partition_broadcast(128), (), (tk("bc"),))
        O.ts("pool", BTm[:], C.trif[:], -EW, ALU.mult, (tcn,), (tk("msk"),))
        O.memset("pool", BTm[0:64, 64:128], 0.0, (tk("msk"),))
        O.memset("pool", BOm[:], 0.0, (tk("msk"),))
        O.memset("pool", BOm[0:64, 0:64], -EW, (tk("msk"),))
        O.memset("pool", BOm[64:128, 64:128], -EW, (tk("msk"),))
        for M_, pat, cm, op in ((Msu, [[0, 8], [1, 64]], -1, ALU.is_gt), (Miu, [[0, 8], [1, 64]], -1, ALU.is_ge),
                                (Msl, [[0, 8], [-1, 64]], 1, ALU.is_gt)):
            O.memset("pool", M_[:], 1.0, (tk("msk"),))
            for hf in range(2):
                P.add("pool", (lambda M_=M_, pat=pat, cm=cm, op=op, hf=hf: lambda e: e.affine_select(
                    out=M_[hf * 64:(hf + 1) * 64], in_=M_[hf * 64:(hf + 1) * 64], pattern=pat, compare_op=op, fill=0.0, base=0,
                    channel_multiplier=cm))(), (tk("msk"),), (tk("msk"),))
        O.memset("pool", H[:], 0.0, (tk("H"),))
        for t_ in (tw, al, sg2):
            O.memset("pool", t_[:], 0.0, (tk("lora"),))
        O.memset("pool", hTx[:, :, 0:1], 0.0, (tk("hTx"),))
        msk = tk("msk")
        f512 = lambda t: t[:]
        v8 = lambda ap: ap.rearrange("p (g d) -> p g d", d=64)
        b8 = lambda t: t[:, 0:8].unsqueeze(2).broadcast_to([128, 8, 64])
        bki = [1]

        def nb():
            bki[0] = bki[0] % 7 + 1
            return bki[0]

        CROSS = ("Bh", "Kh", "Vt", "g", "AtT", "RtT", "GtT", "X", "Nrb", "AKV", "RKV")
        W2 = {k: A("w2_" + k, [128, 512], F32) for k in CROSS}
        rks2 = A("s_rks2", [128, 8], F32)

        def Wp(nm, par):
            return W2[nm] if (par == 1 and nm in W2) else W[nm]

        XTp = lambda nm, h, c2, par: Wp(nm, par)[(h % 2) * 64:(h % 2) * 64 + 64, (h // 2) * 128 + c2 * 64:(h // 2) * 128 + c2 * 64 + 64]
        pv = lambda ap, par_: ap.rearrange("p (q two d) -> p q two d", two=2, d=64)[:, :, par_, :]
        bct = tk("bc")

        def stageA(i):
            par = i % 2
            j = i % 2
            wk, tT, am = tk("a_wk"), tk("a_tT"), tk("a_am")
            xwk, xtT, xam = tk("x_wk%d" % par), tk("x_tT%d" % par), tk("x_am%d" % par)
            rks = rks2 if par else sm["rks"]
            O.dma(xt[j][:], src[i * 128:(i + 1) * 128, :], (C.xtk[i],), (xt_tk[j],))
            norm_transpose(C, xt[j][:], xt_tk[j], gbc[:], tk("gbc"), hT, tk("hT"), 0, scr, 0)
            P.add("pool", lambda e: e.tensor_copy(out=hTx[:, :, 1:129], in_=hT[:]), (tk("hT"),), (tk("hTx"),))
            yield
            for dst_, dtk_, c0 in ((W["rr"], wk, 0), (W["kr"], wk, 512), (Wp("Vt", par), xwk, 1024)):
                bank = nb()
                n = 0
                for c in range(8):
                    for (w_, lo) in ((wa, 1), (wb, 0)):
                        O.mm(C.banks[bank][:, :], hTx[:, c, lo:lo + 128], w_[:, c, c0:c0 + 512], n == 0, n == 15, (tk("hTx"), tk("wa")), (C.bk[bank],))
                        n += 1
                O.act(dst_[:], C.banks[bank][:, :], AF.Copy, (C.bk[bank],), (dtk_,))
                yield
            b4 = nb()
            for (r0, r1, col, c0) in ((0, 64, 0, 1536), (0, 64, 128, 1600), (0, 128, 256, 1664), (0, 32, 384, 1792)):
                n = 0
                for c in range(8):
                    for (w_, lo) in ((wa, 1), (wb, 0)):
                        O.mm(C.banks[b4][r0:r1, col:col + 128], w_[:, c, c0:c0 + (r1 - r0)], hTx[:, c, lo:lo + 128], n == 0, n == 15,
                             (tk("hTx"), tk("wa")), (C.bk[b4],))
                        n += 1
            P.add("pool", lambda e: e.tensor_copy(out=hTx[:, :, 0:1], in_=hT[:, :, 127:128]), (tk("hT"), tk("hTx")), (tk("hTx"),))
            O.act(tw[0:64, :], C.banks[b4][0:64, 0:128], AF.Tanh, (C.bk[b4],), (tk("lora"),))
            O.act(al[0:64, :], C.banks[b4][0:64, 128:256], AF.Copy, (C.bk[b4],), (tk("lora"),))
            O.act(sg1[:], C.banks[b4][0:128, 256:384], AF.Sigmoid, (C.bk[b4],), (tk("lora"),))
            O.act(sg2[0:32, :], C.banks[b4][0:32, 384:512], AF.Sigmoid, (C.bk[b4],), (tk("lora"),))
            yield
            b5, b6, b7 = nb(), nb(), nb()
            O.mm(C.banks[b5][:, :], tw[:], w2b[:], True, True, (tk("lora"), tk("lw2")), (C.bk[b5],))
            O.mm(C.banks[b6][:, :], al[:], a2b[:], True, True, (tk("lora"), tk("lw2")), (C.bk[b6],))
            O.mm(C.banks[b7][:, :], sg1[:], g2b[:], True, False, (tk("lora"), tk("lw2")), (C.bk[b7],))
            O.mm(C.banks[b7][:, :], sg2[:], g2b2[:], False, True, (tk("lora"), tk("lw2")), (C.bk[b7],))
            O.tt("dve", W["t1"][:], C.banks[b5][:, :], bc["w0"][:], ALU.add, (C.bk[b5], bct), (wk,))
            O.act(W["sw"][:], W["t1"][:], AF.Sigmoid, (wk,), (wk,))
            O.tt("dve", W["t1"][:], C.banks[b6][:, :], bc["a0"][:], ALU.add, (C.bk[b6], bct), (wk,))
            O.act(W["a"][:], W["t1"][:], AF.Sigmoid, (wk,), (wk,))
            O.act(Wp("g", par)[:], C.banks[b7][:, :], AF.Copy, (C.bk[b7],), (xwk,))
            yield
            b5, b6 = nb(), nb()
            O.mm(C.banks[b5][:, :], BTm[:], W["sw"][:], True, True, (wk, msk), (C.bk[b5],))
            O.mm(C.banks[b6][:, :], BOm[:], W["sw"][:], True, True, (wk, msk), (C.bk[b6],))
            O.act(W["lc"][:], C.banks[b5][:, :], AF.Copy, (C.bk[b5],), (wk,))
            Bh, Kh = Wp("Bh", par), Wp("Kh", par)
            O.tt("dve", Bh[:], C.banks[b6][:, :], W["lc"][:], ALU.subtract, (C.bk[b6], wk), (xwk,))
            O.act(W["Gt"][:], C.banks[b6][:, :], AF.Exp, (C.bk[b6],), (wk,))
            yield
            O.tt("dve", W["kkn"][:], W["kr"][:], bc["kk"][:], ALU.mult, (wk, bct), (wk,))
            O.act(W["t1"][:], W["kkn"][:], AF.Square, (wk,), (wk,))
            O.red(sm["n1"][:], v8(W["t1"][:]), (wk,), (wk,))
            O.act(sm["n2"][:], sm["n1"][:], AF.Sqrt, (wk,), (wk,))
            O.ts("dve", sm["n2"][:], sm["n2"][:], 1e-12, ALU.max, (wk,), (wk,))
            O.rcp(sm["n1"][:], sm["n2"][:], (wk,), (wk,))
            O.tt("dve", v8(W["kkn"][:]), v8(W["kkn"][:]), b8(sm["n1"]), ALU.mult, (wk,), (wk,))
            yield
            O.stt(W["t1"][:], W["a"][:], -1.0, bc["ka"][:], ALU.add, ALU.mult, (wk, bct), (wk,))
            O.stt(W["kmod"][:], W["t1"][:], 1.0, W["kr"][:], ALU.add, ALU.mult, (wk,), (wk,))
            O.tt("dve", W["b"][:], W["kkn"][:], W["a"][:], ALU.mult, (wk,), (wk,))
            O.tt("dve", W["t1"][:], W["rr"][:], bc["rk"][:], ALU.mult, (wk, bct), (wk,))
            O.tt("dve", W["t1"][:], W["t1"][:], W["kmod"][:], ALU.mult, (wk,), (wk,))
            O.red(rks[:], v8(W["t1"][:]), (wk,), (xwk,))
            yield
            O.stt(W["At"][:], W["sw"][:], EW, W["lc"][:], ALU.mult, ALU.add, (wk,), (wk,))
            O.act(W["At"][:], W["At"][:], AF.Exp, (wk,), (wk,))
            O.stt(W["At"][:], W["kkn"][:], -1.0, W["At"][:], ALU.mult, ALU.mult, (wk,), (wk,))
            O.act(W["Bt"][:], W["lc"][:], AF.Exp, (wk,), (wk,), scale=-1.0)
            O.tt("dve", W["Kt"][:], W["kmod"][:], W["Bt"][:], ALU.mult, (wk,), (wk,))
            O.tt("dve", W["Bt"][:], W["b"][:], W["Bt"][:], ALU.mult, (wk,), (wk,))
            yield
            O.act(W["Rt"][:], W["lc"][:], AF.Exp, (wk,), (wk,))
            O.tt("dve", W["Rt"][:], W["rr"][:], W["Rt"][:], ALU.mult, (wk,), (wk,))
            O.act(Bh[:], Bh[:], AF.Exp, (xwk,), (xwk,))
            O.tt("dve", Kh[:], W["kmod"][:], Bh[:], ALU.mult, (wk, xwk), (xwk,))
            O.tt("dve", Bh[:], W["b"][:], Bh[:], ALU.mult, (wk, xwk), (xwk,))
            yield
            for nm in ("At", "Bt", "Kt", "Rt", "Gt"):
                bank = nb()
                for blk_ in range(4):
                    O.tr(C.banks[bank][:, blk_ * 128:(blk_ + 1) * 128], W[nm][:, blk_ * 128:(blk_ + 1) * 128], C.identf[:], (wk, tcn), (C.bk[bank],))
                O.act(Wp(nm + "T", par)[:], C.banks[bank][:, :], AF.Copy, (C.bk[bank],), (xtT if nm in ("At", "Rt", "Gt") else tT,))
                yield

            def mm_d(lnm, rnm):
                bb = [nb(), nb()]
                for c2 in (0, 1):
                    for h in range(8):
                        par_ = h % 2
                        O.mm(C.banks[bb[par_]][c2 * 64:(c2 + 1) * 64, h * 64:(h + 1) * 64], XTp(lnm, h, c2, par), XTp(rnm, h, c2, par), True, True,
                             (tT, xtT), (C.bk[bb[par_]],))
                return bb

            def mm_t(l_ap, r_ap, rd):
                b_ = nb()
                for h in range(8):
                    hs = slice(h * 64, (h + 1) * 64)
                    for c2 in (0, 1):
                        ps = slice(c2 * 64, (c2 + 1) * 64)
                        O.mm(C.banks[b_][ps, hs], l_ap[ps, hs], r_ap[ps, hs], True, True, rd, (C.bk[b_],))
                return b_

            for (dst, lnm, rnm, msk_) in (("Q0", "BtT", "AtT", Msu), ("P0", "AtT", "BtT", Msl), ("Nak", "KtT", "AtT", Msu),
                                          ("Nrb", "BtT", "RtT", Miu), ("Nrk", "KtT", "RtT", Miu)):
                bb = mm_d(lnm, rnm)
                mflat = msk_[:].rearrange("p g d -> p (g d)")
                for par_ in range(2):
                    O.tt("dve", pv(Wp(dst, par)[:], par_), pv(C.banks[bb[par_]][:, :], par_), pv(mflat, par_), ALU.mult, (C.bk[bb[par_]], msk),
                         (xam if dst == "Nrb" else am,))
                yield
            for (dst, nm) in (("AKV", "Nak"), ("RKV", "Nrk")):
                b_ = mm_t(W[nm], Wp("Vt", par), (am, xwk))
                O.act(Wp(dst, par)[:], C.banks[b_][:, :], AF.Copy, (C.bk[b_],), (xam,))
                yield
            X = Wp("X", par)
            O.tt("dve", X[:], Miu[:].rearrange("p g d -> p (g d)"), Msu[:].rearrange("p g d -> p (g d)"), ALU.subtract, (msk,), (xam,))
            O.tt("dve", X[:], X[:], W["Q0"][:], ALU.add, (am, xam), (xam,))
            Pc, Qc, Pn, Qn = "P0", "Q0", "P1", "Q1"
            for it in range(5):
                bP = mm_t(W[Qc], W[Pc], (am,))
                if it < 4:
                    bQ = mm_t(W[Pc], W[Qc], (am,))
                O.act(W[Pn][:], C.banks[bP][:, :], AF.Copy, (C.bk[bP],), (am,))
                if it < 4:
                    P.add("dve", (lambda Qn=Qn, bQ=bQ: lambda e: e.tensor_copy(out=W[Qn][:], in_=C.banks[bQ][:, :]))(), (C.bk[bQ],), (am,))
                yield
                bX = mm_t(W[Pn], X, (am, xam))
                O.tt("dve", X[:], C.banks[bX][:, :], X[:], ALU.add, (C.bk[bX], xam), (xam,))
                Pc, Qc, Pn, Qn = Pn, Qn, Pc, Qc
                yield

        def stageB(i):
            par = i % 2
            xwk, xtT, xam = tk("x_wk%d" % par), tk("x_tT%d" % par), tk("x_am%d" % par)
            rks = rks2 if par else sm["rks"]
            ch, Ht, yt, ot = tk("b_ch"), tk("H"), tk("b_Y"), tk("b_ot")
            X, Nrb, AKV, RKV = Wp("X", par), Wp("Nrb", par), Wp("AKV", par), Wp("RKV", par)
            Bh, Kh, Vt, g_ = Wp("Bh", par), Wp("Kh", par), Wp("Vt", par), Wp("g", par)
            for c2 in range(2):
                ps = slice(c2 * 64, (c2 + 1) * 64)
                bW, bY1 = [nb(), nb()], [nb(), nb()]
                for (bb, nm) in ((bW, "AtT"), (bY1, "RtT")):
                    for h in range(8):
                        par_ = h % 2
                        pr = slice((h % 2) * 64, (h % 2) * 64 + 64)
                        O.mm(C.banks[bb[par_]][ps, h * 64:(h + 1) * 64], XTp(nm, h, c2, par), H[pr, (h // 2) * 64:(h // 2) * 64 + 64], True, True,
                             (xtT, Ht), (C.bk[bb[par_]],))
                for par_ in range(2):
                    O.tt("dve", pv(W["W0"][ps, :], par_), pv(C.banks[bW[par_]][ps, :], par_), pv(AKV[ps, :], par_), ALU.add,
                         (C.bk[bW[par_]], xam), (ch,))
                for par_ in range(2):
                    O.tt("dve", pv(W["Y"][ps, :], par_), pv(C.banks[bY1[par_]][ps, :], par_), pv(RKV[ps, :], par_), ALU.add,
                         (C.bk[bY1[par_]], xam), (yt,))
                yield
                bU = nb()
                for h in range(8):
                    hs = slice(h * 64, (h + 1) * 64)
                    O.mm(C.banks[bU][ps, hs], X[ps, hs], W["W0"][ps, hs], True, True, (xam, ch), (C.bk[bU],))
                O.act(W["U"][ps, :], C.banks[bU][ps, :], AF.Copy, (C.bk[bU],), (ch,))
                yield
                bY2, bH = nb(), nb()
                for h in range(8):
                    hs = slice(h * 64, (h + 1) * 64)
                    O.mm(C.banks[bY2][ps, hs], Nrb[ps, hs], W["U"][ps, hs], True, True, (xam, ch), (C.bk[bY2],))
                for h in range(8):
                    hs = slice(h * 64, (h + 1) * 64)
                    pr = slice((h % 2) * 64, (h % 2) * 64 + 64)
                    ho = C.banks[bH][pr, (h // 2) * 64:(h // 2) * 64 + 64]
                    O.mm(ho, Bh[ps, hs], W["U"][ps, hs], True, False, (xwk, ch), (C.bk[bH],))
                    O.mm(ho, Kh[ps, hs], Vt[ps, hs], False, True, (xwk,), (C.bk[bH],))
                O.tt("dve", W["Y"][ps, :], C.banks[bY2][ps, :], W["Y"][ps, :], ALU.add, (C.bk[bY2], yt), (yt,))
                GtT = Wp("GtT", par)
                for pair in range(4):
                    cs = slice(pair * 64, (pair + 1) * 64)
                    O.stt(H[:, cs], H[:, cs], GtT[:, pair * 128 + c2 * 64:pair * 128 + c2 * 64 + 1], C.banks[bH][:, cs], ALU.mult, ALU.add,
                          (Ht, xtT, C.bk[bH]), (Ht,))
                yield
            tmp = W["W0"]
            O.red(sm["m1"][:], v8(W["Y"][:]), (yt,), (ot,))
            O.ts("dve", sm["m1"][:], sm["m1"][:], -1.0 / 64, ALU.mult, (ot,), (ot,))
            O.tt("dve", v8(W["yc"][:]), v8(W["Y"][:]), b8(sm["m1"]), ALU.add, (yt, ot), (ot,))
            O.act(tmp[:], W["yc"][:], AF.Square, (ot, ch), (ch,))
            O.red(sm["m2"][:], v8(tmp[:]), (ch,), (ot,))
            O.act(sm["m3"][:], sm["m2"][:], AF.Sqrt, (ot, tcn), (ot,), scale=1.0 / 64, bias=gneps_ap)
            O.rcp(sm["m2"][:], sm["m3"][:], (ot,), (ot,))
            yield
            O.tt("dve", v8(W["yc"][:]), v8(W["yc"][:]), b8(sm["m2"]), ALU.mult, (ot,), (ot,))
            O.tt("dve", W["yc"][:], W["yc"][:], bc["lnw"][:], ALU.mult, (ot, bct), (ot,))
            O.tt("dve", W["yc"][:], W["yc"][:], bc["lnb"][:], ALU.add, (ot, bct), (ot,))
            O.tt("dve", v8(tmp[:]), v8(Vt[:]), rks[:, 0:8].unsqueeze(2).broadcast_to([128, 8, 64]), ALU.mult, (xwk, ch), (ch,))
            O.tt("dve", W["yc"][:], W["yc"][:], tmp[:], ALU.add, (ot, ch), (ot,))
            O.tt("dve", outb[:], W["yc"][:], g_[:], ALU.mult, (ot, xwk), (tk("outb"),))
            O.dma(yr[i * 128:(i + 1) * 128, :], outb[:], (tk("outb"),), (yr_tk,))
            yield

        ntl = C.cfg.get("rw_tiles", NT)
        ratio = C.cfg.get("rw_ratio", 2)
        for i in range(ntl + 1):
            gA = stageA(i) if i < ntl else None
            gB = stageB(i - 1) if i >= 1 else None
            while gA is not None or gB is not None:
                if gB is not None:
                    try:
                        next(gB)
                    except StopIteration:
                        gB = None
                for _ in range(ratio):
                    if gA is not None:
                        try:
                            next(gA)
                        except StopIteration:
                            gA = None
        P.barrier()


def xattn_phase(C, l, src):
    nc, P = C.nc, C.P
    O = Ops(C)
    with ExitStack() as es:
        A = lambda name, shape, dt: es.enter_context(C.sb("xa_" + name, shape, dt))
        wq = A("wq", [128, 8, 512], BF16)
        wkv = A("wkv", [128, 8, 1024], BF16)
        wo = A("wo", [128, 4, D], BF16)
        stg = [A("stg%d" % i, [128, 1024], F32) for i in range(2)]
        gbc = A("gbc", [128, D], F32)
        gmb = A("gmb", [128, D], F32)
        gq = A("gq", [128, 512], F32)
        gk = A("gk", [128, 512], F32)
        xt = [A("xt%d" % i, [128, D], F32) for i in range(2)]
        hT = A("hT", [128, 8, 128], BF16)
        kT = A("kT", [128, 4, 256], BF16)
        vaug = A("vaug", [128, 2, 4, 129], BF16)
        sq = A("sq", [128, 512], F32)
        qn = A("qn", [128, 512], F32)
        qb = A("qb", [128, 512], BF16)
        qT = A("qT", [128, 4, 128], BF16)
        pt = [A("pt%d" % i, [128, 512], BF16) for i in range(2)]
        ob = A("ob", [128, 512], BF16)
        oT = A("oT", [128, 4, 128], BF16)
        st = {k: A("s_" + k, [128, 4], F32) for k in ("ss", "sd", "rs", "rec")}
        scr = make_norm_scratch(C, es, "xa")
        T = {}

        def tk(n):
            if n not in T:
                T[n] = Tk(n)
            return T[n]
        tcn = C.t_const
        stg_tk = [Tk(), Tk()]
        xt_tk = [Tk(), Tk()]
        eps_ap = cconst(C, EPS)[:, 0:1]
        zero_ap = cconst(C, 0.0)[:, 0:1]
        load_cast(C, wq, tk("wq"), C.w["xattn_w_q"][l], D, 512, stg, stg_tk, piece=512)
        load_cast(C, wkv, tk("wkv"), C.w["xattn_w_kv"][l], D, 1024, stg, stg_tk, piece=1024)
        load_cast(C, wo, tk("wo"), C.w["xattn_w_out"][l], 512, D, stg, stg_tk, piece=1024)
        bcast_load(C, gbc[:], tk("g"), C.w["xattn_norm"][l:l + 1, :], D)
        bcast_load(C, gmb[:], tk("g"), C.w["xattn_mem_norm"][l:l + 1, :], D)
        for g in range(4):
            O.dma(gq[:, g * 128:(g + 1) * 128], C.w["xattn_q_gain"][l:l + 1, :].partition_broadcast(128), (), (tk("g"),))
            O.dma(gk[:, g * 128:(g + 1) * 128], C.w["xattn_k_gain"][l:l + 1, :].partition_broadcast(128), (), (tk("g"),))
        O.ts("dve", gq[:], gq[:], float(128 ** -0.5), ALU.mult, (tk("g"),), (tk("g"),))
        O.memset("pool", vaug[:, :, :, 128:129], 1.0, (tk("vaug"),))
        v4 = lambda ap: ap.rearrange("p (g d) -> p g d", d=128)
        b4 = lambda t: t[:, 0:4].unsqueeze(2).broadcast_to([128, 4, 128])

        def gnorm(src_psum, src_tk, gains, dst):
            O.act(sq[:], src_psum, AF.Square, (src_tk,), (tk("sq"),))
            O.red(st["ss"][:], v4(sq[:]), (tk("sq"),), (tk("st"),))
            O.act(st["sd"][:], st["ss"][:], AF.Sqrt, (tk("st"), tcn), (tk("st"),), scale=1.0 / 128, bias=eps_ap)
            O.rcp(st["rs"][:], st["sd"][:], (tk("st"),), (tk("st"),))
            O.tt("dve", v4(qn[:]), v4(src_psum), b4(st["rs"]), ALU.mult, (src_tk, tk("st")), (tk("qn"),))
            O.tt("dve", dst, qn[:], gains[:], ALU.mult, (tk("qn"), tk("g")), (tk("qb"),))

        for mt in range(2):
            O.dma(xt[mt][:], C.mem[mt * 128:(mt + 1) * 128, :], (), (xt_tk[mt],))
            norm_transpose(C, xt[mt][:], xt_tk[mt], gmb[:], tk("g"), hT, tk("hT"), 0, scr, 0)
            for half in range(2):
                for c in range(8):
                    O.mm(C.banks[1 + half][:, :], hT[:, c, :], wkv[:, c, half * 512:(half + 1) * 512], c == 0, c == 7, (tk("hT"), tk("wkv")), (C.bk[1 + half],))
            gnorm(C.banks[1][:, :], C.bk[1], gk, qb[:])
            O.act(vaug[:, mt, :, 0:128], v4(C.banks[2][:, :]), AF.Copy, (C.bk[2],), (tk("vaug"),))
            pb = C.banks[3].bitcast(BF16)
            for h in range(4):
                O.tr(pb[:, h * 128:(h + 1) * 128], qb[:, h * 128:(h + 1) * 128], C.ident[:], (tk("qb"), tcn), (C.bk[3],))
            O.act(kT[:, :, mt * 128:(mt + 1) * 128], pb[:, 0:512].rearrange("p (h t) -> p h t", h=4), AF.Copy, (C.bk[3],), (tk("kT"),))
        P.barrier()
        dup = {}
        for par in range(2):
            d_ = {}
            d_["hT"] = A("hT_%d" % par, [128, 8, 128], BF16)
            d_["sq"] = A("sq_%d" % par, [128, 512], F32)
            d_["qn"] = A("qn_%d" % par, [128, 512], F32)
            d_["qb"] = A("qb_%d" % par, [128, 512], BF16)
            d_["qT"] = A("qT_%d" % par, [128, 4, 128], BF16)
            d_["pt"] = [A("pt%d_%d" % (k_, par), [128, 512], BF16) for k_ in range(2)]
            d_["ob"] = A("ob_%d" % par, [128, 512], BF16)
            d_["oT"] = A("oT_%d" % par, [128, 4, 128], BF16)
            d_["st"] = {k_: A("s%s_%d" % (k_, par), [128, 4], F32) for k_ in ("ss", "sd", "rs", "rec")}
            d_["scr"] = make_norm_scratch(C, es, "xa%d" % par)
            dup[par] = d_

        def tile_gen(i):
            par = i % 2
            d_ = dup[par]
            B = 4 * par
            hT_, sq_, qn_, qb_, qT_, pt_, ob_, oT_, st_ = d_["hT"], d_["sq"], d_["qn"], d_["qb"], d_["qT"], d_["pt"], d_["ob"], d_["oT"], d_["st"]
            t = lambda n: tk("%s_p%d" % (n, par))
            j = par
            O.dma(xt[j][:], src[i * 128:(i + 1) * 128, :], (C.xtk[i],), (xt_tk[j],))
            norm_transpose(C, xt[j][:], xt_tk[j], gbc[:], tk("g"), hT_, t("hT"), 0, d_["scr"], B + 0)
            yield
            for c in range(8):
                O.mm(C.banks[B + 1][:, :], hT_[:, c, :], wq[:, c, :], c == 0, c == 7, (t("hT"), tk("wq")), (C.bk[B + 1],))
            yield
            O.act(sq_[:], C.banks[B + 1][:, :], AF.Square, (C.bk[B + 1],), (t("sq"),))
            O.red(st_["ss"][:], v4(sq_[:]), (t("sq"),), (t("st"),))
            yield
            O.act(st_["sd"][:], st_["ss"][:], AF.Sqrt, (t("st"), tcn), (t("st"),), scale=1.0 / 128, bias=eps_ap)
            O.rcp(st_["rs"][:], st_["sd"][:], (t("st"),), (t("st"),))
            yield
            O.tt("dve", v4(qn_[:]), v4(C.banks[B + 1][:, :]), b4(st_["rs"]), ALU.mult, (C.bk[B + 1], t("st")), (t("qn"),))
            O.tt("dve", qb_[:], qn_[:], gq[:], ALU.mult, (t("qn"), tk("g")), (t("qb"),))
            yield
            pb = C.banks[B + 2].bitcast(BF16)
            for h in range(4):
                O.tr(pb[:, h * 128:(h + 1) * 128], qb_[:, h * 128:(h + 1) * 128], C.ident[:], (t("qb"), tcn), (C.bk[B + 2],))
            O.act(qT_[:], pb[:, 0:512].rearrange("p (h t) -> p h t", h=4), AF.Copy, (C.bk[B + 2],), (t("qT"),))
            yield
            for mt in range(2):
                bs = B + 1 + mt
                for h in range(4):
                    O.mm(C.banks[bs][:, h * 128:(h + 1) * 128], kT[:, h, mt * 128:(mt + 1) * 128], qT_[:, h, :], True, True, (tk("kT"), t("qT")), (C.bk[bs],))
                P.add("act", (lambda mt=mt, bs=bs: lambda e: e.activation(out=pt_[mt][:], in_=C.banks[bs][:, :], func=AF.Exp, bias=zero_ap))(),
                      (C.bk[bs], tcn), (t("pt%d" % mt),))
                yield
            oslot = lambda h: (B + 3, h * 129) if h < 3 else (B + 0, 0)
            for h in range(4):
                bank, ocol = oslot(h)
                for mt in range(2):
                    O.mm(C.banks[bank][:, ocol:ocol + 129], pt_[mt][:, h * 128:(h + 1) * 128], vaug[:, mt, h, :], mt == 0, mt == 1,
                         (t("pt%d" % mt), tk("vaug")), (C.bk[bank],))
            yield
            for h in range(4):
                bank, ocol = oslot(h)
                O.rcp(st_["rec"][:, h:h + 1], C.banks[bank][:, ocol + 128:ocol + 129], (C.bk[bank],), (t("rec"),))
                O.ts("dve", ob_[:, h * 128:(h + 1) * 128], C.banks[bank][:, ocol:ocol + 128], st_["rec"][:, h:h + 1], ALU.mult, (C.bk[bank], t("rec")), (t("ob"),))
                if h % 2 == 1:
                    yield
            pb = C.banks[B + 1].bitcast(BF16)
            for h in range(4):
                O.tr(pb[:, h * 128:(h + 1) * 128], ob_[:, h * 128:(h + 1) * 128], C.ident[:], (t("ob"), tcn), (C.bk[B + 1],))
            O.act(oT_[:], pb[:, 0:512].rearrange("p (h t) -> p h t", h=4), AF.Copy, (C.bk[B + 1],), (t("oT"),))
            yield
            for half in range(2):
                bank = B + 2 + half
                for c in range(4):
                    O.mm(C.banks[bank][:, :], oT_[:, c, :], wo[:, c, half * 512:(half + 1) * 512], c == 0, c == 3, (t("oT"), tk("wo")), (C.bk[bank],))
                O.tt("dve", xt[j][:, half * 512:(half + 1) * 512], C.banks[bank][:, :], xt[j][:, half * 512:(half + 1) * 512], ALU.add,
                     (C.bk[bank], xt_tk[j]), (xt_tk[j],))
                yield
            O.dma(C.out[i * 128:(i + 1) * 128, :], xt[j][:], (xt_tk[j],), (C.xtk[i],))
            yield

        for m in range(NT // 2):
            gens = [tile_gen(2 * m), tile_gen(2 * m + 1)]
            while gens:
                for g_ in list(gens):
                    try:
                        next(g_)
                    except StopIteration:
                        gens.remove(g_)
        P.barrier()


_NC_CACHE = {}

LAUNCHES = (
    ("ffn1", {"plan": (("ffn1", 0),), "LW": 1}),
    ("rwkv", {"plan": (("rwkv", 0),), "LW": 1, "yr_kind": "ExternalOutput"}),
    ("attn", {"plan": (("attn", 0),), "LW": 1, "yr_kind": "ExternalInput"}),
    ("xattn", {"plan": (("xattn", 0),), "LW": 1}),
    ("ffn2", {"plan": (("ffn2", 0),), "LW": 1}),
)


def get_nc(name, cfg):
    if name not in _NC_CACHE:
        nc = bass.Bass("TRN2", target_bir_lowering=False)
        build(nc, dict(cfg))
        _NC_CACHE[name] = nc
    return _NC_CACHE[name]


def make_in_maps(inputs, cores, l=None, extra=None):
    maps = []
    for b in cores:
        m = {}
        for k, v in inputs.items():
            v = np.asarray(v)
            if k in ("x", "mem"):
                m[k] = np.ascontiguousarray(v[b])
            elif k == "positions":
                m[k] = np.ascontiguousarray(v[b:b + 1]).astype(np.int32)
            else:
                if k == "rwkv_r_k":
                    v = v.reshape(L, 512)
                if l is not None:
                    v = v[l:l + 1]
                m[k] = np.ascontiguousarray(v)
        if extra:
            for k, v in extra.items():
                m[k] = np.ascontiguousarray(v[b])
        maps.append(m)
    return maps


FUSED = True


def kernel(**inputs):
    cores = list(range(8))
    inp = dict(inputs)
    if FUSED:
        nc = get_nc("fused", {})
        res = run_bass_kernel_spmd(nc, make_in_maps(inp, cores), core_ids=cores)
        return np.stack([np.asarray(r["out"]) for r in res.results], axis=0).astype(np.float32)
    x = np.asarray(inp["x"])
    for l in range(L):
        yr = None
        for name, cfg in LAUNCHES:
            nc = get_nc(name, cfg)
            inp["x"] = x
            extra = {"yr": yr} if name == "attn" else None
            res = run_bass_kernel_spmd(nc, make_in_maps(inp, cores, l=l, extra=extra), core_ids=cores)
            if name == "rwkv":
                yr = np.stack([np.asarray(r["yr"]) for r in res.results], axis=0)
            else:
                x = np.stack([np.asarray(r["out"]) for r in res.results], axis=0).astype(np.float32)
    return x
```

```python
import numpy as np
from contextlib import ExitStack
import concourse.bass as bass
import concourse.mybir as mybir
from concourse.bass_utils import run_bass_kernel_spmd

F32 = mybir.dt.float32
BF16 = mybir.dt.bfloat16
I32 = mybir.dt.int32
AF = mybir.ActivationFunctionType
ALU = mybir.AluOpType
AX = mybir.AxisListType

D = 1024
S = 4096
NT = S // 128
L = 2
FFN = 2816
NF = FFN // 128
HD = 64
MEM = 256
EPS = 1e-6
IN_W = 3364
FOX_IN = 772
MOBA_IN = 768
RW0 = FOX_IN + MOBA_IN
NEG = -30000.0


class Tk:
    __slots__ = ("name", "lw", "rd", "excl")

    def __init__(self, name="", excl=False):
        self.name = name
        self.lw = None
        self.rd = []
        self.excl = excl


class Ins:
    __slots__ = ("eng", "fn", "deps", "idx", "mark", "cnt", "dma", "dsem", "dval", "waits")


class Prog:
    ENGS = ("pe", "dve", "act", "pool", "sp")
    NDMA = 6

    def __init__(self, nc):
        self.nc = nc
        self.ins = {e: [] for e in self.ENGS}
        self.order = []
        self.dma_slot_last = {}
        self.dma_cnt = {e: 0 for e in self.ENGS}
        self.dma_uses = {}
        self.pending_barrier = {e: None for e in self.ENGS}
        self.pe_mode = None

    def add(self, eng, fn, reads=(), writes=(), dma=False, mode=None):
        if eng == "pe":
            if mode is None:
                mode = (128, 128)
            mode = tuple(32 if v <= 32 else (64 if v <= 64 else 128) for v in mode)
            if self.pe_mode is not None and mode != self.pe_mode:
                self.pe_mode = mode
                self.add("pe", lambda e: e.drain(), (), (), mode=mode)
            self.pe_mode = mode
        I = Ins()
        I.eng = eng
        I.fn = fn
        I.dma = dma
        I.mark = False
        I.cnt = 0
        I.waits = []
        deps = []
        if any(t.excl for t in reads):
            writes = tuple(writes) + tuple(t for t in reads if t.excl and t not in writes)
            reads = tuple(t for t in reads if not t.excl)
        for t in reads:
            if t.lw is not None:
                deps.append(t.lw)
        for t in writes:
            if t.lw is not None:
                deps.append(t.lw)
            deps.extend(t.rd)
        if self.pending_barrier[eng] is not None:
            deps.extend(self.pending_barrier[eng])
            self.pending_barrier[eng] = None
        if dma:
            k = self.dma_cnt[eng] % self.NDMA
            self.dma_cnt[eng] += 1
            key = (eng, k)
            prev = self.dma_slot_last.get(key)
            if prev is not None:
                deps.append(prev)
            self.dma_slot_last[key] = I
            self.dma_uses[key] = self.dma_uses.get(key, 0) + 1
            I.dsem = key
            I.dval = 16 * self.dma_uses[key]
        I.deps = deps
        I.idx = len(self.ins[eng])
        self.ins[eng].append(I)
        self.order.append(I)
        for t in reads:
            t.rd.append(I)
        for t in writes:
            t.lw = I
            t.rd = []
        return I

    def barrier(self):
        last = []
        for e in self.ENGS:
            if self.ins[e]:
                last.append(self.ins[e][-1])
        for key, I in self.dma_slot_last.items():
            last.append(I)
        for e in self.ENGS:
            self.pending_barrier[e] = list(last)

    def finish(self, es):
        nc = self.nc
        self.barrier()
        self.add("sp", lambda e: e.nop(), ())
        waited = {e: {} for e in self.ENGS}
        for I in self.order:
            E = I.eng
            w = waited[E]
            for d in I.deps:
                if d.dma:
                    key = ("dma",) + d.dsem
                    if w.get(key, 0) >= d.dval:
                        continue
                    w[key] = d.dval
                    I.waits.append(d)
                else:
                    if d.eng == E and E in ("pe", "sp"):
                        continue
                    key = d.eng
                    if w.get(key, -1) >= d.idx:
                        continue
                    w[key] = d.idx
                    d.mark = True
                    I.waits.append(d)
        sems = {}
        for e in self.ENGS:
            sems[e] = es.enter_context(nc.semaphore("sem_" + e))
            c = 0
            for I in self.ins[e]:
                if I.mark and not I.dma:
                    c += 1
                    I.cnt = c
        dsems = {}
        for key in self.dma_uses:
            dsems[key] = es.enter_context(nc.semaphore("dsem_%s_%d" % key))

        def emit(eng_name, e):
            for I in self.ins[eng_name]:
                for d in I.waits:
                    if d.dma:
                        e.wait_ge(dsems[d.dsem], d.dval)
                    else:
                        e.wait_ge(sems[d.eng], d.cnt)
                r = I.fn(e)
                if I.dma:
                    r.then_inc(dsems[I.dsem], 16)
                elif I.mark:
                    r.then_inc(sems[eng_name], 1)

        with nc.Block() as block:
            @block.sync
            def _(e):
                emit("sp", e)

            @block.tensor
            def _(e):
                emit("pe", e)

            @block.vector
            def _(e):
                emit("dve", e)

            @block.scalar
            def _(e):
                emit("act", e)

            @block.gpsimd
            def _(e):
                emit("pool", e)


class Ctx:
    pass


def build(nc, cfg):
    P = Prog(nc)
    C = Ctx()
    C.nc = nc
    C.P = P
    C.cfg = cfg
    es_top = ExitStack()
    C.es = es_top
    C.uid = 0

    def sb(name, shape, dt):
        C.uid += 1
        return nc.sbuf_tensor("%s_%d" % (name, C.uid), shape, dt)
    C.sb = sb

    def din(name, shape, dt=F32):
        return nc.dram_tensor(name, list(shape), dt, kind="ExternalInput").ap()

    C.x_in = din("x", [S, D])
    C.mem = din("mem", [MEM, D])
    C.pos = din("positions", [1, S], I32)
    names = {
        "ffn1_norm": [L, D], "ffn1_w_in": [L, D, 2 * FFN], "ffn1_w_out": [L, FFN, D],
        "mix_norm": [L, D], "mix_w_in": [L, D, IN_W], "mix_w_out": [L, D, D],
        "fox_f_bias": [L, 4], "fox_q_gain": [L, HD], "fox_k_gain": [L, HD],
        "moba_q_gain": [L, HD], "moba_k_gain": [L, HD],
        "rwkv_mu": [L, 1824], "rwkv_w0": [L, 512], "rwkv_w2": [L, 64, 512], "rwkv_a0": [L, 512],
        "rwkv_a2": [L, 64, 512], "rwkv_g2": [L, 160, 512], "rwkv_k_k": [L, 512], "rwkv_k_a": [L, 512],
        "rwkv_r_k": [L, 512], "rwkv_ln_w": [L, 512], "rwkv_ln_b": [L, 512],
        "xattn_norm": [L, D], "xattn_mem_norm": [L, D], "xattn_w_q": [L, D, 512], "xattn_w_kv": [L, D, 1024],
        "xattn_q_gain": [L, 128], "xattn_k_gain": [L, 128], "xattn_w_out": [L, 512, D],
        "ffn2_norm": [L, D], "ffn2_w_in": [L, D, 2 * FFN], "ffn2_w_out": [L, FFN, D],
    }
    LW = cfg.get("LW", L)
    C.w = {k: din(k, [LW] + list(v[1:])) for k, v in names.items()}
    C.out = nc.dram_tensor("out", [S, D], F32, kind="ExternalOutput").ap()
    C.xtk = [Tk("x%d" % i) for i in range(NT)]

    es = es_top
    C.ident = es.enter_context(C.sb("ident", [128, 128], BF16))
    C.identf = es.enter_context(C.sb("identf", [128, 128], F32))
    C.t_const = Tk("const")
    C.banks = [es.enter_context(nc.psum_tensor("bank%d" % i, [128, 512], F32)) for i in range(8)]
    C.bk = [Tk("bank%d" % i, excl=True) for i in range(8)]
    P.add("pool", lambda e: e.memset(C.identf[:], 1.0), (), (C.t_const,))
    P.add("pool", lambda e: e.affine_select(out=C.identf[:], in_=C.identf[:], pattern=[[-1, 128]],
                                            compare_op=ALU.is_equal, fill=0.0, base=0, channel_multiplier=1),
          (), (C.t_const,))
    P.add("pool", lambda e: e.tensor_copy(out=C.ident[:], in_=C.identf[:]), (), (C.t_const,))

    setup_globals(C)
    src = C.x_in
    plan = cfg.get("plan")
    if plan is None:
        plan = []
        stop = cfg.get("stop_after")
        skip = cfg.get("skip", ())
        for l in range(L):
            for ph in ("ffn1", "mix", "xattn", "ffn2"):
                if ph not in skip:
                    plan.append((ph, l))
                if stop == (ph, l):
                    break
            else:
                continue
            break
    for ph, l in plan:
        if ph in ("ffn1", "ffn2"):
            ffn_phase(C, l, ph, src)
            src = C.out
        elif ph == "mix":
            mixer_phase(C, l, src)
            src = C.out
        elif ph == "rwkv":
            rwkv_phase(C, l, src, C.yr, C.yr_tk)
        elif ph == "attn":
            mixer_phase(C, l, src, do_rwkv=False)
            src = C.out
        elif ph == "xattn":
            xattn_phase(C, l, src)
            src = C.out
    P.finish(es_top)
    return nc


def load_cast(C, dst, dst_tk, w2d, K, N, stg, stg_tk, piece=2048, engs=("dve", "act")):
    P = C.P
    nchunk = K // 128
    i = 0
    for c in range(nchunk):
        for n0 in range(0, N, piece):
            n1 = min(N, n0 + piece)
            sb = i % len(stg)
            s_ap = stg[sb][:, 0:n1 - n0]
            P.add("sp", (lambda s_ap=s_ap, c=c, n0=n0, n1=n1: lambda e: e.dma_start(
                out=s_ap, in_=w2d[c * 128:(c + 1) * 128, n0:n1]))(), (), (stg_tk[sb],), dma=True)
            eng = engs[i % len(engs)]
            if eng == "act":
                P.add(eng, (lambda s_ap=s_ap, c=c, n0=n0, n1=n1: lambda e: e.copy(
                    out=dst[:, c, n0:n1], in_=s_ap))(), (stg_tk[sb],), (dst_tk,))
            else:
                P.add(eng, (lambda s_ap=s_ap, c=c, n0=n0, n1=n1: lambda e: e.tensor_copy(
                    out=dst[:, c, n0:n1], in_=s_ap))(), (stg_tk[sb],), (dst_tk,))
            i += 1


def bcast_load(C, dst, dst_tk, row_ap, n):
    C.P.add("sp", lambda e: e.dma_start(out=dst, in_=row_ap.partition_broadcast(128)), (), (dst_tk,), dma=True)


def norm_transpose(C, xt, xt_tk, gain_bc, gain_tk, hT, hT_tk, col0, scr, bank, nchunks=8):
    P = C.P
    W = 128 * nchunks
    P.add("act", lambda e: e.activation(out=scr["junk"][:, 0:W], in_=xt, func=AF.Square, accum_out=scr["ss"][:]),
          (xt_tk,), (scr["junk_tk"], scr["ss_tk"]))
    P.add("act", lambda e: e.activation(out=scr["std"][:], in_=scr["ss"][:], func=AF.Sqrt, scale=1.0 / W, bias=scr["eps"][:]),
          (scr["ss_tk"], C.t_const), (scr["std_tk"],))
    P.add("dve", lambda e: e.reciprocal(out=scr["rstd"][:], in_=scr["std"][:]), (scr["std_tk"],), (scr["rstd_tk"],))
    P.add("dve", lambda e: e.scalar_tensor_tensor(out=scr["hb"][:, 0:W], in0=xt, scalar=scr["rstd"][:], in1=gain_bc,
                                                  op0=ALU.mult, op1=ALU.mult),
          (xt_tk, scr["rstd_tk"], gain_tk), (scr["hb_tk"],))
    pb = C.banks[bank].bitcast(BF16)
    for c in range(nchunks):
        P.add("pe", (lambda c=c: lambda e: e.transpose(out=pb[:, c * 128:(c + 1) * 128], in_=scr["hb"][:, c * 128:(c + 1) * 128],
                                                       identity=C.ident[:]))(), (scr["hb_tk"], C.t_const), (C.bk[bank],))
    P.add("act", lambda e: e.copy(out=hT[:, 0:nchunks, col0:col0 + 128],
                                  in_=pb[:, 0:W].rearrange("p (c t) -> p c t", c=nchunks)),
          (C.bk[bank],), (hT_tk,))


def make_norm_scratch(C, es, tag):
    nc = C.nc
    scr = {}
    scr["junk"] = es.enter_context(C.sb(tag + "junk", [128, D], BF16))
    scr["hb"] = es.enter_context(C.sb(tag + "hb", [128, D], BF16))
    for k in ("ss", "std", "rstd", "eps"):
        scr[k] = es.enter_context(C.sb(tag + k, [128, 1], F32))
    for k in ("junk", "hb", "ss", "std", "rstd"):
        scr[k + "_tk"] = Tk(tag + k)
    C.P.add("pool", lambda e: e.memset(scr["eps"][:], EPS), (), (C.t_const,))
    return scr


def ffn_phase(C, l, name, src):
    nc, P = C.nc, C.P
    TB = 512
    NTB = TB // 128
    NB = S // TB
    with ExitStack() as es:
        win = es.enter_context(C.sb(name + "win", [128, 8, 2 * FFN], BF16))
        wout = es.enter_context(C.sb(name + "wout", [128, NF, D], BF16))
        win_tk, wout_tk, gbc_tk, hT_tk, aT_tk = Tk("win"), Tk("wout"), Tk("gbc"), Tk("hT"), Tk("aT")
        with ExitStack() as es2:
            stg = [es2.enter_context(C.sb(name + "stg%d" % i, [128, 2816], F32)) for i in range(3)]
            stg_tk = [Tk("stg0"), Tk("stg1"), Tk("stg2")]
            load_cast(C, win, win_tk, C.w[name + "_w_in"][l], D, 2 * FFN, stg, stg_tk, piece=2816)
            load_cast(C, wout, wout_tk, C.w[name + "_w_out"][l], FFN, D, stg, stg_tk, piece=1024)
            P.barrier()
        gbc = es.enter_context(C.sb(name + "gbc", [128, D], F32))
        xt = [es.enter_context(C.sb(name + "xt%d" % i, [128, D], F32)) for i in range(NTB)]
        hT = es.enter_context(C.sb(name + "hT", [128, 8, TB], BF16))
        aT = es.enter_context(C.sb(name + "aT", [128, NF, TB], BF16))
        sg = [es.enter_context(C.sb(name + "sg%d" % i, [128, TB], F32)) for i in range(2)]
        scr = make_norm_scratch(C, es, name)
        xt_tk = [Tk("xt%d" % i) for i in range(NTB)]
        sg_tk = [Tk("sg0"), Tk("sg1")]
        bcast_load(C, gbc[:], gbc_tk, C.w[name + "_norm"][l:l + 1, :], D)
        for b in range(NB):
            for j in range(NTB):
                ti = b * NTB + j
                P.add("sp", (lambda j=j, ti=ti: lambda e: e.dma_start(out=xt[j][:], in_=src[ti * 128:(ti + 1) * 128, :]))(),
                      (C.xtk[ti],), (xt_tk[j],), dma=True)
                norm_transpose(C, xt[j][:], xt_tk[j], gbc[:], gbc_tk, hT, hT_tk, j * 128, scr, 0)
            for f in range(NF):
                bg, bu = 1 + (f % 2), 3 + (f % 2)
                for (bank, col) in ((bg, f * 128), (bu, FFN + f * 128)):
                    for c in range(8):
                        P.add("pe", (lambda bank=bank, col=col, c=c: lambda e: e.matmul(
                            C.banks[bank][:, 0:TB], lhsT=win[:, c, col:col + 128], rhs=hT[:, c, :],
                            start=(c == 0), stop=(c == 7)))(), (win_tk, hT_tk), (C.bk[bank],))
                k = f % 2
                P.add("act", (lambda bg=bg, k=k: lambda e: e.activation(out=sg[k][:], in_=C.banks[bg][:, 0:TB], func=AF.Silu))(),
                      (C.bk[bg],), (sg_tk[k],))
                P.add("dve", (lambda bu=bu, k=k, f=f: lambda e: e.tensor_tensor(out=aT[:, f, :], in0=C.banks[bu][:, 0:TB],
                                                                              in1=sg[k][:], op=ALU.mult))(),
                      (C.bk[bu], sg_tk[k]), (aT_tk,))
            for j in range(NTB):
                ti = b * NTB + j
                for half in range(2):
                    bank = 5 + half
                    for f in range(NF):
                        P.add("pe", (lambda bank=bank, f=f, j=j, half=half: lambda e: e.matmul(
                            C.banks[bank][:, :], lhsT=aT[:, f, j * 128:(j + 1) * 128], rhs=wout[:, f, half * 512:(half + 1) * 512],
                            start=(f == 0), stop=(f == NF - 1)))(), (aT_tk, wout_tk), (C.bk[bank],))
                    P.add("dve", (lambda bank=bank, j=j, half=half: lambda e: e.scalar_tensor_tensor(
                        out=xt[j][:, half * 512:(half + 1) * 512], in0=C.banks[bank][:, :], scalar=0.5,
                        in1=xt[j][:, half * 512:(half + 1) * 512], op0=ALU.mult, op1=ALU.add))(),
                        (C.bk[bank], xt_tk[j]), (xt_tk[j],))
                P.add("pool", (lambda j=j, ti=ti: lambda e: e.dma_start(out=C.out[ti * 128:(ti + 1) * 128, :], in_=xt[j][:]))(),
                      (xt_tk[j],), (C.xtk[ti],), dma=True)
        P.barrier()


TWO_PI = 6.283185307179586
C1 = 6.28125
C2 = TWO_PI - C1
MAGIC = 12582912.0
INVF = [float(np.float32(500000.0) ** (-np.float32(2 * i) / np.float32(16.0))) for i in range(8)]


def cconst(C, val):
    key = float(val)
    if key not in C.consts:
        t = C.es.enter_context(C.sb("c%d" % len(C.consts), [128, 1], F32))
        C.P.add("pool", lambda e: e.memset(t[:], key), (), (C.t_const,))
        C.consts[key] = t
    return C.consts[key]


def setup_globals(C):
    nc, P, es = C.nc, C.P, C.es
    C.consts = {}
    C.tri = es.enter_context(C.sb("tri", [128, 128], BF16))
    C.trif = es.enter_context(C.sb("trif", [128, 128], F32))
    C.onesf = es.enter_context(C.sb("onesf", [128, 128], F32))
    C.onesb = es.enter_context(C.sb("onesb", [128, 512], BF16))
    C.cos = es.enter_context(C.sb("cos", [128, NT, 8], F32))
    C.sin = es.enter_context(C.sb("sin", [128, NT, 8], F32))
    tc_ = (C.t_const,)
    P.add("pool", lambda e: e.memset(C.onesf[:], 1.0), (), tc_)
    P.add("pool", lambda e: e.memset(C.onesb[:], 1.0), (), tc_)
    P.add("pool", lambda e: e.affine_select(out=C.trif[:], in_=C.onesf[:], pattern=[[1, 128]], compare_op=ALU.is_ge,
                                            fill=0.0, base=0, channel_multiplier=-1), tc_, tc_)
    P.add("pool", lambda e: e.tensor_copy(out=C.tri[:], in_=C.trif[:]), tc_, tc_)
    for v in (EPS, 1.0, 0.0, np.pi / 2, 64e-5):
        cconst(C, v)
    C.qaf = [nc.dram_tensor("qaf%d" % h, [66, S], BF16).ap() for h in range(4)]
    C.kaf = [nc.dram_tensor("kaf%d" % h, [66, S], BF16).ap() for h in range(4)]
    C.qam = [nc.dram_tensor("qam%d" % h, [80, S], BF16).ap() for h in range(4)]
    C.kam = [nc.dram_tensor("kam%d" % h, [80, S], BF16).ap() for h in range(4)]
    yk = C.cfg.get("yr_kind")
    C.yr = (nc.dram_tensor("yr", [S, 512], BF16, kind=yk) if yk else nc.dram_tensor("yr", [S, 512], BF16)).ap()
    C.yr_tk = Tk("yr")
    C.qa_tk = {("f", h): Tk() for h in range(4)}
    C.qa_tk.update({("m", h): Tk() for h in range(4)})
    C.ka_tk = {("f", h): Tk() for h in range(4)}
    C.ka_tk.update({("m", h): Tk() for h in range(4)})
    with ExitStack() as s3:
        ohf = s3.enter_context(C.sb("oh_full", [16, S], BF16))
        onf = s3.enter_context(C.sb("ones_full", [16, S], BF16))
        tko = Tk("ohfull")
        P.add("pool", lambda e: e.memset(onf[:], 1.0), (), (tko,))
        P.add("pool", lambda e: e.affine_select(out=ohf[:], in_=onf[:], pattern=[[1, S]], compare_op=ALU.is_ge, fill=0.0,
                                                base=0, channel_multiplier=-256), (tko,), (tko,))
        P.add("pool", lambda e: e.affine_select(out=ohf[:], in_=ohf[:], pattern=[[-1, S]], compare_op=ALU.is_ge, fill=0.0,
                                                base=255, channel_multiplier=256), (tko,), (tko,))
        for h in range(4):
            P.add("sp", (lambda h=h: lambda e: e.dma_start(out=C.kam[h][64:80, :], in_=ohf[:]))(), (tko,), (C.ka_tk[("m", h)],), dma=True)
            P.add("sp", (lambda h=h: lambda e: e.dma_start(out=C.kaf[h][64:66, :], in_=onf[0:2, :]))(), (tko,), (C.ka_tk[("f", h)],), dma=True)
        P.barrier()
    with ExitStack() as s2:
        posi = s2.enter_context(C.sb("posi", [32, 128], I32))
        posf = s2.enter_context(C.sb("posf", [32, 128], F32))
        posT = s2.enter_context(C.sb("posT", [128, NT], F32))
        ang = s2.enter_context(C.sb("ang", [128, NT, 8], F32))
        t1 = s2.enter_context(C.sb("rp1", [128, NT * 8], F32))
        t2 = s2.enter_context(C.sb("rp2", [128, NT * 8], F32))
        r = s2.enter_context(C.sb("rpr", [128, NT * 8], F32))
        tk = Tk("rope")
        P.add("sp", lambda e: e.dma_start(out=posi[:], in_=C.pos.rearrange("o (j p) -> (o j) p", p=128)), (), (tk,), dma=True)
        P.add("dve", lambda e: e.tensor_copy(out=posf[:], in_=posi[:]), (tk,), (tk,))
        P.add("pe", lambda e: e.transpose(out=C.banks[0][:, 0:32], in_=posf[:], identity=C.identf[0:32, 0:32]),
              (tk, C.t_const), (C.bk[0],), mode=(32, 128))
        P.add("dve", lambda e: e.tensor_copy(out=posT[:], in_=C.banks[0][:, 0:32]), (C.bk[0],), (tk,))
        for i in range(8):
            P.add("dve", (lambda i=i: lambda e: e.tensor_scalar(out=ang[:, :, i], in0=posT[:], scalar1=INVF[i], scalar2=None,
                                                                op0=ALU.mult))(), (tk,), (tk,))
        af = ang[:].rearrange("p j i -> p (j i)")
        P.add("dve", lambda e: e.tensor_scalar(out=t1[:], in0=af, scalar1=1.0 / TWO_PI, scalar2=MAGIC, op0=ALU.mult, op1=ALU.add), (tk,), (tk,))
        P.add("dve", lambda e: e.tensor_scalar(out=t2[:], in0=t1[:], scalar1=-MAGIC, scalar2=None, op0=ALU.add), (tk,), (tk,))
        P.add("dve", lambda e: e.scalar_tensor_tensor(out=r[:], in0=t2[:], scalar=-C1, in1=af, op0=ALU.mult, op1=ALU.add), (tk,), (tk,))
        P.add("dve", lambda e: e.scalar_tensor_tensor(out=r[:], in0=t2[:], scalar=-C2, in1=r[:], op0=ALU.mult, op1=ALU.add), (tk,), (tk,))
        P.add("dve", lambda e: e.tensor_scalar(out=t1[:], in0=r[:], scalar1=float(np.pi), scalar2=-TWO_PI, op0=ALU.is_gt, op1=ALU.mult), (tk,), (tk,))
        P.add("dve", lambda e: e.tensor_tensor(out=r[:], in0=r[:], in1=t1[:], op=ALU.add), (tk,), (tk,))
        P.add("dve", lambda e: e.tensor_scalar(out=t1[:], in0=r[:], scalar1=-float(np.pi), scalar2=TWO_PI, op0=ALU.is_lt, op1=ALU.mult), (tk,), (tk,))
        P.add("dve", lambda e: e.tensor_tensor(out=r[:], in0=r[:], in1=t1[:], op=ALU.add), (tk,), (tk,))
        P.add("dve", lambda e: e.tensor_scalar(out=r[:], in0=r[:], scalar1=3.14159, scalar2=-3.14159, op0=ALU.min, op1=ALU.max), (tk,), (tk,))
        P.add("act", lambda e: e.activation(out=C.sin[:].rearrange("p j i -> p (j i)"), in_=r[:], func=AF.Sin), (tk,), tc_)
        P.add("act", lambda e: e.activation(out=t1[:], in_=r[:], func=AF.Abs), (tk,), (tk,))
        P.add("act", lambda e: e.activation(out=C.cos[:].rearrange("p j i -> p (j i)"), in_=t1[:], func=AF.Sin, scale=-1.0,
                                            bias=cconst(C, np.pi / 2)[:]), (tk, C.t_const), tc_)
        P.barrier()


def qk_norm(C, src_psum, ngrp, gains, sq, ssq, std, rstd, dst, tk, eps_ap):
    P = C.P
    W = ngrp * 64
    src_tk, sq_tk, st_tk, dst_tk, g_tk = tk
    P.add("act", lambda e: e.activation(out=sq[:, 0:W], in_=src_psum, func=AF.Square), (src_tk,), (sq_tk,))
    P.add("dve", lambda e: e.tensor_reduce(out=ssq[:, 0:ngrp], in_=sq[:, 0:W].rearrange("p (g d) -> p g d", d=64), axis=AX.X, op=ALU.add),
          (sq_tk,), (st_tk,))
    P.add("act", lambda e: e.activation(out=std[:, 0:ngrp], in_=ssq[:, 0:ngrp], func=AF.Sqrt, scale=1.0 / 64, bias=eps_ap),
          (st_tk, C.t_const), (st_tk,))
    P.add("dve", lambda e: e.reciprocal(out=rstd[:, 0:ngrp], in_=std[:, 0:ngrp]), (st_tk,), (st_tk,))
    P.add("dve", lambda e: e.tensor_tensor(out=dst[:, 0:W].rearrange("p (g d) -> p g d", d=64),
                                           in0=src_psum.rearrange("p (g d) -> p g d", d=64),
                                           in1=rstd[:, 0:ngrp].unsqueeze(2).broadcast_to([128, ngrp, 64]), op=ALU.mult),
          (src_tk, st_tk), (dst_tk,))
    P.add("dve", lambda e: e.tensor_tensor(out=dst[:, 0:W], in0=dst[:, 0:W], in1=gains[:, 0:W], op=ALU.mult), (dst_tk, g_tk), (dst_tk,))


def mixer_phase(C, l, src, do_rwkv=True):
    nc, P = C.nc, C.P
    if do_rwkv and "rwkv" not in C.cfg.get("skip", ()):
        rwkv_phase(C, l, src, C.yr, C.yr_tk)
    with ExitStack() as es:
        ymix = es.enter_context(C.sb("ymix", [128, NT, D], BF16))
        ymix_tk = [Tk("ymix%d" % i) for i in range(NT)]
        if "attn" not in C.cfg.get("skip", ()):
            with ExitStack() as es2:
                vaug = {k: es2.enter_context(C.sb("vaug" + k, [128, NT, 4, 65], BF16)) for k in ("f", "m")}
                vaug_tk = {k: Tk("vaug" + k) for k in ("f", "m")}
                cneg = es2.enter_context(C.sb("cneg", [128, NT, 4], F32))
                cneg_tk = Tk("cneg")
                mixer_prep_attn(C, l, src, vaug, vaug_tk, cneg, cneg_tk)
                P.barrier()
                attn_heads(C, l, vaug, vaug_tk, cneg, cneg_tk, ymix, ymix_tk)
                P.barrier()
        else:
            P.add("pool", lambda e: e.memset(ymix[:, :, 0:256], 0.0), (), tuple(ymix_tk))
            P.add("pool", lambda e: e.memset(ymix[:, :, 768:1024], 0.0), (), tuple(ymix_tk))
        if C.cfg.get("no_outproj"):
            return
        if "rwkv" not in C.cfg.get("skip", ()):
            for i in range(NT):
                P.add("sp", (lambda i=i: lambda e: e.dma_start(out=ymix[:, i, 256:768], in_=C.yr[i * 128:(i + 1) * 128, :]))(),
                      (C.yr_tk,), (ymix_tk[i],), dma=True)
        else:
            P.add("pool", lambda e: e.memset(ymix[:, :, 256:768], 0.0), (), tuple(ymix_tk))
        P.barrier()
        outproj_phase(C, l, src, ymix, ymix_tk)
        P.barrier()


def mixer_prep_attn(C, l, src, vaug, vaug_tk, cneg, cneg_tk):
    nc, P = C.nc, C.P
    C.qk_wr = {(w_, k_, h_): [] for w_ in ("q", "k") for k_ in ("f", "m") for h_ in range(4)}
    NW = RW0
    with ExitStack() as es:
        A = lambda name, shape, dt: es.enter_context(C.sb("mp_" + name, shape, dt))
        win = A("win", [128, 8, NW], BF16)
        stg = [A("stg%d" % i, [128, NW], F32) for i in range(2)]
        gbc = A("gbc", [128, D], F32)
        xt = [A("xt%d" % i, [128, D], F32) for i in range(2)]
        gains = {k: A("g" + k, [128, 512], F32) for k in ("f", "m")}
        fbias = A("fbias", [128, 4], F32)
        stT = A("stT", [128, 8, 512], BF16)
        stM = A("stM", [64, 512], BF16)
        kmT = [A("kmT%d" % i, [128, 16], BF16) for i in range(2)]
        kms = A("kms", [128, 2, 2], F32)
        carry = A("carry", [128, 4], F32)
        cT = A("cT", [4, 512], F32)
        chi = A("chi", [4, 512], BF16)
        chf = A("chf", [4, 512], F32)
        clo = A("clo", [4, 512], BF16)
        oh = A("oh", [16, 512], BF16)
        tks = {n: Tk(n) for n in ("win", "gbc", "hT", "gf", "gm", "fbias", "sq", "st", "qknf", "qknm", "qkb", "rtmp", "stT", "stM",
                                  "km", "gt", "sel", "mbt", "f", "carry", "cT", "chi", "oh")}
        stg_tk = [Tk(), Tk()]
        xt_tk = [Tk(), Tk()]
        eps_ap = cconst(C, EPS)[:, 0:1]
        one_ap = cconst(C, 1.0)[:, 0:1]
        bcast_load(C, gbc[:], tks["gbc"], C.w["mix_norm"][l:l + 1, :], D)
        load_cast(C, win, tks["win"], C.w["mix_w_in"][l][:, 0:NW], D, NW, stg, stg_tk, piece=NW)
        for k, qn, kn in (("f", "fox_q_gain", "fox_k_gain"), ("m", "moba_q_gain", "moba_k_gain")):
            for g0, nm in ((0, qn), (4, kn)):
                P.add("sp", (lambda k=k, g0=g0, nm=nm: lambda e: e.dma_start(
                    out=gains[k][:, g0 * 64:(g0 + 4) * 64].rearrange("p (g d) -> p g d", d=64),
                    in_=C.w[nm][l:l + 1, :].partition_broadcast(128).broadcast_to([128, 4, 64])))(),
                    (), (tks["g" + k],), dma=True)
            P.add("dve", (lambda k=k: lambda e: e.tensor_scalar(out=gains[k][:, 0:256], in0=gains[k][:, 0:256], scalar1=0.125,
                                                                scalar2=None, op0=ALU.mult))(), (tks["g" + k],), (tks["g" + k],))
        P.add("sp", lambda e: e.dma_start(out=fbias[:], in_=C.w["fox_f_bias"][l:l + 1, :].partition_broadcast(128)), (), (tks["fbias"],), dma=True)
        P.add("pool", lambda e: e.memset(carry[:], 0.0), (), (tks["carry"],))
        for k in ("f", "m"):
            P.add("pool", (lambda k=k: lambda e: e.memset(vaug[k][:, :, :, 64:65], 1.0))(), (), (vaug_tk[k],))
        for i in range(2):
            P.add("pool", (lambda i=i: lambda e: e.memset(kmT[i][:], 0.0))(), (), (tks["km"],))
        P.barrier()
        O = Ops(C)
        dup = {}
        for par in range(2):
            d_ = {}
            d_["hT"] = A("hT_%d" % par, [128, 8, 128], BF16)
            d_["sq"] = A("sq_%d" % par, [128, 512], F32)
            for k_ in ("ssq", "std", "rstd"):
                d_[k_] = A("%s_%d" % (k_, par), [128, 8], F32)
            d_["qknf"] = A("qknf_%d" % par, [128, 512], F32)
            d_["qknm"] = A("qknm_%d" % par, [128, 512], F32)
            d_["qkb"] = A("qkb_%d" % par, [128, 1024], BF16)
            d_["rtmp"] = A("rtmp_%d" % par, [128, 4, 8, 8], F32)
            d_["gt"] = A("gt_%d" % par, [128, 4, 16], F32)
            d_["top8"] = A("top8_%d" % par, [128, 4, 8], F32)
            d_["selt"] = A("selt_%d" % par, [128, 4, 16], F32)
            d_["mbt"] = A("mbt_%d" % par, [128, 4, 16], BF16)
            d_["fb"] = A("fb_%d" % par, [128, 4], F32)
            d_["lf"] = A("lf_%d" % par, [128, 4], F32)
            d_["scr"] = make_norm_scratch(C, es, "mp%d" % par)
            d_["tk"] = {}
            dup[par] = d_

        def tile_gen(i):
            par = i % 2
            d_ = dup[par]
            B = 4 * par
            j = par
            g4 = i % 4
            qblk = i // 2

            def t(n):
                if n not in d_["tk"]:
                    d_["tk"][n] = Tk("%s_p%d" % (n, par))
                return d_["tk"][n]
            hT_, sq_, ssq_, std_, rstd_ = d_["hT"], d_["sq"], d_["ssq"], d_["std"], d_["rstd"]
            qkb_, rtmp_, gt_, top8_, selt_, mbt_, fb_, lf_ = d_["qkb"], d_["rtmp"], d_["gt"], d_["top8"], d_["selt"], d_["mbt"], d_["fb"], d_["lf"]
            O.dma(xt[j][:], src[i * 128:(i + 1) * 128, :], (C.xtk[i],), (xt_tk[j],))
            norm_transpose(C, xt[j][:], xt_tk[j], gbc[:], tks["gbc"], hT_, t("hT"), 0, d_["scr"], B + 0)
            yield
            for bank, c0, c1 in ((B + 1, 0, 512), (B + 2, 512, 772), (B + 3, 772, 1284), (B + 0, 1284, 1540)):
                for c in range(8):
                    O.mm(C.banks[bank][:, 0:c1 - c0], hT_[:, c, :], win[:, c, c0:c1], c == 0, c == 7, (t("hT"), tks["win"]), (C.bk[bank],))
                yield
            P.add("act", lambda e: e.copy(out=vaug["f"][:, i, :, 0:64], in_=C.banks[B + 2][:, 0:256].rearrange("p (h d) -> p h d", d=64)),
                  (C.bk[B + 2],), (vaug_tk["f"],))
            P.add("act", lambda e: e.copy(out=vaug["m"][:, i, :, 0:64], in_=C.banks[B + 0][:, 0:256].rearrange("p (h d) -> p h d", d=64)),
                  (C.bk[B + 0],), (vaug_tk["m"],))
            O.tt("dve", fb_[:], C.banks[B + 2][:, 256:260], fbias[:], ALU.add, (C.bk[B + 2], tks["fbias"]), (t("f"),))
            yield
            O.act(lf_[:], fb_[:], AF.Exp, (t("f"),), (t("f"),), scale=-1.0)
            O.act(lf_[:], lf_[:], AF.Ln, (t("f"), C.t_const), (t("f"),), bias=one_ap)
            yield
            for kk_, bnk in (("f", B + 1), ("m", B + 3)):
                qk_norm(C, C.banks[bnk][:, 0:512], 8, gains[kk_], sq_, ssq_, std_, rstd_, d_["qkn" + kk_],
                        (C.bk[bnk], t("sq"), t("st"), t("qkn" + kk_), tks["g" + kk_]), eps_ap)
                yield
            P.add("act", lambda e: e.copy(out=qkb_[:, 0:512], in_=d_["qknf"][:]), (t("qknf"),), (t("qkb"),))
            O.mm(C.banks[B + 0][:, 0:4], C.trif[:], lf_[:], True, True, (t("f"), C.t_const), (C.bk[B + 0],))
            O.mm(C.banks[B + 0][:, 4:8], C.onesf[:], lf_[:], True, True, (t("f"), C.t_const), (C.bk[B + 0],))
            O.tt("dve", cneg[:, i, :], C.banks[B + 0][:, 0:4], carry[:], ALU.add, (C.bk[B + 0], tks["carry"]), (cneg_tk,))
            O.tt("dve", carry[:], C.banks[B + 0][:, 4:8], carry[:], ALU.add, (C.bk[B + 0], tks["carry"]), (tks["carry"],))
            yield
            P.add("pe", lambda e: e.transpose(out=C.banks[B + 0][0:4, 128:256], in_=cneg[:, i, :], identity=C.identf[:]),
                  (cneg_tk, C.t_const), (C.bk[B + 0],), mode=(128, 4))
            O.act(cT[:, g4 * 128:(g4 + 1) * 128], C.banks[B + 0][0:4, 128:256], AF.Copy, (C.bk[B + 0],), (tks["cT"],), scale=-1.0)
            yield
            qv = d_["qknm"][:].rearrange("p (g d) -> p g d", d=64)
            x1, x2 = qv[:, :, 0:8], qv[:, :, 8:16]
            cb = C.cos[:, i:i + 1, :].broadcast_to([128, 8, 8])
            sb = C.sin[:, i:i + 1, :].broadcast_to([128, 8, 8])
            rt, qm = t("rtmp"), t("qknm")
            O.tt("dve", rtmp_[:, 0], x1, cb, ALU.mult, (qm, C.t_const), (rt,))
            O.tt("dve", rtmp_[:, 1], x2, sb, ALU.mult, (qm, C.t_const), (rt,))
            O.tt("dve", rtmp_[:, 2], x2, cb, ALU.mult, (qm, C.t_const), (rt,))
            O.tt("dve", rtmp_[:, 3], x1, sb, ALU.mult, (qm, C.t_const), (rt,))
            yield
            O.tt("dve", x1, rtmp_[:, 0], rtmp_[:, 1], ALU.subtract, (rt,), (qm,))
            O.tt("dve", x2, rtmp_[:, 2], rtmp_[:, 3], ALU.add, (rt,), (qm,))
            P.add("act", lambda e: e.copy(out=qkb_[:, 512:1024], in_=d_["qknm"][:]), (qm,), (t("qkb"),))
            yield
            pbq = C.banks[B + 1].bitcast(BF16)
            for blk in range(8):
                O.tr(pbq[:, blk * 128:(blk + 1) * 128], qkb_[:, blk * 128:(blk + 1) * 128], C.ident[:], (t("qkb"), C.t_const), (C.bk[B + 1],))
            P.add("act", lambda e: e.copy(out=stT[:, :, g4 * 128:(g4 + 1) * 128], in_=pbq[:, :].rearrange("p (b t) -> p b t", b=8)),
                  (C.bk[B + 1],), (tks["stT"],))
            yield
            if qblk > 0:
                for h in range(4):
                    pr = (h % 2) * 64
                    gb = B + 2 + (h % 2)
                    P.add("pe", (lambda h=h, pr=pr, gb=gb: lambda e: e.matmul(
                        C.banks[gb][:, h * 16:(h + 1) * 16], lhsT=stT[pr:pr + 64, 4 + h // 2, g4 * 128:(g4 + 1) * 128],
                        rhs=kmT[h // 2][pr:pr + 64, :], start=True, stop=True))(), (tks["stT"], tks["km"]), (C.bk[gb],), mode=(64, 128))
                for p2 in range(2):
                    P.add("dve", (lambda p2=p2: lambda e: e.tensor_copy(
                        out=gt_[:, p2:4:2, :], in_=C.banks[B + 2 + p2][:, 0:64].rearrange("p (h n) -> p h n", n=16)[:, p2:4:2, :]))(),
                        (C.bk[B + 2 + p2],), (t("gt"),))
                P.add("dve", lambda e: e.memset(gt_[:, :, qblk:16], -1e30), (), (t("gt"),))
                yield
                for h in range(4):
                    P.add("dve", (lambda h=h: lambda e: e.max(out=top8_[:, h, :], in_=gt_[:, h, :]))(), (t("gt"),), (t("sel"),))
                yield
                for h in range(4):
                    P.add("dve", (lambda h=h: lambda e: e.tensor_scalar(out=selt_[:, h, :], in0=gt_[:, h, :], scalar1=top8_[:, h, 2:3], scalar2=None,
                                                                        op0=ALU.is_ge))(), (t("gt"), t("sel")), (t("sel"),))
                yield
                O.ts("dve", mbt_[:], selt_[:], -NEG, ALU.mult, (t("sel"),), (t("mbt"),), s2=NEG, op1=ALU.add)
                if qblk < 15:
                    P.add("dve", lambda e: e.memset(mbt_[:, :, qblk + 1:16], NEG), (), (t("mbt"),))
                P.add("dve", lambda e: e.memset(mbt_[:, :, qblk:qblk + 1], 0.0), (), (t("mbt"),))
            else:
                P.add("dve", lambda e: e.memset(mbt_[:], NEG), (), (t("mbt"),))
                P.add("dve", lambda e: e.memset(mbt_[:, :, 0:1], 0.0), (), (t("mbt"),))
            yield
            pbm = C.banks[B + 3].bitcast(BF16)
            P.add("pe", lambda e: e.transpose(out=pbm[0:64, 0:128], in_=mbt_[:].rearrange("p h n -> p (h n)"), identity=C.ident[:]),
                  (t("mbt"), C.t_const), (C.bk[B + 3],), mode=(128, 64))
            P.add("act", lambda e: e.copy(out=stM[:, g4 * 128:(g4 + 1) * 128], in_=pbm[0:64, 0:128]), (C.bk[B + 3],), (tks["stM"],))
            yield

        for m in range(C.cfg.get("prep_tiles", NT) // 2):
            gens = [tile_gen(2 * m), tile_gen(2 * m + 1)]
            while gens:
                for g_ in list(gens):
                    try:
                        next(g_)
                    except StopIteration:
                        gens.remove(g_)
            i = 2 * m + 1
            g4 = i % 4
            qblk = m
            c0 = (g4 - 1) * 128
            for hp in range(2):
                P.add("dve", (lambda hp=hp, c0=c0: lambda e: e.tensor_reduce(out=kms[:, hp, 0:1], in_=stT[:, 6 + hp, c0:c0 + 256], axis=AX.X, op=ALU.add))(),
                      (tks["stT"],), (tks["km"],))
                P.add("dve", (lambda hp=hp, qblk=qblk: lambda e: e.tensor_scalar(out=kmT[hp][:, qblk:qblk + 1], in0=kms[:, hp, 0:1], scalar1=1.0 / 256,
                                                                                  scalar2=None, op0=ALU.mult))(), (tks["km"],), (tks["km"],))
            if g4 == 3:
                t0 = (i - 3) * 128
                P.add("dve", lambda e: e.tensor_copy(out=chi[:], in_=cT[:]), (tks["cT"],), (tks["chi"],))
                P.add("dve", lambda e: e.tensor_copy(out=chf[:], in_=chi[:]), (tks["chi"],), (tks["chi"],))
                P.add("dve", lambda e: e.tensor_tensor(out=clo[:], in0=cT[:], in1=chf[:], op=ALU.subtract), (tks["cT"], tks["chi"]), (tks["chi"],))
                def wtk(kind_, which, h_):
                    t_ = Tk()
                    C.qk_wr[(which, kind_, h_)].append(t_)
                    return t_
                for h in range(4):
                    P.add("pool", (lambda h=h, t0=t0: lambda e: e.dma_start(out=C.qaf[h][64:65, t0:t0 + 512], in_=chi[h:h + 1, :]))(),
                          (tks["chi"],), (wtk("f", "q", h),), dma=True)
                    P.add("pool", (lambda h=h, t0=t0: lambda e: e.dma_start(out=C.qaf[h][65:66, t0:t0 + 512], in_=clo[h:h + 1, :]))(),
                          (tks["chi"],), (wtk("f", "q", h),), dma=True)
                for h in range(4):
                    pr = (h % 2) * 64
                    for (dst, kind_, which, blk) in ((C.qaf[h], "f", "q", 0 + h // 2), (C.kaf[h], "f", "k", 2 + h // 2),
                                                     (C.qam[h], "m", "q", 4 + h // 2), (C.kam[h], "m", "k", 6 + h // 2)):
                        P.add("pool", (lambda dst=dst, blk=blk, pr=pr, t0=t0: lambda e: e.dma_start(
                            out=dst[0:64, t0:t0 + 512], in_=stT[pr:pr + 64, blk, :]))(), (tks["stT"],), (wtk(kind_, which, h),), dma=True)
                    P.add("pool", (lambda h=h, t0=t0: lambda e: e.dma_start(out=C.qam[h][64:80, t0:t0 + 512], in_=stM[h * 16:(h + 1) * 16, :]))(),
                          (tks["stM"],), (wtk("m", "q", h),), dma=True)


def attn_heads(C, l, vaug, vaug_tk, cneg, cneg_tk, ymix, ymix_tk):
    nc, P = C.nc, C.P
    with ExitStack() as es:
        A = lambda name, shape, dt: es.enter_context(C.sb("at_" + name, shape, dt))
        qa = [A("qa%d" % i, [80, S], BF16) for i in range(2)]
        ka = [A("ka%d" % i, [80, S], BF16) for i in range(2)]
        pt = [A("pt%d" % i, [128, 512], BF16) for i in range(4)]
        rec = A("rec", [128, 4], F32)
        qa_tk = [Tk(), Tk()]
        ka_tk = [Tk(), Tk()]
        pt_tk = [Tk(), Tk(), Tk(), Tk()]
        rec_tk = Tk()
        zero_ap = cconst(C, 0.0)[:, 0:1]
        hi = 0
        pti = 0
        blk = 0
        pending = []
        for kind, KA, ycol in (("f", 66, 0), ("m", 80, 768)):
            for h in range(C.cfg.get("attn_heads", 4)):
                b = hi % 2
                hi += 1
                qsrc = (C.qaf if kind == "f" else C.qam)[h]
                ksrc = (C.kaf if kind == "f" else C.kam)[h]
                P.add("sp", (lambda b=b, qsrc=qsrc, KA=KA: lambda e: e.dma_start(out=qa[b][0:KA, :], in_=qsrc[:, :]))(),
                      tuple(C.qk_wr[("q", kind, h)]) + (C.qa_tk[(kind, h)],), (qa_tk[b],), dma=True)
                P.add("sp", (lambda b=b, ksrc=ksrc, KA=KA: lambda e: e.dma_start(out=ka[b][0:KA, :], in_=ksrc[:, :]))(),
                      tuple(C.qk_wr[("k", kind, h)]) + (C.ka_tk[(kind, h)],), (ka_tk[b],), dma=True)
                for qb in range(8):
                    ob = 2 + (blk % 2)
                    blk += 1
                    first = True
                    for j in range(4 * qb + 4):
                        jj = j - 4 * qb
                        c0 = 0 if jj < 0 else jj * 128
                        sbk = (0, 1, 4)[pti % 3]
                        p = pti % 4
                        pti += 1
                        P.add("pe", (lambda sbk=sbk, b=b, j=j, qb=qb, c0=c0, KA=KA: lambda e: e.matmul(
                            C.banks[sbk][:, c0:512], lhsT=ka[b][0:KA, j * 128:(j + 1) * 128], rhs=qa[b][0:KA, qb * 512 + c0:(qb + 1) * 512],
                            start=True, stop=True))(), (ka_tk[b], qa_tk[b]), (C.bk[sbk],))
                        if kind == "f":
                            bias_ap = cneg[:, j, h:h + 1]
                            rd = (C.bk[sbk], cneg_tk)
                        else:
                            bias_ap = zero_ap
                            rd = (C.bk[sbk], C.t_const)
                        P.add("act", (lambda sbk=sbk, p=p, c0=c0, bias_ap=bias_ap: lambda e: e.activation(
                            out=pt[p][:, c0:512], in_=C.banks[sbk][:, c0:512], func=AF.Exp, bias=bias_ap))(), rd, (pt_tk[p],))
                        if jj >= 0:
                            P.add("pool", (lambda p=p, c0=c0: lambda e: e.tensor_tensor(out=pt[p][:, c0:c0 + 128], in0=pt[p][:, c0:c0 + 128],
                                                                                        in1=C.tri[:], op=ALU.mult))(), (pt_tk[p], C.t_const), (pt_tk[p],))
                        if len(pending) >= 2:
                            pending.pop(0)()

                        def pv_step(ob=ob, p=p, j=j, h=h, kind=kind, first=first, qb=qb, jj=jj, ycol=ycol, last=(j == 4 * qb + 3)):
                            fst = first
                            for ii in range(max(jj, 0), 4):
                                qt = 4 * qb + ii
                                P.add("pe", (lambda ii=ii, fst=fst, qt=qt: lambda e: e.matmul(
                                    C.banks[ob][:, ii * 128:ii * 128 + 65], lhsT=pt[p][:, ii * 128:(ii + 1) * 128], rhs=vaug[kind][:, j, h, :],
                                    start=fst, stop=(j == qt), skip_group_check=True))(), (pt_tk[p], vaug_tk[kind]), (C.bk[ob],))
                                fst = False
                            if last:
                                ov = C.banks[ob][:, :].rearrange("p (i c) -> p i c", c=128)
                                P.add("dve", lambda e: e.reciprocal(out=rec[:], in_=ov[:, :, 64]), (C.bk[ob],), (rec_tk,))
                                for ii in range(4):
                                    qt = 4 * qb + ii
                                    P.add("dve", (lambda ii=ii, qt=qt: lambda e: e.tensor_scalar(
                                        out=ymix[:, qt, ycol + h * 64:ycol + (h + 1) * 64], in0=ov[:, ii, 0:64], scalar1=rec[:, ii:ii + 1], scalar2=None,
                                        op0=ALU.mult))(), (C.bk[ob], rec_tk), (ymix_tk[qt],))
                        pending.append(pv_step)
                        first = False
        while pending:
            pending.pop(0)()


def outproj_phase(C, l, src, ymix, ymix_tk):
    nc, P = C.nc, C.P
    with ExitStack() as es:
        A = lambda name, shape, dt: es.enter_context(C.sb("op_" + name, shape, dt))
        wo = A("wo", [128, 8, D], BF16)
        stg = [A("stg%d" % i, [128, D], F32) for i in range(2)]
        xt = [A("xt%d" % i, [128, D], F32) for i in range(2)]
        yT = [A("yT%d" % i, [128, 8, 128], BF16) for i in range(2)]
        wo_tk, stg_tk, xt_tk, yT_tk = Tk(), [Tk(), Tk()], [Tk(), Tk()], [Tk(), Tk()]
        load_cast(C, wo, wo_tk, C.w["mix_w_out"][l], D, D, stg, stg_tk, piece=D)
        pend = []
        for i in range(NT):
            j = i % 2
            P.add("sp", (lambda j=j, i=i: lambda e: e.dma_start(out=xt[j][:], in_=src[i * 128:(i + 1) * 128, :]))(),
                  (C.xtk[i],), (xt_tk[j],), dma=True)
            pb = C.banks[j].bitcast(BF16)
            for c in range(8):
                P.add("pe", (lambda pb=pb, c=c, i=i: lambda e: e.transpose(out=pb[:, c * 128:(c + 1) * 128], in_=ymix[:, i, c * 128:(c + 1) * 128],
                                                                          identity=C.ident[:]))(), (ymix_tk[i], C.t_const), (C.bk[j],))
            P.add("act", (lambda pb=pb, j=j: lambda e: e.copy(out=yT[j][:], in_=pb[:, :].rearrange("p (c t) -> p c t", c=8)))(),
                  (C.bk[j],), (yT_tk[j],))

            def mm_step(i=i, j=j):
                for half in range(2):
                    bank = 2 + 2 * j + half
                    for c in range(8):
                        P.add("pe", (lambda bank=bank, c=c, half=half: lambda e: e.matmul(
                            C.banks[bank][:, :], lhsT=yT[j][:, c, :], rhs=wo[:, c, half * 512:(half + 1) * 512], start=(c == 0), stop=(c == 7)))(),
                            (yT_tk[j], wo_tk), (C.bk[bank],))
                    P.add("dve", (lambda bank=bank, half=half: lambda e: e.tensor_tensor(
                        out=xt[j][:, half * 512:(half + 1) * 512], in0=C.banks[bank][:, :], in1=xt[j][:, half * 512:(half + 1) * 512], op=ALU.add))(),
                        (C.bk[bank], xt_tk[j]), (xt_tk[j],))
                P.add("sp", lambda e: e.dma_start(out=C.out[i * 128:(i + 1) * 128, :], in_=xt[j][:]), (xt_tk[j],), (C.xtk[i],), dma=True)
            if pend:
                pend.pop(0)()
            pend.append(mm_step)
        while pend:
            pend.pop(0)()


class Ops:
    def __init__(self, C):
        self.P = C.P
        self.C = C

    def tt(self, eng, out, in0, in1, op, rd, wr):
        self.P.add(eng, lambda e: e.tensor_tensor(out=out, in0=in0, in1=in1, op=op), rd, wr)

    def ts(self, eng, out, in0, s1, op0, rd, wr, s2=None, op1=None):
        if op1 is None:
            self.P.add(eng, lambda e: e.tensor_scalar(out=out, in0=in0, scalar1=s1, scalar2=None, op0=op0), rd, wr)
        else:
            self.P.add(eng, lambda e: e.tensor_scalar(out=out, in0=in0, scalar1=s1, scalar2=s2, op0=op0, op1=op1), rd, wr)

    def stt(self, out, in0, scalar, in1, op0, op1, rd, wr):
        self.P.add("dve", lambda e: e.scalar_tensor_tensor(out=out, in0=in0, scalar=scalar, in1=in1, op0=op0, op1=op1), rd, wr)

    def act(self, out, in_, func, rd, wr, scale=1.0, bias=None):
        if bias is None:
            self.P.add("act", lambda e: e.activation(out=out, in_=in_, func=func, scale=scale), rd, wr)
        else:
            self.P.add("act", lambda e: e.activation(out=out, in_=in_, func=func, scale=scale, bias=bias), rd, tuple(wr))

    def red(self, out, in_, rd, wr):
        self.P.add("dve", lambda e: e.tensor_reduce(out=out, in_=in_, axis=AX.X, op=ALU.add), rd, wr)

    def rcp(self, out, in_, rd, wr):
        self.P.add("dve", lambda e: e.reciprocal(out=out, in_=in_), rd, wr)

    def mm(self, out, lhsT, rhs, start, stop, rd, wr):
        mode = (lhsT.shape[0], int(np.prod(lhsT.shape[1:])))
        self.P.add("pe", lambda e: e.matmul(out, lhsT=lhsT, rhs=rhs, start=start, stop=stop, skip_group_check=True), rd, wr, mode=mode)

    def tr(self, out, in_, ident, rd, wr):
        mode = (in_.shape[0], int(np.prod(in_.shape[1:])))
        self.P.add("pe", lambda e: e.transpose(out=out, in_=in_, identity=ident), rd, wr, mode=mode)

    def dma(self, out, in_, rd, wr, eng="sp"):
        self.P.add(eng, lambda e: e.dma_start(out=out, in_=in_), rd, wr, dma=True)

    def memset(self, eng, ap, val, wr):
        self.P.add(eng, lambda e: e.memset(ap, val), (), wr)


EW = 0.6065306597126334
RWN = 1824


def rwkv_phase(C, l, src, yr, yr_tk):
    nc, P = C.nc, C.P
    O = Ops(C)
    with ExitStack() as es:
        A = lambda name, shape, dt: es.enter_context(C.sb("rw_" + name, shape, dt))
        wa = A("wa", [128, 8, RWN], BF16)
        wb = A("wb", [128, 8, RWN], BF16)
        w2b, a2b = A("w2b", [128, 512], BF16), A("a2b", [128, 512], BF16)
        g2b, g2b2 = A("g2b", [128, 512], BF16), A("g2b2", [128, 512], BF16)
        T = {}

        def tk(n):
            if n not in T:
                T[n] = Tk(n)
            return T[n]
        tcn = C.t_const
        xt_tk = [Tk(), Tk()]
        eps_ap = cconst(C, EPS)[:, 0:1]
        gneps_ap = cconst(C, 64e-5)[:, 0:1]
        with ExitStack() as es3:
            mub = es3.enter_context(C.sb("rw_mub", [128, RWN], F32))
            omm = es3.enter_context(C.sb("rw_omm", [128, RWN], F32))
            stg = [es3.enter_context(C.sb("rw_stg%d" % i, [128, RWN], F32)) for i in range(2)]
            stg_tk = [Tk(), Tk()]
            O.dma(mub[:], C.w["rwkv_mu"][l:l + 1, :].partition_broadcast(128), (), (tk("mub"),))
            O.ts("dve", omm[:], mub[:], -1.0, ALU.mult, (tk("mub"),), (tk("mub"),), s2=1.0, op1=ALU.add)
            for c in range(8):
                sb = c % 2
                O.dma(stg[sb][:], C.w["mix_w_in"][l][c * 128:(c + 1) * 128, RW0:RW0 + RWN], (), (stg_tk[sb],))
                O.tt("dve", wa[:, c, :], stg[sb][:], omm[:], ALU.mult, (stg_tk[sb], tk("mub")), (tk("wa"),))
                O.tt("pool", wb[:, c, :], stg[sb][:], mub[:], ALU.mult, (stg_tk[sb], tk("mub")), (tk("wa"),))
            for dst in (w2b, a2b, g2b2):
                O.memset("pool", dst[:], 0.0, (tk("lw2"),))
            for (dst, nm, r0, r1) in ((w2b, "rwkv_w2", 0, 64), (a2b, "rwkv_a2", 0, 64), (g2b, "rwkv_g2", 0, 128), (g2b2, "rwkv_g2", 128, 160)):
                sb = 0
                O.dma(stg[sb][0:r1 - r0, 0:512], C.w[nm][l][r0:r1, :], (), (stg_tk[sb],))
                O.P.add("dve", (lambda dst=dst, n=r1 - r0: lambda e: e.tensor_copy(out=dst[0:n, :], in_=stg[0][0:n, 0:512]))(), (stg_tk[sb],), (tk("lw2"),))
            P.barrier()
        gbc = A("gbc", [128, D], F32)
        xt = [A("xt%d" % i, [128, D], F32) for i in range(2)]
        hTx = A("hTx", [128, 8, 129], BF16)
        hT = A("hT", [128, 8, 128], BF16)
        bc = {k: A("bc_" + k, [128, 512], F32) for k in ("w0", "a0", "kk", "ka", "rk", "lnw", "lnb")}
        tw, al = A("tw", [128, 128], BF16), A("al", [128, 128], BF16)
        sg1, sg2 = A("sg1", [128, 128], BF16), A("sg2", [128, 128], BF16)
        BTm, BOm = A("BTm", [128, 128], F32), A("BOm", [128, 128], F32)
        Msu, Miu, Msl = A("Msu", [128, 8, 64], F32), A("Miu", [128, 8, 64], F32), A("Msl", [128, 8, 64], F32)
        W = {k: A("w_" + k, [128, 512], F32) for k in (
            "sw", "a", "kkn", "kmod", "b", "lc", "t1", "At", "Bt", "Kt", "Rt", "Gt", "Bh", "Kh", "Vt", "g",
            "AtT", "BtT", "KtT", "RtT", "GtT", "Q0", "P0", "Q1", "P1", "Nak", "Nrb", "Nrk", "X", "AKV", "RKV", "W0", "U", "Y", "yc", "rr", "kr")}
        H = A("H", [128, 256], F32)
        sm = {k: A("s_" + k, [128, 8], F32) for k in ("n1", "n2", "rks", "m1", "m2", "m3")}
        outb = A("outb", [128, 512], BF16)
        scr = make_norm_scratch(C, es, "rw")
        bcast_load(C, gbc[:], tk("gbc"), C.w["mix_norm"][l:l + 1, :], D)
        for k, nm in (("w0", "rwkv_w0"), ("a0", "rwkv_a0"), ("kk", "rwkv_k_k"), ("ka", "rwkv_k_a"), ("rk", "rwkv_r_k"),
                      ("lnw", "rwkv_ln_w"), ("lnb", "rwkv_ln_b")):
            O.dma(bc[k][:], C.w[nm][l:l + 1, :].partition_broadcast(128), (), (tk("bc"),))
        O.ts("pool", BTm[:], C.trif[:], -EW, ALU.mult, (tcn,), (tk("msk"),))
        O.memset("pool", BTm[0:64, 64:128], 0.0, (tk("msk"),))
        O.memset("pool", BOm[:], 0.0, (tk("msk"),))
        O.memset("pool", BOm[0:64, 0:64], -EW, (tk("msk"),))
        O.memset("pool", BOm[64:128, 64:128], -EW, (tk("msk"),))
        for M_, pat, cm, op in ((Msu, [[0, 8], [1, 64]], -1, ALU.is_gt), (Miu, [[0, 8], [1, 64]], -1, ALU.is_ge),
                                (Msl, [[0, 8], [-1, 64]], 1, ALU.is_gt)):
            O.memset("pool", M_[:], 1.0, (tk("msk"),))
            for hf in range(2):
                P.add("pool", (lambda M_=M_, pat=pat, cm=cm, op=op, hf=hf: lambda e: e.affine_select(
                    out=M_[hf * 64:(hf + 1) * 64], in_=M_[hf * 64:(hf + 1) * 64], pattern=pat, compare_op=op, fill=0.0, base=0,
                    channel_multiplier=cm))(), (tk("msk"),), (tk("msk"),))
        O.memset("pool", H[:], 0.0, (tk("H"),))
        for t_ in (tw, al, sg2):
            O.memset("pool", t_[:], 0.0, (tk("lora"),))
        O.memset("pool", hTx[:, :, 0:1], 0.0, (tk("hTx"),))
        msk = tk("msk")
        f512 = lambda t: t[:]
        v8 = lambda ap: ap.rearrange("p (g d) -> p g d", d=64)
        b8 = lambda t: t[:, 0:8].unsqueeze(2).broadcast_to([128, 8, 64])
        bki = [1]

        def nb():
            bki[0] = bki[0] % 7 + 1
            return bki[0]

        CROSS = ("Bh", "Kh", "Vt", "g", "AtT", "RtT", "GtT", "X", "Nrb", "AKV", "RKV")
        W2 = {k: A("w2_" + k, [128, 512], F32) for k in CROSS}
        rks2 = A("s_rks2", [128, 8], F32)

        def Wp(nm, par):
            return W2[nm] if (par == 1 and nm in W2) else W[nm]

        XTp = lambda nm, h, c2, par: Wp(nm, par)[(h % 2) * 64:(h % 2) * 64 + 64, (h // 2) * 128 + c2 * 64:(h // 2) * 128 + c2 * 64 + 64]
        pv = lambda ap, par_: ap.rearrange("p (q two d) -> p q two d", two=2, d=64)[:, :, par_, :]
        bct = tk("bc")

        def stageA(i):
            par = i % 2
            j = i % 2
            wk, tT, am = tk("a_wk"), tk("a_tT"), tk("a_am")
            xwk, xtT, xam = tk("x_wk%d" % par), tk("x_tT%d" % par), tk("x_am%d" % par)
            rks = rks2 if par else sm["rks"]
            O.dma(xt[j][:], src[i * 128:(i + 1) * 128, :], (C.xtk[i],), (xt_tk[j],))
            norm_transpose(C, xt[j][:], xt_tk[j], gbc[:], tk("gbc"), hT, tk("hT"), 0, scr, 0)
            P.add("pool", lambda e: e.tensor_copy(out=hTx[:, :, 1:129], in_=hT[:]), (tk("hT"),), (tk("hTx"),))
            yield
            for dst_, dtk_, c0 in ((W["rr"], wk, 0), (W["kr"], wk, 512), (Wp("Vt", par), xwk, 1024)):
                bank = nb()
                n = 0
                for c in range(8):
                    for (w_, lo) in ((wa, 1), (wb, 0)):
                        O.mm(C.banks[bank][:, :], hTx[:, c, lo:lo + 128], w_[:, c, c0:c0 + 512], n == 0, n == 15, (tk("hTx"), tk("wa")), (C.bk[bank],))
                        n += 1
                O.act(dst_[:], C.banks[bank][:, :], AF.Copy, (C.bk[bank],), (dtk_,))
                yield
            b4 = nb()
            for (r0, r1, col, c0) in ((0, 64, 0, 1536), (0, 64, 128, 1600), (0, 128, 256, 1664), (0, 32, 384, 1792)):
                n = 0
                for c in range(8):
                    for (w_, lo) in ((wa, 1), (wb, 0)):
                        O.mm(C.banks[b4][r0:r1, col:col + 128], w_[:, c, c0:c0 + (r1 - r0)], hTx[:, c, lo:lo + 128], n == 0, n == 15,
                             (tk("hTx"), tk("wa")), (C.bk[b4],))
                        n += 1
            P.add("pool", lambda e: e.tensor_copy(out=hTx[:, :, 0:1], in_=hT[:, :, 127:128]), (tk("hT"), tk("hTx")), (tk("hTx"),))
            O.act(tw[0:64, :], C.banks[b4][0:64, 0:128], AF.Tanh, (C.bk[b4],), (tk("lora"),))
            O.act(al[0:64, :], C.banks[b4][0:64, 128:256], AF.Copy, (C.bk[b4],), (tk("lora"),))
            O.act(sg1[:], C.banks[b4][0:128, 256:384], AF.Sigmoid, (C.bk[b4],), (tk("lora"),))
            O.act(sg2[0:32, :], C.banks[b4][0:32, 384:512], AF.Sigmoid, (C.bk[b4],), (tk("lora"),))
            yield
            b5, b6, b7 = nb(), nb(), nb()
            O.mm(C.banks[b5][:, :], tw[:], w2b[:], True, True, (tk("lora"), tk("lw2")), (C.bk[b5],))
            O.mm(C.banks[b6][:, :], al[:], a2b[:], True, True, (tk("lora"), tk("lw2")), (C.bk[b6],))
            O.mm(C.banks[b7][:, :], sg1[:], g2b[:], True, False, (tk("lora"), tk("lw2")), (C.bk[b7],))
            O.mm(C.banks[b7][:, :], sg2[:], g2b2[:], False, True, (tk("lora"), tk("lw2")), (C.bk[b7],))
            O.tt("dve", W["t1"][:], C.banks[b5][:, :], bc["w0"][:], ALU.add, (C.bk[b5], bct), (wk,))
            O.act(W["sw"][:], W["t1"][:], AF.Sigmoid, (wk,), (wk,))
            O.tt("dve", W["t1"][:], C.banks[b6][:, :], bc["a0"][:], ALU.add, (C.bk[b6], bct), (wk,))
            O.act(W["a"][:], W["t1"][:], AF.Sigmoid, (wk,), (wk,))
            O.act(Wp("g", par)[:], C.banks[b7][:, :], AF.Copy, (C.bk[b7],), (xwk,))
            yield
            b5, b6 = nb(), nb()
            O.mm(C.banks[b5][:, :], BTm[:], W["sw"][:], True, True, (wk, msk), (C.bk[b5],))
            O.mm(C.banks[b6][:, :], BOm[:], W["sw"][:], True, True, (wk, msk), (C.bk[b6],))
            O.act(W["lc"][:], C.banks[b5][:, :], AF.Copy, (C.bk[b5],), (wk,))
            Bh, Kh = Wp("Bh", par), Wp("Kh", par)
            O.tt("dve", Bh[:], C.banks[b6][:, :], W["lc"][:], ALU.subtract, (C.bk[b6], wk), (xwk,))
            O.act(W["Gt"][:], C.banks[b6][:, :], AF.Exp, (C.bk[b6],), (wk,))
            yield
            O.tt("dve", W["kkn"][:], W["kr"][:], bc["kk"][:], ALU.mult, (wk, bct), (wk,))
            O.act(W["t1"][:], W["kkn"][:], AF.Square, (wk,), (wk,))
            O.red(sm["n1"][:], v8(W["t1"][:]), (wk,), (wk,))
            O.act(sm["n2"][:], sm["n1"][:], AF.Sqrt, (wk,), (wk,))
            O.ts("dve", sm["n2"][:], sm["n2"][:], 1e-12, ALU.max, (wk,), (wk,))
            O.rcp(sm["n1"][:], sm["n2"][:], (wk,), (wk,))
            O.tt("dve", v8(W["kkn"][:]), v8(W["kkn"][:]), b8(sm["n1"]), ALU.mult, (wk,), (wk,))
            yield
            O.stt(W["t1"][:], W["a"][:], -1.0, bc["ka"][:], ALU.add, ALU.mult, (wk, bct), (wk,))
            O.stt(W["kmod"][:], W["t1"][:], 1.0, W["kr"][:], ALU.add, ALU.mult, (wk,), (wk,))
            O.tt("dve", W["b"][:], W["kkn"][:], W["a"][:], ALU.mult, (wk,), (wk,))
            O.tt("dve", W["t1"][:], W["rr"][:], bc["rk"][:], ALU.mult, (wk, bct), (wk,))
            O.tt("dve", W["t1"][:], W["t1"][:], W["kmod"][:], ALU.mult, (wk,), (wk,))
            O.red(rks[:], v8(W["t1"][:]), (wk,), (xwk,))
            yield
            O.stt(W["At"][:], W["sw"][:], EW, W["lc"][:], ALU.mult, ALU.add, (wk,), (wk,))
            O.act(W["At"][:], W["At"][:], AF.Exp, (wk,), (wk,))
            O.stt(W["At"][:], W["kkn"][:], -1.0, W["At"][:], ALU.mult, ALU.mult, (wk,), (wk,))
            O.act(W["Bt"][:], W["lc"][:], AF.Exp, (wk,), (wk,), scale=-1.0)
            O.tt("dve", W["Kt"][:], W["kmod"][:], W["Bt"][:], ALU.mult, (wk,), (wk,))
            O.tt("dve", W["Bt"][:], W["b"][:], W["Bt"][:], ALU.mult, (wk,), (wk,))
            yield
            O.act(W["Rt"][:], W["lc"][:], AF.Exp, (wk,), (wk,))
            O.tt("dve", W["Rt"][:], W["rr"][:], W["Rt"][:], ALU.mult, (wk,), (wk,))
            O.act(Bh[:], Bh[:], AF.Exp, (xwk,), (xwk,))
            O.tt("dve", Kh[:], W["kmod"][:], Bh[:], ALU.mult, (wk, xwk), (xwk,))
            O.tt("dve", Bh[:], W["b"][:], Bh[:], ALU.mult, (wk, xwk), (xwk,))
            yield
            for nm in ("At", "Bt", "Kt", "Rt", "Gt"):
                bank = nb()
                for blk_ in range(4):
                    O.tr(C.banks[bank][:, blk_ * 128:(blk_ + 1) * 128], W[nm][:, blk_ * 128:(blk_ + 1) * 128], C.identf[:], (wk, tcn), (C.bk[bank],))
                O.act(Wp(nm + "T", par)[:], C.banks[bank][:, :], AF.Copy, (C.bk[bank],), (xtT if nm in ("At", "Rt", "Gt") else tT,))
                yield

            def mm_d(lnm, rnm):
                bb = [nb(), nb()]
                for c2 in (0, 1):
                    for h in range(8):
                        par_ = h % 2
                        O.mm(C.banks[bb[par_]][c2 * 64:(c2 + 1) * 64, h * 64:(h + 1) * 64], XTp(lnm, h, c2, par), XTp(rnm, h, c2, par), True, True,
                             (tT, xtT), (C.bk[bb[par_]],))
                return bb

            def mm_t(l_ap, r_ap, rd):
                b_ = nb()
                for h in range(8):
                    hs = slice(h * 64, (h + 1) * 64)
                    for c2 in (0, 1):
                        ps = slice(c2 * 64, (c2 + 1) * 64)
                        O.mm(C.banks[b_][ps, hs], l_ap[ps, hs], r_ap[ps, hs], True, True, rd, (C.bk[b_],))
                return b_

            for (dst, lnm, rnm, msk_) in (("Q0", "BtT", "AtT", Msu), ("P0", "AtT", "BtT", Msl), ("Nak", "KtT", "AtT", Msu),
                                          ("Nrb", "BtT", "RtT", Miu), ("Nrk", "KtT", "RtT", Miu)):
                bb = mm_d(lnm, rnm)
                mflat = msk_[:].rearrange("p g d -> p (g d)")
                for par_ in range(2):
                    O.tt("dve", pv(Wp(dst, par)[:], par_), pv(C.banks[bb[par_]][:, :], par_), pv(mflat, par_), ALU.mult, (C.bk[bb[par_]], msk),
                         (xam if dst == "Nrb" else am,))
                yield
            for (dst, nm) in (("AKV", "Nak"), ("RKV", "Nrk")):
                b_ = mm_t(W[nm], Wp("Vt", par), (am, xwk))
                O.act(Wp(dst, par)[:], C.banks[b_][:, :], AF.Copy, (C.bk[b_],), (xam,))
                yield
            X = Wp("X", par)
            O.tt("dve", X[:], Miu[:].rearrange("p g d -> p (g d)"), Msu[:].rearrange("p g d -> p (g d)"), ALU.subtract, (msk,), (xam,))
            O.tt("dve", X[:], X[:], W["Q0"][:], ALU.add, (am, xam), (xam,))
            Pc, Qc, Pn, Qn = "P0", "Q0", "P1", "Q1"
            for it in range(5):
                bP = mm_t(W[Qc], W[Pc], (am,))
                if it < 4:
                    bQ = mm_t(W[Pc], W[Qc], (am,))
                O.act(W[Pn][:], C.banks[bP][:, :], AF.Copy, (C.bk[bP],), (am,))
                if it < 4:
                    P.add("dve", (lambda Qn=Qn, bQ=bQ: lambda e: e.tensor_copy(out=W[Qn][:], in_=C.banks[bQ][:, :]))(), (C.bk[bQ],), (am,))
                yield
                bX = mm_t(W[Pn], X, (am, xam))
                O.tt("dve", X[:], C.banks[bX][:, :], X[:], ALU.add, (C.bk[bX], xam), (xam,))
                Pc, Qc, Pn, Qn = Pn, Qn, Pc, Qc
                yield

        def stageB(i):
            par = i % 2
            xwk, xtT, xam = tk("x_wk%d" % par), tk("x_tT%d" % par), tk("x_am%d" % par)
            rks = rks2 if par else sm["rks"]
            ch, Ht, yt, ot = tk("b_ch"), tk("H"), tk("b_Y"), tk("b_ot")
            X, Nrb, AKV, RKV = Wp("X", par), Wp("Nrb", par), Wp("AKV", par), Wp("RKV", par)
            Bh, Kh, Vt, g_ = Wp("Bh", par), Wp("Kh", par), Wp("Vt", par), Wp("g", par)
            for c2 in range(2):
                ps = slice(c2 * 64, (c2 + 1) * 64)
                bW, bY1 = [nb(), nb()], [nb(), nb()]
                for (bb, nm) in ((bW, "AtT"), (bY1, "RtT")):
                    for h in range(8):
                        par_ = h % 2
                        pr = slice((h % 2) * 64, (h % 2) * 64 + 64)
                        O.mm(C.banks[bb[par_]][ps, h * 64:(h + 1) * 64], XTp(nm, h, c2, par), H[pr, (h // 2) * 64:(h // 2) * 64 + 64], True, True,
                             (xtT, Ht), (C.bk[bb[par_]],))
                for par_ in range(2):
                    O.tt("dve", pv(W["W0"][ps, :], par_), pv(C.banks[bW[par_]][ps, :], par_), pv(AKV[ps, :], par_), ALU.add,
                         (C.bk[bW[par_]], xam), (ch,))
                for par_ in range(2):
                    O.tt("dve", pv(W["Y"][ps, :], par_), pv(C.banks[bY1[par_]][ps, :], par_), pv(RKV[ps, :], par_), ALU.add,
                         (C.bk[bY1[par_]], xam), (yt,))
                yield
                bU = nb()
                for h in range(8):
                    hs = slice(h * 64, (h + 1) * 64)
                    O.mm(C.banks[bU][ps, hs], X[ps, hs], W["W0"][ps, hs], True, True, (xam, ch), (C.bk[bU],))
                O.act(W["U"][ps, :], C.banks[bU][ps, :], AF.Copy, (C.bk[bU],), (ch,))
                yield
                bY2, bH = nb(), nb()
                for h in range(8):
                    hs = slice(h * 64, (h + 1) * 64)
                    O.mm(C.banks[bY2][ps, hs], Nrb[ps, hs], W["U"][ps, hs], True, True, (xam, ch), (C.bk[bY2],))
                for h in range(8):
                    hs = slice(h * 64, (h + 1) * 64)
                    pr = slice((h % 2) * 64, (h % 2) * 64 + 64)
                    ho = C.banks[bH][pr, (h // 2) * 64:(h // 2) * 64 + 64]
                    O.mm(ho, Bh[ps, hs], W["U"][ps, hs], True, False, (xwk, ch), (C.bk[bH],))
                    O.mm(ho, Kh[ps, hs], Vt[ps, hs], False, True, (xwk,), (C.bk[bH],))
                O.tt("dve", W["Y"][ps, :], C.banks[bY2][ps, :], W["Y"][ps, :], ALU.add, (C.bk[bY2], yt), (yt,))
                GtT = Wp("GtT", par)
                for pair in range(4):
                    cs = slice(pair * 64, (pair + 1) * 64)
                    O.stt(H[:, cs], H[:, cs], GtT[:, pair * 128 + c2 * 64:pair * 128 + c2 * 64 + 1], C.banks[bH][:, cs], ALU.mult, ALU.add,
                          (Ht, xtT, C.bk[bH]), (Ht,))
                yield
            tmp = W["W0"]
            O.red(sm["m1"][:], v8(W["Y"][:]), (yt,), (ot,))
            O.ts("dve", sm["m1"][:], sm["m1"][:], -1.0 / 64, ALU.mult, (ot,), (ot,))
            O.tt("dve", v8(W["yc"][:]), v8(W["Y"][:]), b8(sm["m1"]), ALU.add, (yt, ot), (ot,))
            O.act(tmp[:], W["yc"][:], AF.Square, (ot, ch), (ch,))
            O.red(sm["m2"][:], v8(tmp[:]), (ch,), (ot,))
            O.act(sm["m3"][:], sm["m2"][:], AF.Sqrt, (ot, tcn), (ot,), scale=1.0 / 64, bias=gneps_ap)
            O.rcp(sm["m2"][:], sm["m3"][:], (ot,), (ot,))
            yield
            O.tt("dve", v8(W["yc"][:]), v8(W["yc"][:]), b8(sm["m2"]), ALU.mult, (ot,), (ot,))
            O.tt("dve", W["yc"][:], W["yc"][:], bc["lnw"][:], ALU.mult, (ot, bct), (ot,))
            O.tt("dve", W["yc"][:], W["yc"][:], bc["lnb"][:], ALU.add, (ot, bct), (ot,))
            O.tt("dve", v8(tmp[:]), v8(Vt[:]), rks[:, 0:8].unsqueeze(2).broadcast_to([128, 8, 64]), ALU.mult, (xwk, ch), (ch,))
            O.tt("dve", W["yc"][:], W["yc"][:], tmp[:], ALU.add, (ot, ch), (ot,))
            O.tt("dve", outb[:], W["yc"][:], g_[:], ALU.mult, (ot, xwk), (tk("outb"),))
            O.dma(yr[i * 128:(i + 1) * 128, :], outb[:], (tk("outb"),), (yr_tk,))
            yield

        ntl = C.cfg.get("rw_tiles", NT)
        ratio = C.cfg.get("rw_ratio", 2)
        for i in range(ntl + 1):
            gA = stageA(i) if i < ntl else None
            gB = stageB(i - 1) if i >= 1 else None
            while gA is not None or gB is not None:
                if gB is not None:
                    try:
                        next(gB)
                    except StopIteration:
                        gB = None
                for _ in range(ratio):
                    if gA is not None:
                        try:
                            next(gA)
                        except StopIteration:
                            gA = None
        P.barrier()


def xattn_phase(C, l, src):
    nc, P = C.nc, C.P
    O = Ops(C)
    with ExitStack() as es:
        A = lambda name, shape, dt: es.enter_context(C.sb("xa_" + name, shape, dt))
        wq = A("wq", [128, 8, 512], BF16)
        wkv = A("wkv", [128, 8, 1024], BF16)
        wo = A("wo", [128, 4, D], BF16)
        stg = [A("stg%d" % i, [128, 1024], F32) for i in range(2)]
        gbc = A("gbc", [128, D], F32)
        gmb = A("gmb", [128, D], F32)
        gq = A("gq", [128, 512], F32)
        gk = A("gk", [128, 512], F32)
        xt = [A("xt%d" % i, [128, D], F32) for i in range(2)]
        hT = A("hT", [128, 8, 128], BF16)
        kT = A("kT", [128, 4, 256], BF16)
        vaug = A("vaug", [128, 2, 4, 129], BF16)
        sq = A("sq", [128, 512], F32)
        qn = A("qn", [128, 512], F32)
        qb = A("qb", [128, 512], BF16)
        qT = A("qT", [128, 4, 128], BF16)
        pt = [A("pt%d" % i, [128, 512], BF16) for i in range(2)]
        ob = A("ob", [128, 512], BF16)
        oT = A("oT", [128, 4, 128], BF16)
        st = {k: A("s_" + k, [128, 4], F32) for k in ("ss", "sd", "rs", "rec")}
        scr = make_norm_scratch(C, es, "xa")
        T = {}

        def tk(n):
            if n not in T:
                T[n] = Tk(n)
            return T[n]
        tcn = C.t_const
        stg_tk = [Tk(), Tk()]
        xt_tk = [Tk(), Tk()]
        eps_ap = cconst(C, EPS)[:, 0:1]
        zero_ap = cconst(C, 0.0)[:, 0:1]
        load_cast(C, wq, tk("wq"), C.w["xattn_w_q"][l], D, 512, stg, stg_tk, piece=512)
        load_cast(C, wkv, tk("wkv"), C.w["xattn_w_kv"][l], D, 1024, stg, stg_tk, piece=1024)
        load_cast(C, wo, tk("wo"), C.w["xattn_w_out"][l], 512, D, stg, stg_tk, piece=1024)
        bcast_load(C, gbc[:], tk("g"), C.w["xattn_norm"][l:l + 1, :], D)
        bcast_load(C, gmb[:], tk("g"), C.w["xattn_mem_norm"][l:l + 1, :], D)
        O.dma(gq[:].rearrange("p (g d) -> p g d", d=128), C.w["xattn_q_gain"][l:l + 1, :].partition_broadcast(128).broadcast_to([128, 4, 128]), (), (tk("g"),))
        O.dma(gk[:].rearrange("p (g d) -> p g d", d=128), C.w["xattn_k_gain"][l:l + 1, :].partition_broadcast(128).broadcast_to([128, 4, 128]), (), (tk("g"),))
        O.ts("dve", gq[:], gq[:], float(128 ** -0.5), ALU.mult, (tk("g"),), (tk("g"),))
        O.memset("pool", vaug[:, :, :, 128:129], 1.0, (tk("vaug"),))
        v4 = lambda ap: ap.rearrange("p (g d) -> p g d", d=128)
        b4 = lambda t: t[:, 0:4].unsqueeze(2).broadcast_to([128, 4, 128])

        def gnorm(src_psum, src_tk, gains, dst):
            O.act(sq[:], src_psum, AF.Square, (src_tk,), (tk("sq"),))
            O.red(st["ss"][:], v4(sq[:]), (tk("sq"),), (tk("st"),))
            O.act(st["sd"][:], st["ss"][:], AF.Sqrt, (tk("st"), tcn), (tk("st"),), scale=1.0 / 128, bias=eps_ap)
            O.rcp(st["rs"][:], st["sd"][:], (tk("st"),), (tk("st"),))
            O.tt("dve", v4(qn[:]), v4(src_psum), b4(st["rs"]), ALU.mult, (src_tk, tk("st")), (tk("qn"),))
            O.tt("dve", dst, qn[:], gains[:], ALU.mult, (tk("qn"), tk("g")), (tk("qb"),))

        for mt in range(2):
            O.dma(xt[mt][:], C.mem[mt * 128:(mt + 1) * 128, :], (), (xt_tk[mt],))
            norm_transpose(C, xt[mt][:], xt_tk[mt], gmb[:], tk("g"), hT, tk("hT"), 0, scr, 0)
            for half in range(2):
                for c in range(8):
                    O.mm(C.banks[1 + half][:, :], hT[:, c, :], wkv[:, c, half * 512:(half + 1) * 512], c == 0, c == 7, (tk("hT"), tk("wkv")), (C.bk[1 + half],))
            gnorm(C.banks[1][:, :], C.bk[1], gk, qb[:])
            O.act(vaug[:, mt, :, 0:128], v4(C.banks[2][:, :]), AF.Copy, (C.bk[2],), (tk("vaug"),))
            pb = C.banks[3].bitcast(BF16)
            for h in range(4):
                O.tr(pb[:, h * 128:(h + 1) * 128], qb[:, h * 128:(h + 1) * 128], C.ident[:], (tk("qb"), tcn), (C.bk[3],))
            O.act(kT[:, :, mt * 128:(mt + 1) * 128], pb[:, 0:512].rearrange("p (h t) -> p h t", h=4), AF.Copy, (C.bk[3],), (tk("kT"),))
        P.barrier()
        dup = {}
        for par in range(2):
            d_ = {}
            d_["hT"] = A("hT_%d" % par, [128, 8, 128], BF16)
            d_["sq"] = A("sq_%d" % par, [128, 512], F32)
            d_["qn"] = A("qn_%d" % par, [128, 512], F32)
            d_["qb"] = A("qb_%d" % par, [128, 512], BF16)
            d_["qT"] = A("qT_%d" % par, [128, 4, 128], BF16)
            d_["pt"] = [A("pt%d_%d" % (k_, par), [128, 512], BF16) for k_ in range(2)]
            d_["ob"] = A("ob_%d" % par, [128, 512], BF16)
            d_["oT"] = A("oT_%d" % par, [128, 4, 128], BF16)
            d_["st"] = {k_: A("s%s_%d" % (k_, par), [128, 4], F32) for k_ in ("ss", "sd", "rs", "rec")}
            d_["scr"] = make_norm_scratch(C, es, "xa%d" % par)
            dup[par] = d_

        def tile_gen(i):
            par = i % 2
            d_ = dup[par]
            B = 4 * par
            hT_, sq_, qn_, qb_, qT_, pt_, ob_, oT_, st_ = d_["hT"], d_["sq"], d_["qn"], d_["qb"], d_["qT"], d_["pt"], d_["ob"], d_["oT"], d_["st"]
            t = lambda n: tk("%s_p%d" % (n, par))
            j = par
            O.dma(xt[j][:], src[i * 128:(i + 1) * 128, :], (C.xtk[i],), (xt_tk[j],))
            norm_transpose(C, xt[j][:], xt_tk[j], gbc[:], tk("g"), hT_, t("hT"), 0, d_["scr"], B + 0)
            yield
            for c in range(8):
                O.mm(C.banks[B + 1][:, :], hT_[:, c, :], wq[:, c, :], c == 0, c == 7, (t("hT"), tk("wq")), (C.bk[B + 1],))
            yield
            O.act(sq_[:], C.banks[B + 1][:, :], AF.Square, (C.bk[B + 1],), (t("sq"),))
            O.red(st_["ss"][:], v4(sq_[:]), (t("sq"),), (t("st"),))
            yield
            O.act(st_["sd"][:], st_["ss"][:], AF.Sqrt, (t("st"), tcn), (t("st"),), scale=1.0 / 128, bias=eps_ap)
            O.rcp(st_["rs"][:], st_["sd"][:], (t("st"),), (t("st"),))
            yield
            O.tt("dve", v4(qn_[:]), v4(C.banks[B + 1][:, :]), b4(st_["rs"]), ALU.mult, (C.bk[B + 1], t("st")), (t("qn"),))
            O.tt("dve", qb_[:], qn_[:], gq[:], ALU.mult, (t("qn"), tk("g")), (t("qb"),))
            yield
            pb = C.banks[B + 2].bitcast(BF16)
            for h in range(4):
                O.tr(pb[:, h * 128:(h + 1) * 128], qb_[:, h * 128:(h + 1) * 128], C.ident[:], (t("qb"), tcn), (C.bk[B + 2],))
            O.act(qT_[:], pb[:, 0:512].rearrange("p (h t) -> p h t", h=4), AF.Copy, (C.bk[B + 2],), (t("qT"),))
            yield
            for mt in range(2):
                bs = B + 1 + mt
                for h in range(4):
                    O.mm(C.banks[bs][:, h * 128:(h + 1) * 128], kT[:, h, mt * 128:(mt + 1) * 128], qT_[:, h, :], True, True, (tk("kT"), t("qT")), (C.bk[bs],))
                P.add("act", (lambda mt=mt, bs=bs: lambda e: e.activation(out=pt_[mt][:], in_=C.banks[bs][:, :], func=AF.Exp, bias=zero_ap))(),
                      (C.bk[bs], tcn), (t("pt%d" % mt),))
                yield
            oslot = lambda h: (B + 3, h * 129) if h < 3 else (B + 0, 0)
            for h in range(4):
                bank, ocol = oslot(h)
                for mt in range(2):
                    O.mm(C.banks[bank][:, ocol:ocol + 129], pt_[mt][:, h * 128:(h + 1) * 128], vaug[:, mt, h, :], mt == 0, mt == 1,
                         (t("pt%d" % mt), tk("vaug")), (C.bk[bank],))
            yield
            for h in range(4):
                bank, ocol = oslot(h)
                O.rcp(st_["rec"][:, h:h + 1], C.banks[bank][:, ocol + 128:ocol + 129], (C.bk[bank],), (t("rec"),))
                O.ts("dve", ob_[:, h * 128:(h + 1) * 128], C.banks[bank][:, ocol:ocol + 128], st_["rec"][:, h:h + 1], ALU.mult, (C.bk[bank], t("rec")), (t("ob"),))
                if h % 2 == 1:
                    yield
            pb = C.banks[B + 1].bitcast(BF16)
            for h in range(4):
                O.tr(pb[:, h * 128:(h + 1) * 128], ob_[:, h * 128:(h + 1) * 128], C.ident[:], (t("ob"), tcn), (C.bk[B + 1],))
            O.act(oT_[:], pb[:, 0:512].rearrange("p (h t) -> p h t", h=4), AF.Copy, (C.bk[B + 1],), (t("oT"),))
            yield
            for half in range(2):
                bank = B + 2 + half
                for c in range(4):
                    O.mm(C.banks[bank][:, :], oT_[:, c, :], wo[:, c, half * 512:(half + 1) * 512], c == 0, c == 3, (t("oT"), tk("wo")), (C.bk[bank],))
                O.tt("dve", xt[j][:, half * 512:(half + 1) * 512], C.banks[bank][:, :], xt[j][:, half * 512:(half + 1) * 512], ALU.add,
                     (C.bk[bank], xt_tk[j]), (xt_tk[j],))
                yield
            O.dma(C.out[i * 128:(i + 1) * 128, :], xt[j][:], (xt_tk[j],), (C.xtk[i],))
            yield

        for m in range(NT // 2):
            gens = [tile_gen(2 * m), tile_gen(2 * m + 1)]
            while gens:
                for g_ in list(gens):
                    try:
                        next(g_)
                    except StopIteration:
                        gens.remove(g_)
        P.barrier()


_NC_CACHE = {}

LAUNCHES = (
    ("ffn1", {"plan": (("ffn1", 0),), "LW": 1}),
    ("rwkv", {"plan": (("rwkv", 0),), "LW": 1, "yr_kind": "ExternalOutput"}),
    ("attn", {"plan": (("attn", 0),), "LW": 1, "yr_kind": "ExternalInput"}),
    ("xattn", {"plan": (("xattn", 0),), "LW": 1}),
    ("ffn2", {"plan": (("ffn2", 0),), "LW": 1}),
)


def get_nc(name, cfg):
    if name not in _NC_CACHE:
        nc = bass.Bass("TRN2", target_bir_lowering=False)
        build(nc, dict(cfg))
        _NC_CACHE[name] = nc
    return _NC_CACHE[name]


def make_in_maps(inputs, cores, l=None, extra=None):
    maps = []
    for b in cores:
        m = {}
        for k, v in inputs.items():
            v = np.asarray(v)
            if k in ("x", "mem"):
                m[k] = np.ascontiguousarray(v[b])
            elif k == "positions":
                m[k] = np.ascontiguousarray(v[b:b + 1]).astype(np.int32)
            else:
                if k == "rwkv_r_k":
                    v = v.reshape(L, 512)
                if l is not None:
                    v = v[l:l + 1]
                m[k] = np.ascontiguousarray(v)
        if extra:
            for k, v in extra.items():
                m[k] = np.ascontiguousarray(v[b])
        maps.append(m)
    return maps


FUSED = True


def kernel(**inputs):
    cores = list(range(8))
    inp = dict(inputs)
    if FUSED:
        nc = get_nc("fused", {})
        res = run_bass_kernel_spmd(nc, make_in_maps(inp, cores), core_ids=cores)
        return np.stack([np.asarray(r["out"]) for r in res.results], axis=0).astype(np.float32)
    x = np.asarray(inp["x"])
    for l in range(L):
        yr = None
        for name, cfg in LAUNCHES:
            nc = get_nc(name, cfg)
            inp["x"] = x
            extra = {"yr": yr} if name == "attn" else None
            res = run_bass_kernel_spmd(nc, make_in_maps(inp, cores, l=l, extra=extra), core_ids=cores)
            if name == "rwkv":
                yr = np.stack([np.asarray(r["yr"]) for r in res.results], axis=0)
            else:
                x = np.stack([np.asarray(r["out"]) for r in res.results], axis=0).astype(np.float32)
    return x
```

```python
import numpy as np
from contextlib import ExitStack
import concourse.bass as bass
import concourse.mybir as mybir
from concourse.bass_utils import run_bass_kernel_spmd

F32 = mybir.dt.float32
BF16 = mybir.dt.bfloat16
I32 = mybir.dt.int32
AF = mybir.ActivationFunctionType
ALU = mybir.AluOpType
AX = mybir.AxisListType

D = 1024
S = 4096
NT = S // 128
L = 2
FFN = 2816
NF = FFN // 128
HD = 64
MEM = 256
EPS = 1e-6
IN_W = 3364
FOX_IN = 772
MOBA_IN = 768
RW0 = FOX_IN + MOBA_IN
NEG = -30000.0


class Tk:
    __slots__ = ("name", "lw", "rd", "excl")

    def __init__(self, name="", excl=False):
        self.name = name
        self.lw = None
        self.rd = []
        self.excl = excl


class Ins:
    __slots__ = ("eng", "fn", "deps", "idx", "mark", "cnt", "dma", "dsem", "dval", "waits")


class Prog:
    ENGS = ("pe", "dve", "act", "pool", "sp")
    NDMA = 6

    def __init__(self, nc):
        self.nc = nc
        self.ins = {e: [] for e in self.ENGS}
        self.order = []
        self.dma_slot_last = {}
        self.dma_cnt = {e: 0 for e in self.ENGS}
        self.dma_uses = {}
        self.pending_barrier = {e: None for e in self.ENGS}
        self.pe_mode = None

    def add(self, eng, fn, reads=(), writes=(), dma=False, mode=None):
        if eng == "pe":
            if mode is None:
                mode = (128, 128)
            mode = tuple(32 if v <= 32 else (64 if v <= 64 else 128) for v in mode)
            if self.pe_mode is not None and mode != self.pe_mode:
                self.pe_mode = mode
                self.add("pe", lambda e: e.drain(), (), (), mode=mode)
            self.pe_mode = mode
        I = Ins()
        I.eng = eng
        I.fn = fn
        I.dma = dma
        I.mark = False
        I.cnt = 0
        I.waits = []
        deps = []
        if any(t.excl for t in reads):
            writes = tuple(writes) + tuple(t for t in reads if t.excl and t not in writes)
            reads = tuple(t for t in reads if not t.excl)
        for t in reads:
            if t.lw is not None:
                deps.append(t.lw)
        for t in writes:
            if t.lw is not None:
                deps.append(t.lw)
            deps.extend(t.rd)
        if self.pending_barrier[eng] is not None:
            deps.extend(self.pending_barrier[eng])
            self.pending_barrier[eng] = None
        if dma:
            k = self.dma_cnt[eng] % self.NDMA
            self.dma_cnt[eng] += 1
            key = (eng, k)
            prev = self.dma_slot_last.get(key)
            if prev is not None:
                deps.append(prev)
            self.dma_slot_last[key] = I
            self.dma_uses[key] = self.dma_uses.get(key, 0) + 1
            I.dsem = key
            I.dval = 16 * self.dma_uses[key]
        I.deps = deps
        I.idx = len(self.ins[eng])
        self.ins[eng].append(I)
        self.order.append(I)
        for t in reads:
            t.rd.append(I)
        for t in writes:
            t.lw = I
            t.rd = []
        return I

    def barrier(self):
        last = []
        for e in self.ENGS:
            if self.ins[e]:
                last.append(self.ins[e][-1])
        for key, I in self.dma_slot_last.items():
            last.append(I)
        for e in self.ENGS:
            self.pending_barrier[e] = list(last)

    def finish(self, es):
        nc = self.nc
        self.barrier()
        self.add("sp", lambda e: e.nop(), ())
        waited = {e: {} for e in self.ENGS}
        for I in self.order:
            E = I.eng
            w = waited[E]
            for d in I.deps:
                if d.dma:
                    key = ("dma",) + d.dsem
                    if w.get(key, 0) >= d.dval:
                        continue
                    w[key] = d.dval
                    I.waits.append(d)
                else:
                    if d.eng == E and E in ("pe", "sp"):
                        continue
                    key = d.eng
                    if w.get(key, -1) >= d.idx:
                        continue
                    w[key] = d.idx
                    d.mark = True
                    I.waits.append(d)
        sems = {}
        for e in self.ENGS:
            sems[e] = es.enter_context(nc.semaphore("sem_" + e))
            c = 0
            for I in self.ins[e]:
                if I.mark and not I.dma:
                    c += 1
                    I.cnt = c
        dsems = {}
        for key in self.dma_uses:
            dsems[key] = es.enter_context(nc.semaphore("dsem_%s_%d" % key))

        def emit(eng_name, e):
            for I in self.ins[eng_name]:
                for d in I.waits:
                    if d.dma:
                        e.wait_ge(dsems[d.dsem], d.dval)
                    else:
                        e.wait_ge(sems[d.eng], d.cnt)
                r = I.fn(e)
                if I.dma:
                    r.then_inc(dsems[I.dsem], 16)
                elif I.mark:
                    r.then_inc(sems[eng_name], 1)

        with nc.Block() as block:
            @block.sync
            def _(e):
                emit("sp", e)

            @block.tensor
            def _(e):
                emit("pe", e)

            @block.vector
            def _(e):
                emit("dve", e)

            @block.scalar
            def _(e):
                emit("act", e)

            @block.gpsimd
            def _(e):
                emit("pool", e)


class Ctx:
    pass


def build(nc, cfg):
    P = Prog(nc)
    C = Ctx()
    C.nc = nc
    C.P = P
    C.cfg = cfg
    es_top = ExitStack()
    C.es = es_top
    C.uid = 0

    def sb(name, shape, dt):
        C.uid += 1
        return nc.sbuf_tensor("%s_%d" % (name, C.uid), shape, dt)
    C.sb = sb

    def din(name, shape, dt=F32):
        return nc.dram_tensor(name, list(shape), dt, kind="ExternalInput").ap()

    C.x_in = din("x", [S, D])
    C.mem = din("mem", [MEM, D])
    C.pos = din("positions", [1, S], I32)
    names = {
        "ffn1_norm": [L, D], "ffn1_w_in": [L, D, 2 * FFN], "ffn1_w_out": [L, FFN, D],
        "mix_norm": [L, D], "mix_w_in": [L, D, IN_W], "mix_w_out": [L, D, D],
        "fox_f_bias": [L, 4], "fox_q_gain": [L, HD], "fox_k_gain": [L, HD],
        "moba_q_gain": [L, HD], "moba_k_gain": [L, HD],
        "rwkv_mu": [L, 1824], "rwkv_w0": [L, 512], "rwkv_w2": [L, 64, 512], "rwkv_a0": [L, 512],
        "rwkv_a2": [L, 64, 512], "rwkv_g2": [L, 160, 512], "rwkv_k_k": [L, 512], "rwkv_k_a": [L, 512],
        "rwkv_r_k": [L, 512], "rwkv_ln_w": [L, 512], "rwkv_ln_b": [L, 512],
        "xattn_norm": [L, D], "xattn_mem_norm": [L, D], "xattn_w_q": [L, D, 512], "xattn_w_kv": [L, D, 1024],
        "xattn_q_gain": [L, 128], "xattn_k_gain": [L, 128], "xattn_w_out": [L, 512, D],
        "ffn2_norm": [L, D], "ffn2_w_in": [L, D, 2 * FFN], "ffn2_w_out": [L, FFN, D],
    }
    LW = cfg.get("LW", L)
    C.w = {k: din(k, [LW] + list(v[1:])) for k, v in names.items()}
    C.out = nc.dram_tensor("out", [S, D], F32, kind="ExternalOutput").ap()
    C.xtk = [Tk("x%d" % i) for i in range(NT)]

    es = es_top
    C.ident = es.enter_context(C.sb("ident", [128, 128], BF16))
    C.identf = es.enter_context(C.sb("identf", [128, 128], F32))
    C.t_const = Tk("const")
    C.banks = [es.enter_context(nc.psum_tensor("bank%d" % i, [128, 512], F32)) for i in range(8)]
    C.bk = [Tk("bank%d" % i, excl=True) for i in range(8)]
    P.add("pool", lambda e: e.memset(C.identf[:], 1.0), (), (C.t_const,))
    P.add("pool", lambda e: e.affine_select(out=C.identf[:], in_=C.identf[:], pattern=[[-1, 128]],
                                            compare_op=ALU.is_equal, fill=0.0, base=0, channel_multiplier=1),
          (), (C.t_const,))
    P.add("pool", lambda e: e.tensor_copy(out=C.ident[:], in_=C.identf[:]), (), (C.t_const,))

    setup_globals(C)
    src = C.x_in
    plan = cfg.get("plan")
    if plan is None:
        plan = []
        stop = cfg.get("stop_after")
        skip = cfg.get("skip", ())
        for l in range(L):
            for ph in ("ffn1", "mix", "xattn", "ffn2"):
                if ph not in skip:
                    plan.append((ph, l))
                if stop == (ph, l):
                    break
            else:
                continue
            break
    for ph, l in plan:
        if ph in ("ffn1", "ffn2"):
            ffn_phase(C, l, ph, src)
            src = C.out
        elif ph == "mix":
            mixer_phase(C, l, src)
            src = C.out
        elif ph == "rwkv":
            rwkv_phase(C, l, src, C.yr, C.yr_tk)
        elif ph == "attn":
            mixer_phase(C, l, src, do_rwkv=False)
            src = C.out
        elif ph == "xattn":
            xattn_phase(C, l, src)
            src = C.out
    P.finish(es_top)
    return nc


def load_cast(C, dst, dst_tk, w2d, K, N, stg, stg_tk, piece=2048, engs=("dve", "act")):
    P = C.P
    nchunk = K // 128
    i = 0
    for c in range(nchunk):
        for n0 in range(0, N, piece):
            n1 = min(N, n0 + piece)
            sb = i % len(stg)
            s_ap = stg[sb][:, 0:n1 - n0]
            P.add("sp", (lambda s_ap=s_ap, c=c, n0=n0, n1=n1: lambda e: e.dma_start(
                out=s_ap, in_=w2d[c * 128:(c + 1) * 128, n0:n1]))(), (), (stg_tk[sb],), dma=True)
            eng = engs[i % len(engs)]
            if eng == "act":
                P.add(eng, (lambda s_ap=s_ap, c=c, n0=n0, n1=n1: lambda e: e.copy(
                    out=dst[:, c, n0:n1], in_=s_ap))(), (stg_tk[sb],), (dst_tk,))
            else:
                P.add(eng, (lambda s_ap=s_ap, c=c, n0=n0, n1=n1: lambda e: e.tensor_copy(
                    out=dst[:, c, n0:n1], in_=s_ap))(), (stg_tk[sb],), (dst_tk,))
            i += 1


def bcast_load(C, dst, dst_tk, row_ap, n):
    C.P.add("sp", lambda e: e.dma_start(out=dst, in_=row_ap.partition_broadcast(128)), (), (dst_tk,), dma=True)


def norm_transpose(C, xt, xt_tk, gain_bc, gain_tk, hT, hT_tk, col0, scr, bank, nchunks=8):
    P = C.P
    W = 128 * nchunks
    P.add("act", lambda e: e.activation(out=scr["junk"][:, 0:W], in_=xt, func=AF.Square, accum_out=scr["ss"][:]),
          (xt_tk,), (scr["junk_tk"], scr["ss_tk"]))
    P.add("act", lambda e: e.activation(out=scr["std"][:], in_=scr["ss"][:], func=AF.Sqrt, scale=1.0 / W, bias=scr["eps"][:]),
          (scr["ss_tk"], C.t_const), (scr["std_tk"],))
    P.add("dve", lambda e: e.reciprocal(out=scr["rstd"][:], in_=scr["std"][:]), (scr["std_tk"],), (scr["rstd_tk"],))
    P.add("dve", lambda e: e.scalar_tensor_tensor(out=scr["hb"][:, 0:W], in0=xt, scalar=scr["rstd"][:], in1=gain_bc,
                                                  op0=ALU.mult, op1=ALU.mult),
          (xt_tk, scr["rstd_tk"], gain_tk), (scr["hb_tk"],))
    pb = C.banks[bank].bitcast(BF16)
    for c in range(nchunks):
        P.add("pe", (lambda c=c: lambda e: e.transpose(out=pb[:, c * 128:(c + 1) * 128], in_=scr["hb"][:, c * 128:(c + 1) * 128],
                                                       identity=C.ident[:]))(), (scr["hb_tk"], C.t_const), (C.bk[bank],))
    P.add("act", lambda e: e.copy(out=hT[:, 0:nchunks, col0:col0 + 128],
                                  in_=pb[:, 0:W].rearrange("p (c t) -> p c t", c=nchunks)),
          (C.bk[bank],), (hT_tk,))


def make_norm_scratch(C, es, tag):
    nc = C.nc
    scr = {}
    scr["junk"] = es.enter_context(C.sb(tag + "junk", [128, D], BF16))
    scr["hb"] = es.enter_context(C.sb(tag + "hb", [128, D], BF16))
    for k in ("ss", "std", "rstd", "eps"):
        scr[k] = es.enter_context(C.sb(tag + k, [128, 1], F32))
    for k in ("junk", "hb", "ss", "std", "rstd"):
        scr[k + "_tk"] = Tk(tag + k)
    C.P.add("pool", lambda e: e.memset(scr["eps"][:], EPS), (), (C.t_const,))
    return scr


def ffn_phase(C, l, name, src):
    nc, P = C.nc, C.P
    TB = 512
    NTB = TB // 128
    NB = S // TB
    with ExitStack() as es:
        win = es.enter_context(C.sb(name + "win", [128, 8, 2 * FFN], BF16))
        wout = es.enter_context(C.sb(name + "wout", [128, NF, D], BF16))
        win_tk, wout_tk, gbc_tk, hT_tk, aT_tk = Tk("win"), Tk("wout"), Tk("gbc"), Tk("hT"), Tk("aT")
        with ExitStack() as es2:
            stg = [es2.enter_context(C.sb(name + "stg%d" % i, [128, 2816], F32)) for i in range(3)]
            stg_tk = [Tk("stg0"), Tk("stg1"), Tk("stg2")]
            load_cast(C, win, win_tk, C.w[name + "_w_in"][l], D, 2 * FFN, stg, stg_tk, piece=2816)
            load_cast(C, wout, wout_tk, C.w[name + "_w_out"][l], FFN, D, stg, stg_tk, piece=1024)
            P.barrier()
        gbc = es.enter_context(C.sb(name + "gbc", [128, D], F32))
        xt = [es.enter_context(C.sb(name + "xt%d" % i, [128, D], F32)) for i in range(NTB)]
        hT = es.enter_context(C.sb(name + "hT", [128, 8, TB], BF16))
        aT = es.enter_context(C.sb(name + "aT", [128, NF, TB], BF16))
        sg = [es.enter_context(C.sb(name + "sg%d" % i, [128, TB], F32)) for i in range(2)]
        scr = make_norm_scratch(C, es, name)
        xt_tk = [Tk("xt%d" % i) for i in range(NTB)]
        sg_tk = [Tk("sg0"), Tk("sg1")]
        bcast_load(C, gbc[:], gbc_tk, C.w[name + "_norm"][l:l + 1, :], D)
        for b in range(NB):
            for j in range(NTB):
                ti = b * NTB + j
                P.add("sp", (lambda j=j, ti=ti: lambda e: e.dma_start(out=xt[j][:], in_=src[ti * 128:(ti + 1) * 128, :]))(),
                      (C.xtk[ti],), (xt_tk[j],), dma=True)
                norm_transpose(C, xt[j][:], xt_tk[j], gbc[:], gbc_tk, hT, hT_tk, j * 128, scr, 0)
            for f in range(NF):
                bg, bu = 1 + (f % 2), 3 + (f % 2)
                for (bank, col) in ((bg, f * 128), (bu, FFN + f * 128)):
                    for c in range(8):
                        P.add("pe", (lambda bank=bank, col=col, c=c: lambda e: e.matmul(
                            C.banks[bank][:, 0:TB], lhsT=win[:, c, col:col + 128], rhs=hT[:, c, :],
                            start=(c == 0), stop=(c == 7)))(), (win_tk, hT_tk), (C.bk[bank],))
                k = f % 2
                P.add("act", (lambda bg=bg, k=k: lambda e: e.activation(out=sg[k][:], in_=C.banks[bg][:, 0:TB], func=AF.Silu))(),
                      (C.bk[bg],), (sg_tk[k],))
                P.add("dve", (lambda bu=bu, k=k, f=f: lambda e: e.tensor_tensor(out=aT[:, f, :], in0=C.banks[bu][:, 0:TB],
                                                                              in1=sg[k][:], op=ALU.mult))(),
                      (C.bk[bu], sg_tk[k]), (aT_tk,))
            for j in range(NTB):
                ti = b * NTB + j
                for half in range(2):
                    bank = 5 + half
                    for f in range(NF):
                        P.add("pe", (lambda bank=bank, f=f, j=j, half=half: lambda e: e.matmul(
                            C.banks[bank][:, :], lhsT=aT[:, f, j * 128:(j + 1) * 128], rhs=wout[:, f, half * 512:(half + 1) * 512],
                            start=(f == 0), stop=(f == NF - 1)))(), (aT_tk, wout_tk), (C.bk[bank],))
                    P.add("dve", (lambda bank=bank, j=j, half=half: lambda e: e.scalar_tensor_tensor(
                        out=xt[j][:, half * 512:(half + 1) * 512], in0=C.banks[bank][:, :], scalar=0.5,
                        in1=xt[j][:, half * 512:(half + 1) * 512], op0=ALU.mult, op1=ALU.add))(),
                        (C.bk[bank], xt_tk[j]), (xt_tk[j],))
                P.add("pool", (lambda j=j, ti=ti: lambda e: e.dma_start(out=C.out[ti * 128:(ti + 1) * 128, :], in_=xt[j][:]))(),
                      (xt_tk[j],), (C.xtk[ti],), dma=True)
        P.barrier()


TWO_PI = 6.283185307179586
C1 = 6.28125
C2 = TWO_PI - C1
MAGIC = 12582912.0
INVF = [float(np.float32(500000.0) ** (-np.float32(2 * i) / np.float32(16.0))) for i in range(8)]


def cconst(C, val):
    key = float(val)
    if key not in C.consts:
        t = C.es.enter_context(C.sb("c%d" % len(C.consts), [128, 1], F32))
        C.P.add("pool", lambda e: e.memset(t[:], key), (), (C.t_const,))
        C.consts[key] = t
    return C.consts[key]


def setup_globals(C):
    nc, P, es = C.nc, C.P, C.es
    C.consts = {}
    C.tri = es.enter_context(C.sb("tri", [128, 128], BF16))
    C.trif = es.enter_context(C.sb("trif", [128, 128], F32))
    C.onesf = es.enter_context(C.sb("onesf", [128, 128], F32))
    C.onesb = es.enter_context(C.sb("onesb", [128, 512], BF16))
    C.cos = es.enter_context(C.sb("cos", [128, NT, 8], F32))
    C.sin = es.enter_context(C.sb("sin", [128, NT, 8], F32))
    tc_ = (C.t_const,)
    P.add("pool", lambda e: e.memset(C.onesf[:], 1.0), (), tc_)
    P.add("pool", lambda e: e.memset(C.onesb[:], 1.0), (), tc_)
    P.add("pool", lambda e: e.affine_select(out=C.trif[:], in_=C.onesf[:], pattern=[[1, 128]], compare_op=ALU.is_ge,
                                            fill=0.0, base=0, channel_multiplier=-1), tc_, tc_)
    P.add("pool", lambda e: e.tensor_copy(out=C.tri[:], in_=C.trif[:]), tc_, tc_)
    for v in (EPS, 1.0, 0.0, np.pi / 2, 64e-5):
        cconst(C, v)
    C.qaf = [nc.dram_tensor("qaf%d" % h, [66, S], BF16).ap() for h in range(4)]
    C.kaf = [nc.dram_tensor("kaf%d" % h, [66, S], BF16).ap() for h in range(4)]
    C.qam = [nc.dram_tensor("qam%d" % h, [80, S], BF16).ap() for h in range(4)]
    C.kam = [nc.dram_tensor("kam%d" % h, [80, S], BF16).ap() for h in range(4)]
    yk = C.cfg.get("yr_kind")
    C.yr = (nc.dram_tensor("yr", [S, 512], BF16, kind=yk) if yk else nc.dram_tensor("yr", [S, 512], BF16)).ap()
    C.yr_tk = Tk("yr")
    C.qa_tk = {("f", h): Tk() for h in range(4)}
    C.qa_tk.update({("m", h): Tk() for h in range(4)})
    C.ka_tk = {("f", h): Tk() for h in range(4)}
    C.ka_tk.update({("m", h): Tk() for h in range(4)})
    with ExitStack() as s3:
        ohf = s3.enter_context(C.sb("oh_full", [16, S], BF16))
        onf = s3.enter_context(C.sb("ones_full", [16, S], BF16))
        tko = Tk("ohfull")
        P.add("pool", lambda e: e.memset(onf[:], 1.0), (), (tko,))
        P.add("pool", lambda e: e.affine_select(out=ohf[:], in_=onf[:], pattern=[[1, S]], compare_op=ALU.is_ge, fill=0.0,
                                                base=0, channel_multiplier=-256), (tko,), (tko,))
        P.add("pool", lambda e: e.affine_select(out=ohf[:], in_=ohf[:], pattern=[[-1, S]], compare_op=ALU.is_ge, fill=0.0,
                                                base=255, channel_multiplier=256), (tko,), (tko,))
        for h in range(4):
            P.add("sp", (lambda h=h: lambda e: e.dma_start(out=C.kam[h][64:80, :], in_=ohf[:]))(), (tko,), (C.ka_tk[("m", h)],), dma=True)
            P.add("sp", (lambda h=h: lambda e: e.dma_start(out=C.kaf[h][64:66, :], in_=onf[0:2, :]))(), (tko,), (C.ka_tk[("f", h)],), dma=True)
        P.barrier()
    with ExitStack() as s2:
        posi = s2.enter_context(C.sb("posi", [32, 128], I32))
        posf = s2.enter_context(C.sb("posf", [32, 128], F32))
        posT = s2.enter_context(C.sb("posT", [128, NT], F32))
        ang = s2.enter_context(C.sb("ang", [128, NT, 8], F32))
        t1 = s2.enter_context(C.sb("rp1", [128, NT * 8], F32))
        t2 = s2.enter_context(C.sb("rp2", [128, NT * 8], F32))
        r = s2.enter_context(C.sb("rpr", [128, NT * 8], F32))
        tk = Tk("rope")
        P.add("sp", lambda e: e.dma_start(out=posi[:], in_=C.pos.rearrange("o (j p) -> (o j) p", p=128)), (), (tk,), dma=True)
        P.add("dve", lambda e: e.tensor_copy(out=posf[:], in_=posi[:]), (tk,), (tk,))
        P.add("pe", lambda e: e.transpose(out=C.banks[0][:, 0:32], in_=posf[:], identity=C.identf[0:32, 0:32]),
              (tk, C.t_const), (C.bk[0],), mode=(32, 128))
        P.add("dve", lambda e: e.tensor_copy(out=posT[:], in_=C.banks[0][:, 0:32]), (C.bk[0],), (tk,))
        for i in range(8):
            P.add("dve", (lambda i=i: lambda e: e.tensor_scalar(out=ang[:, :, i], in0=posT[:], scalar1=INVF[i], scalar2=None,
                                                                op0=ALU.mult))(), (tk,), (tk,))
        af = ang[:].rearrange("p j i -> p (j i)")
        P.add("dve", lambda e: e.tensor_scalar(out=t1[:], in0=af, scalar1=1.0 / TWO_PI, scalar2=MAGIC, op0=ALU.mult, op1=ALU.add), (tk,), (tk,))
        P.add("dve", lambda e: e.tensor_scalar(out=t2[:], in0=t1[:], scalar1=-MAGIC, scalar2=None, op0=ALU.add), (tk,), (tk,))
        P.add("dve", lambda e: e.scalar_tensor_tensor(out=r[:], in0=t2[:], scalar=-C1, in1=af, op0=ALU.mult, op1=ALU.add), (tk,), (tk,))
        P.add("dve", lambda e: e.scalar_tensor_tensor(out=r[:], in0=t2[:], scalar=-C2, in1=r[:], op0=ALU.mult, op1=ALU.add), (tk,), (tk,))
        P.add("dve", lambda e: e.tensor_scalar(out=t1[:], in0=r[:], scalar1=float(np.pi), scalar2=-TWO_PI, op0=ALU.is_gt, op1=ALU.mult), (tk,), (tk,))
        P.add("dve", lambda e: e.tensor_tensor(out=r[:], in0=r[:], in1=t1[:], op=ALU.add), (tk,), (tk,))
        P.add("dve", lambda e: e.tensor_scalar(out=t1[:], in0=r[:], scalar1=-float(np.pi), scalar2=TWO_PI, op0=ALU.is_lt, op1=ALU.mult), (tk,), (tk,))
        P.add("dve", lambda e: e.tensor_tensor(out=r[:], in0=r[:], in1=t1[:], op=ALU.add), (tk,), (tk,))
        P.add("dve", lambda e: e.tensor_scalar(out=r[:], in0=r[:], scalar1=3.14159, scalar2=-3.14159, op0=ALU.min, op1=ALU.max), (tk,), (tk,))
        P.add("act", lambda e: e.activation(out=C.sin[:].rearrange("p j i -> p (j i)"), in_=r[:], func=AF.Sin), (tk,), tc_)
        P.add("act", lambda e: e.activation(out=t1[:], in_=r[:], func=AF.Abs), (tk,), (tk,))
        P.add("act", lambda e: e.activation(out=C.cos[:].rearrange("p j i -> p (j i)"), in_=t1[:], func=AF.Sin, scale=-1.0,
                                            bias=cconst(C, np.pi / 2)[:]), (tk, C.t_const), tc_)
        P.barrier()


def qk_norm(C, src_psum, ngrp, gains, sq, ssq, std, rstd, dst, tk, eps_ap):
    P = C.P
    W = ngrp * 64
    src_tk, sq_tk, st_tk, dst_tk, g_tk = tk
    P.add("act", lambda e: e.activation(out=sq[:, 0:W], in_=src_psum, func=AF.Square), (src_tk,), (sq_tk,))
    P.add("dve", lambda e: e.tensor_reduce(out=ssq[:, 0:ngrp], in_=sq[:, 0:W].rearrange("p (g d) -> p g d", d=64), axis=AX.X, op=ALU.add),
          (sq_tk,), (st_tk,))
    P.add("act", lambda e: e.activation(out=std[:, 0:ngrp], in_=ssq[:, 0:ngrp], func=AF.Sqrt, scale=1.0 / 64, bias=eps_ap),
          (st_tk, C.t_const), (st_tk,))
    P.add("dve", lambda e: e.reciprocal(out=rstd[:, 0:ngrp], in_=std[:, 0:ngrp]), (st_tk,), (st_tk,))
    P.add("dve", lambda e: e.tensor_tensor(out=dst[:, 0:W].rearrange("p (g d) -> p g d", d=64),
                                           in0=src_psum.rearrange("p (g d) -> p g d", d=64),
                                           in1=rstd[:, 0:ngrp].unsqueeze(2).broadcast_to([128, ngrp, 64]), op=ALU.mult),
          (src_tk, st_tk), (dst_tk,))
    P.add("dve", lambda e: e.tensor_tensor(out=dst[:, 0:W], in0=dst[:, 0:W], in1=gains[:, 0:W], op=ALU.mult), (dst_tk, g_tk), (dst_tk,))


def mixer_phase(C, l, src, do_rwkv=True):
    nc, P = C.nc, C.P
    if do_rwkv and "rwkv" not in C.cfg.get("skip", ()):
        rwkv_phase(C, l, src, C.yr, C.yr_tk)
    with ExitStack() as es:
        ymix = es.enter_context(C.sb("ymix", [128, NT, D], BF16))
        ymix_tk = [Tk("ymix%d" % i) for i in range(NT)]
        if "attn" not in C.cfg.get("skip", ()):
            with ExitStack() as es2:
                vaug = {k: es2.enter_context(C.sb("vaug" + k, [128, NT, 4, 65], BF16)) for k in ("f", "m")}
                vaug_tk = {k: Tk("vaug" + k) for k in ("f", "m")}
                cneg = es2.enter_context(C.sb("cneg", [128, NT, 4], F32))
                cneg_tk = Tk("cneg")
                mixer_prep_attn(C, l, src, vaug, vaug_tk, cneg, cneg_tk)
                P.barrier()
                attn_heads(C, l, vaug, vaug_tk, cneg, cneg_tk, ymix, ymix_tk)
                P.barrier()
        else:
            P.add("pool", lambda e: e.memset(ymix[:, :, 0:256], 0.0), (), tuple(ymix_tk))
            P.add("pool", lambda e: e.memset(ymix[:, :, 768:1024], 0.0), (), tuple(ymix_tk))
        if C.cfg.get("no_outproj"):
            return
        if "rwkv" not in C.cfg.get("skip", ()):
            for i in range(NT):
                P.add("sp", (lambda i=i: lambda e: e.dma_start(out=ymix[:, i, 256:768], in_=C.yr[i * 128:(i + 1) * 128, :]))(),
                      (C.yr_tk,), (ymix_tk[i],), dma=True)
        else:
            P.add("pool", lambda e: e.memset(ymix[:, :, 256:768], 0.0), (), tuple(ymix_tk))
        P.barrier()
        outproj_phase(C, l, src, ymix, ymix_tk)
        P.barrier()


def mixer_prep_attn(C, l, src, vaug, vaug_tk, cneg, cneg_tk):
    nc, P = C.nc, C.P
    C.qk_wr = {(w_, k_, h_): [] for w_ in ("q", "k") for k_ in ("f", "m") for h_ in range(4)}
    NW = RW0
    with ExitStack() as es:
        A = lambda name, shape, dt: es.enter_context(C.sb("mp_" + name, shape, dt))
        win = A("win", [128, 8, NW], BF16)
        stg = [A("stg%d" % i, [128, NW], F32) for i in range(2)]
        gbc = A("gbc", [128, D], F32)
        xt = [A("xt%d" % i, [128, D], F32) for i in range(2)]
        gains = {k: A("g" + k, [128, 512], F32) for k in ("f", "m")}
        fbias = A("fbias", [128, 4], F32)
        stT = A("stT", [128, 8, 512], BF16)
        stM = A("stM", [64, 512], BF16)
        kmT = [A("kmT%d" % i, [128, 16], BF16) for i in range(2)]
        kms = A("kms", [128, 2, 2], F32)
        carry = A("carry", [128, 4], F32)
        cT = A("cT", [4, 512], F32)
        chi = A("chi", [4, 512], BF16)
        chf = A("chf", [4, 512], F32)
        clo = A("clo", [4, 512], BF16)
        oh = A("oh", [16, 512], BF16)
        tks = {n: Tk(n) for n in ("win", "gbc", "hT", "gf", "gm", "fbias", "sq", "st", "qknf", "qknm", "qkb", "rtmp", "stT", "stM",
                                  "km", "gt", "sel", "mbt", "f", "carry", "cT", "chi", "oh")}
        stg_tk = [Tk(), Tk()]
        xt_tk = [Tk(), Tk()]
        eps_ap = cconst(C, EPS)[:, 0:1]
        one_ap = cconst(C, 1.0)[:, 0:1]
        bcast_load(C, gbc[:], tks["gbc"], C.w["mix_norm"][l:l + 1, :], D)
        load_cast(C, win, tks["win"], C.w["mix_w_in"][l][:, 0:NW], D, NW, stg, stg_tk, piece=NW)
        for k, qn, kn in (("f", "fox_q_gain", "fox_k_gain"), ("m", "moba_q_gain", "moba_k_gain")):
            for g0, nm in ((0, qn), (4, kn)):
                P.add("sp", (lambda k=k, g0=g0, nm=nm: lambda e: e.dma_start(
                    out=gains[k][:, g0 * 64:(g0 + 4) * 64].rearrange("p (g d) -> p g d", d=64),
                    in_=C.w[nm][l:l + 1, :].partition_broadcast(128).broadcast_to([128, 4, 64])))(),
                    (), (tks["g" + k],), dma=True)
            P.add("dve", (lambda k=k: lambda e: e.tensor_scalar(out=gains[k][:, 0:256], in0=gains[k][:, 0:256], scalar1=0.125,
                                                                scalar2=None, op0=ALU.mult))(), (tks["g" + k],), (tks["g" + k],))
        P.add("sp", lambda e: e.dma_start(out=fbias[:], in_=C.w["fox_f_bias"][l:l + 1, :].partition_broadcast(128)), (), (tks["fbias"],), dma=True)
        P.add("pool", lambda e: e.memset(carry[:], 0.0), (), (tks["carry"],))
        for k in ("f", "m"):
            P.add("pool", (lambda k=k: lambda e: e.memset(vaug[k][:, :, :, 64:65], 1.0))(), (), (vaug_tk[k],))
        for i in range(2):
            P.add("pool", (lambda i=i: lambda e: e.memset(kmT[i][:], 0.0))(), (), (tks["km"],))
        P.barrier()
        O = Ops(C)
        dup = {}
        for par in range(2):
            d_ = {}
            d_["hT"] = A("hT_%d" % par, [128, 8, 128], BF16)
            d_["sq"] = A("sq_%d" % par, [128, 512], F32)
            for k_ in ("ssq", "std", "rstd"):
                d_[k_] = A("%s_%d" % (k_, par), [128, 8], F32)
            d_["qknf"] = A("qknf_%d" % par, [128, 512], F32)
            d_["qknm"] = A("qknm_%d" % par, [128, 512], F32)
            d_["qkb"] = A("qkb_%d" % par, [128, 1024], BF16)
            d_["rtmp"] = A("rtmp_%d" % par, [128, 4, 8, 8], F32)
            d_["gt"] = A("gt_%d" % par, [128, 4, 16], F32)
            d_["top8"] = A("top8_%d" % par, [128, 4, 8], F32)
            d_["selt"] = A("selt_%d" % par, [128, 4, 16], F32)
            d_["mbt"] = A("mbt_%d" % par, [128, 4, 16], BF16)
            d_["fb"] = A("fb_%d" % par, [128, 4], F32)
            d_["lf"] = A("lf_%d" % par, [128, 4], F32)
            d_["scr"] = make_norm_scratch(C, es, "mp%d" % par)
            d_["tk"] = {}
            dup[par] = d_

        def tile_gen(i):
            par = i % 2
            d_ = dup[par]
            B = 4 * par
            j = par
            g4 = i % 4
            qblk = i // 2

            def t(n):
                if n not in d_["tk"]:
                    d_["tk"][n] = Tk("%s_p%d" % (n, par))
                return d_["tk"][n]
            hT_, sq_, ssq_, std_, rstd_ = d_["hT"], d_["sq"], d_["ssq"], d_["std"], d_["rstd"]
            qkb_, rtmp_, gt_, top8_, selt_, mbt_, fb_, lf_ = d_["qkb"], d_["rtmp"], d_["gt"], d_["top8"], d_["selt"], d_["mbt"], d_["fb"], d_["lf"]
            O.dma(xt[j][:], src[i * 128:(i + 1) * 128, :], (C.xtk[i],), (xt_tk[j],))
            norm_transpose(C, xt[j][:], xt_tk[j], gbc[:], tks["gbc"], hT_, t("hT"), 0, d_["scr"], B + 0)
            yield
            for bank, c0, c1 in ((B + 1, 0, 512), (B + 2, 512, 772), (B + 3, 772, 1284), (B + 0, 1284, 1540)):
                for c in range(8):
                    O.mm(C.banks[bank][:, 0:c1 - c0], hT_[:, c, :], win[:, c, c0:c1], c == 0, c == 7, (t("hT"), tks["win"]), (C.bk[bank],))
                yield
            P.add("act", lambda e: e.copy(out=vaug["f"][:, i, :, 0:64], in_=C.banks[B + 2][:, 0:256].rearrange("p (h d) -> p h d", d=64)),
                  (C.bk[B + 2],), (vaug_tk["f"],))
            P.add("act", lambda e: e.copy(out=vaug["m"][:, i, :, 0:64], in_=C.banks[B + 0][:, 0:256].rearrange("p (h d) -> p h d", d=64)),
                  (C.bk[B + 0],), (vaug_tk["m"],))
            O.tt("dve", fb_[:], C.banks[B + 2][:, 256:260], fbias[:], ALU.add, (C.bk[B + 2], tks["fbias"]), (t("f"),))
            yield
            O.act(lf_[:], fb_[:], AF.Exp, (t("f"),), (t("f"),), scale=-1.0)
            O.act(lf_[:], lf_[:], AF.Ln, (t("f"), C.t_const), (t("f"),), bias=one_ap)
            yield
            for kk_, bnk in (("f", B + 1), ("m", B + 3)):
                qk_norm(C, C.banks[bnk][:, 0:512], 8, gains[kk_], sq_, ssq_, std_, rstd_, d_["qkn" + kk_],
                        (C.bk[bnk], t("sq"), t("st"), t("qkn" + kk_), tks["g" + kk_]), eps_ap)
                yield
            P.add("act", lambda e: e.copy(out=qkb_[:, 0:512], in_=d_["qknf"][:]), (t("qknf"),), (t("qkb"),))
            O.mm(C.banks[B + 0][:, 0:4], C.trif[:], lf_[:], True, True, (t("f"), C.t_const), (C.bk[B + 0],))
            O.mm(C.banks[B + 0][:, 4:8], C.onesf[:], lf_[:], True, True, (t("f"), C.t_const), (C.bk[B + 0],))
            O.tt("dve", cneg[:, i, :], C.banks[B + 0][:, 0:4], carry[:], ALU.add, (C.bk[B + 0], tks["carry"]), (cneg_tk,))
            O.tt("dve", carry[:], C.banks[B + 0][:, 4:8], carry[:], ALU.add, (C.bk[B + 0], tks["carry"]), (tks["carry"],))
            yield
            P.add("pe", lambda e: e.transpose(out=C.banks[B + 0][0:4, 128:256], in_=cneg[:, i, :], identity=C.identf[:]),
                  (cneg_tk, C.t_const), (C.bk[B + 0],), mode=(128, 4))
            O.act(cT[:, g4 * 128:(g4 + 1) * 128], C.banks[B + 0][0:4, 128:256], AF.Copy, (C.bk[B + 0],), (tks["cT"],), scale=-1.0)
            yield
            qv = d_["qknm"][:].rearrange("p (g d) -> p g d", d=64)
            x1, x2 = qv[:, :, 0:8], qv[:, :, 8:16]
            cb = C.cos[:, i:i + 1, :].broadcast_to([128, 8, 8])
            sb = C.sin[:, i:i + 1, :].broadcast_to([128, 8, 8])
            rt, qm = t("rtmp"), t("qknm")
            O.tt("dve", rtmp_[:, 0], x1, cb, ALU.mult, (qm, C.t_const), (rt,))
            O.tt("dve", rtmp_[:, 1], x2, sb, ALU.mult, (qm, C.t_const), (rt,))
            O.tt("dve", rtmp_[:, 2], x2, cb, ALU.mult, (qm, C.t_const), (rt,))
            O.tt("dve", rtmp_[:, 3], x1, sb, ALU.mult, (qm, C.t_const), (rt,))
            yield
            O.tt("dve", x1, rtmp_[:, 0], rtmp_[:, 1], ALU.subtract, (rt,), (qm,))
            O.tt("dve", x2, rtmp_[:, 2], rtmp_[:, 3], ALU.add, (rt,), (qm,))
            P.add("act", lambda e: e.copy(out=qkb_[:, 512:1024], in_=d_["qknm"][:]), (qm,), (t("qkb"),))
            yield
            pbq = C.banks[B + 1].bitcast(BF16)
            for blk in range(8):
                O.tr(pbq[:, blk * 128:(blk + 1) * 128], qkb_[:, blk * 128:(blk + 1) * 128], C.ident[:], (t("qkb"), C.t_const), (C.bk[B + 1],))
            P.add("act", lambda e: e.copy(out=stT[:, :, g4 * 128:(g4 + 1) * 128], in_=pbq[:, :].rearrange("p (b t) -> p b t", b=8)),
                  (C.bk[B + 1],), (tks["stT"],))
            yield
            if qblk > 0:
                for h in range(4):
                    pr = (h % 2) * 64
                    gb = B + 2 + (h % 2)
                    P.add("pe", (lambda h=h, pr=pr, gb=gb: lambda e: e.matmul(
                        C.banks[gb][:, h * 16:(h + 1) * 16], lhsT=stT[pr:pr + 64, 4 + h // 2, g4 * 128:(g4 + 1) * 128],
                        rhs=kmT[h // 2][pr:pr + 64, :], start=True, stop=True))(), (tks["stT"], tks["km"]), (C.bk[gb],), mode=(64, 128))
                for p2 in range(2):
                    P.add("dve", (lambda p2=p2: lambda e: e.tensor_copy(
                        out=gt_[:, p2:4:2, :], in_=C.banks[B + 2 + p2][:, 0:64].rearrange("p (h n) -> p h n", n=16)[:, p2:4:2, :]))(),
                        (C.bk[B + 2 + p2],), (t("gt"),))
                P.add("dve", lambda e: e.memset(gt_[:, :, qblk:16], -1e30), (), (t("gt"),))
                yield
                for h in range(4):
                    P.add("dve", (lambda h=h: lambda e: e.max(out=top8_[:, h, :], in_=gt_[:, h, :]))(), (t("gt"),), (t("sel"),))
                yield
                for h in range(4):
                    P.add("dve", (lambda h=h: lambda e: e.tensor_scalar(out=selt_[:, h, :], in0=gt_[:, h, :], scalar1=top8_[:, h, 2:3], scalar2=None,
                                                                        op0=ALU.is_ge))(), (t("gt"), t("sel")), (t("sel"),))
                yield
                O.ts("dve", mbt_[:], selt_[:], -NEG, ALU.mult, (t("sel"),), (t("mbt"),), s2=NEG, op1=ALU.add)
                if qblk < 15:
                    P.add("dve", lambda e: e.memset(mbt_[:, :, qblk + 1:16], NEG), (), (t("mbt"),))
                P.add("dve", lambda e: e.memset(mbt_[:, :, qblk:qblk + 1], 0.0), (), (t("mbt"),))
            else:
                P.add("dve", lambda e: e.memset(mbt_[:], NEG), (), (t("mbt"),))
                P.add("dve", lambda e: e.memset(mbt_[:, :, 0:1], 0.0), (), (t("mbt"),))
            yield
            pbm = C.banks[B + 3].bitcast(BF16)
            P.add("pe", lambda e: e.transpose(out=pbm[0:64, 0:128], in_=mbt_[:].rearrange("p h n -> p (h n)"), identity=C.ident[:]),
                  (t("mbt"), C.t_const), (C.bk[B + 3],), mode=(128, 64))
            P.add("act", lambda e: e.copy(out=stM[:, g4 * 128:(g4 + 1) * 128], in_=pbm[0:64, 0:128]), (C.bk[B + 3],), (tks["stM"],))
            yield

        for m in range(C.cfg.get("prep_tiles", NT) // 2):
            gens = [tile_gen(2 * m), tile_gen(2 * m + 1)]
            while gens:
                for g_ in list(gens):
                    try:
                        next(g_)
                    except StopIteration:
                        gens.remove(g_)
            i = 2 * m + 1
            g4 = i % 4
            qblk = m
            c0 = (g4 - 1) * 128
            for hp in range(2):
                P.add("dve", (lambda hp=hp, c0=c0: lambda e: e.tensor_reduce(out=kms[:, hp, 0:1], in_=stT[:, 6 + hp, c0:c0 + 256], axis=AX.X, op=ALU.add))(),
                      (tks["stT"],), (tks["km"],))
                P.add("dve", (lambda hp=hp, qblk=qblk: lambda e: e.tensor_scalar(out=kmT[hp][:, qblk:qblk + 1], in0=kms[:, hp, 0:1], scalar1=1.0 / 256,
                                                                                  scalar2=None, op0=ALU.mult))(), (tks["km"],), (tks["km"],))
            if g4 == 3:
                t0 = (i - 3) * 128
                P.add("dve", lambda e: e.tensor_copy(out=chi[:], in_=cT[:]), (tks["cT"],), (tks["chi"],))
                P.add("dve", lambda e: e.tensor_copy(out=chf[:], in_=chi[:]), (tks["chi"],), (tks["chi"],))
                P.add("dve", lambda e: e.tensor_tensor(out=clo[:], in0=cT[:], in1=chf[:], op=ALU.subtract), (tks["cT"], tks["chi"]), (tks["chi"],))
                def wtk(kind_, which, h_):
                    t_ = Tk()
                    C.qk_wr[(which, kind_, h_)].append(t_)
                    return t_
                for h in range(4):
                    P.add("pool", (lambda h=h, t0=t0: lambda e: e.dma_start(out=C.qaf[h][64:65, t0:t0 + 512], in_=chi[h:h + 1, :]))(),
                          (tks["chi"],), (wtk("f", "q", h),), dma=True)
                    P.add("pool", (lambda h=h, t0=t0: lambda e: e.dma_start(out=C.qaf[h][65:66, t0:t0 + 512], in_=clo[h:h + 1, :]))(),
                          (tks["chi"],), (wtk("f", "q", h),), dma=True)
                for h in range(4):
                    pr = (h % 2) * 64
                    for (dst, kind_, which, blk) in ((C.qaf[h], "f", "q", 0 + h // 2), (C.kaf[h], "f", "k", 2 + h // 2),
                                                     (C.qam[h], "m", "q", 4 + h // 2), (C.kam[h], "m", "k", 6 + h // 2)):
                        P.add("pool", (lambda dst=dst, blk=blk, pr=pr, t0=t0: lambda e: e.dma_start(
                            out=dst[0:64, t0:t0 + 512], in_=stT[pr:pr + 64, blk, :]))(), (tks["stT"],), (wtk(kind_, which, h),), dma=True)
                    P.add("pool", (lambda h=h, t0=t0: lambda e: e.dma_start(out=C.qam[h][64:80, t0:t0 + 512], in_=stM[h * 16:(h + 1) * 16, :]))(),
                          (tks["stM"],), (wtk("m", "q", h),), dma=True)


def attn_heads(C, l, vaug, vaug_tk, cneg, cneg_tk, ymix, ymix_tk):
    nc, P = C.nc, C.P
    with ExitStack() as es:
        A = lambda name, shape, dt: es.enter_context(C.sb("at_" + name, shape, dt))
        qa = [A("qa%d" % i, [80, S], BF16) for i in range(2)]
        ka = [A("ka%d" % i, [80, S], BF16) for i in range(2)]
        pt = [A("pt%d" % i, [128, 512], BF16) for i in range(4)]
        rec = A("rec", [128, 4], F32)
        qa_tk = [Tk(), Tk()]
        ka_tk = [Tk(), Tk()]
        pt_tk = [Tk(), Tk(), Tk(), Tk()]
        rec_tk = Tk()
        zero_ap = cconst(C, 0.0)[:, 0:1]
        hi = 0
        pti = 0
        blk = 0
        pending = []
        for kind, KA, ycol in (("f", 66, 0), ("m", 80, 768)):
            for h in range(C.cfg.get("attn_heads", 4)):
                b = hi % 2
                hi += 1
                qsrc = (C.qaf if kind == "f" else C.qam)[h]
                ksrc = (C.kaf if kind == "f" else C.kam)[h]
                P.add("sp", (lambda b=b, qsrc=qsrc, KA=KA: lambda e: e.dma_start(out=qa[b][0:KA, :], in_=qsrc[:, :]))(),
                      tuple(C.qk_wr[("q", kind, h)]) + (C.qa_tk[(kind, h)],), (qa_tk[b],), dma=True)
                P.add("sp", (lambda b=b, ksrc=ksrc, KA=KA: lambda e: e.dma_start(out=ka[b][0:KA, :], in_=ksrc[:, :]))(),
                      tuple(C.qk_wr[("k", kind, h)]) + (C.ka_tk[(kind, h)],), (ka_tk[b],), dma=True)
                for qb in range(8):
                    ob = 2 + (blk % 2)
                    blk += 1
                    first = True
                    for j in range(4 * qb + 4):
                        jj = j - 4 * qb
                        c0 = 0 if jj < 0 else jj * 128
                        sbk = (0, 1, 4)[pti % 3]
                        p = pti % 4
                        pti += 1
                        P.add("pe", (lambda sbk=sbk, b=b, j=j, qb=qb, c0=c0, KA=KA: lambda e: e.matmul(
                            C.banks[sbk][:, c0:512], lhsT=ka[b][0:KA, j * 128:(j + 1) * 128], rhs=qa[b][0:KA, qb * 512 + c0:(qb + 1) * 512],
                            start=True, stop=True))(), (ka_tk[b], qa_tk[b]), (C.bk[sbk],))
                        if kind == "f":
                            bias_ap = cneg[:, j, h:h + 1]
                            rd = (C.bk[sbk], cneg_tk)
                        else:
                            bias_ap = zero_ap
                            rd = (C.bk[sbk], C.t_const)
                        P.add("act", (lambda sbk=sbk, p=p, c0=c0, bias_ap=bias_ap: lambda e: e.activation(
                            out=pt[p][:, c0:512], in_=C.banks[sbk][:, c0:512], func=AF.Exp, bias=bias_ap))(), rd, (pt_tk[p],))
                        if jj >= 0:
                            P.add("pool", (lambda p=p, c0=c0: lambda e: e.tensor_tensor(out=pt[p][:, c0:c0 + 128], in0=pt[p][:, c0:c0 + 128],
                                                                                        in1=C.tri[:], op=ALU.mult))(), (pt_tk[p], C.t_const), (pt_tk[p],))
                        if len(pending) >= 2:
                            pending.pop(0)()

                        def pv_step(ob=ob, p=p, j=j, h=h, kind=kind, first=first, qb=qb, jj=jj, ycol=ycol, last=(j == 4 * qb + 3)):
                            fst = first
                            for ii in range(max(jj, 0), 4):
                                qt = 4 * qb + ii
                                P.add("pe", (lambda ii=ii, fst=fst, qt=qt: lambda e: e.matmul(
                                    C.banks[ob][:, ii * 128:ii * 128 + 65], lhsT=pt[p][:, ii * 128:(ii + 1) * 128], rhs=vaug[kind][:, j, h, :],
                                    start=fst, stop=(j == qt), skip_group_check=True))(), (pt_tk[p], vaug_tk[kind]), (C.bk[ob],))
                                fst = False
                            if last:
                                ov = C.banks[ob][:, :].rearrange("p (i c) -> p i c", c=128)
                                P.add("dve", lambda e: e.reciprocal(out=rec[:], in_=ov[:, :, 64]), (C.bk[ob],), (rec_tk,))
                                for ii in range(4):
                                    qt = 4 * qb + ii
                                    P.add("dve", (lambda ii=ii, qt=qt: lambda e: e.tensor_scalar(
                                        out=ymix[:, qt, ycol + h * 64:ycol + (h + 1) * 64], in0=ov[:, ii, 0:64], scalar1=rec[:, ii:ii + 1], scalar2=None,
                                        op0=ALU.mult))(), (C.bk[ob], rec_tk), (ymix_tk[qt],))
                        pending.append(pv_step)
                        first = False
        while pending:
            pending.pop(0)()


def outproj_phase(C, l, src, ymix, ymix_tk):
    nc, P = C.nc, C.P
    with ExitStack() as es:
        A = lambda name, shape, dt: es.enter_context(C.sb("op_" + name, shape, dt))
        wo = A("wo", [128, 8, D], BF16)
        stg = [A("stg%d" % i, [128, D], F32) for i in range(2)]
        xt = [A("xt%d" % i, [128, D], F32) for i in range(2)]
        yT = [A("yT%d" % i, [128, 8, 128], BF16) for i in range(2)]
        wo_tk, stg_tk, xt_tk, yT_tk = Tk(), [Tk(), Tk()], [Tk(), Tk()], [Tk(), Tk()]
        load_cast(C, wo, wo_tk, C.w["mix_w_out"][l], D, D, stg, stg_tk, piece=D)
        pend = []
        for i in range(NT):
            j = i % 2
            P.add("sp", (lambda j=j, i=i: lambda e: e.dma_start(out=xt[j][:], in_=src[i * 128:(i + 1) * 128, :]))(),
                  (C.xtk[i],), (xt_tk[j],), dma=True)
            pb = C.banks[j].bitcast(BF16)
            for c in range(8):
                P.add("pe", (lambda pb=pb, c=c, i=i: lambda e: e.transpose(out=pb[:, c * 128:(c + 1) * 128], in_=ymix[:, i, c * 128:(c + 1) * 128],
                                                                          identity=C.ident[:]))(), (ymix_tk[i], C.t_const), (C.bk[j],))
            P.add("act", (lambda pb=pb, j=j: lambda e: e.copy(out=yT[j][:], in_=pb[:, :].rearrange("p (c t) -> p c t", c=8)))(),
                  (C.bk[j],), (yT_tk[j],))

            def mm_step(i=i, j=j):
                for half in range(2):
                    bank = 2 + 2 * j + half
                    for c in range(8):
                        P.add("pe", (lambda bank=bank, c=c, half=half: lambda e: e.matmul(
                            C.banks[bank][:, :], lhsT=yT[j][:, c, :], rhs=wo[:, c, half * 512:(half + 1) * 512], start=(c == 0), stop=(c == 7)))(),
                            (yT_tk[j], wo_tk), (C.bk[bank],))
                    P.add("dve", (lambda bank=bank, half=half: lambda e: e.tensor_tensor(
                        out=xt[j][:, half * 512:(half + 1) * 512], in0=C.banks[bank][:, :], in1=xt[j][:, half * 512:(half + 1) * 512], op=ALU.add))(),
                        (C.bk[bank], xt_tk[j]), (xt_tk[j],))
                P.add("sp", lambda e: e.dma_start(out=C.out[i * 128:(i + 1) * 128, :], in_=xt[j][:]), (xt_tk[j],), (C.xtk[i],), dma=True)
            if pend:
                pend.pop(0)()
            pend.append(mm_step)
        while pend:
            pend.pop(0)()


class Ops:
    def __init__(self, C):
        self.P = C.P
        self.C = C

    def tt(self, eng, out, in0, in1, op, rd, wr):
        self.P.add(eng, lambda e: e.tensor_tensor(out=out, in0=in0, in1=in1, op=op), rd, wr)

    def ts(self, eng, out, in0, s1, op0, rd, wr, s2=None, op1=None):
        if op1 is None:
            self.P.add(eng, lambda e: e.tensor_scalar(out=out, in0=in0, scalar1=s1, scalar2=None, op0=op0), rd, wr)
        else:
            self.P.add(eng, lambda e: e.tensor_scalar(out=out, in0=in0, scalar1=s1, scalar2=s2, op0=op0, op1=op1), rd, wr)

    def stt(self, out, in0, scalar, in1, op0, op1, rd, wr):
        self.P.add("dve", lambda e: e.scalar_tensor_tensor(out=out, in0=in0, scalar=scalar, in1=in1, op0=op0, op1=op1), rd, wr)

    def act(self, out, in_, func, rd, wr, scale=1.0, bias=None):
        if bias is None:
            self.P.add("act", lambda e: e.activation(out=out, in_=in_, func=func, scale=scale), rd, wr)
        else:
            self.P.add("act", lambda e: e.activation(out=out, in_=in_, func=func, scale=scale, bias=bias), rd, tuple(wr))

    def red(self, out, in_, rd, wr):
        self.P.add("dve", lambda e: e.tensor_reduce(out=out, in_=in_, axis=AX.X, op=ALU.add), rd, wr)

    def rcp(self, out, in_, rd, wr):
        self.P.add("dve", lambda e: e.reciprocal(out=out, in_=in_), rd, wr)

    def mm(self, out, lhsT, rhs, start, stop, rd, wr):
        mode = (lhsT.shape[0], int(np.prod(lhsT.shape[1:])))
        self.P.add("pe", lambda e: e.matmul(out, lhsT=lhsT, rhs=rhs, start=start, stop=stop, skip_group_check=True), rd, wr, mode=mode)

    def tr(self, out, in_, ident, rd, wr):
        mode = (in_.shape[0], int(np.prod(in_.shape[1:])))
        self.P.add("pe", lambda e: e.transpose(out=out, in_=in_, identity=ident), rd, wr, mode=mode)

    def dma(self, out, in_, rd, wr, eng="sp"):
        self.P.add(eng, lambda e: e.dma_start(out=out, in_=in_), rd, wr, dma=True)

    def memset(self, eng, ap, val, wr):
        self.P.add(eng, lambda e: e.memset(ap, val), (), wr)


EW = 0.6065306597126334
RWN = 1824


def rwkv_phase(C, l, src, yr, yr_tk):
    nc, P = C.nc, C.P
    O = Ops(C)
    with ExitStack() as es:
        A = lambda name, shape, dt: es.enter_context(C.sb("rw_" + name, shape, dt))
        wa = A("wa", [128, 8, RWN], BF16)
        wb = A("wb", [128, 8, RWN], BF16)
        w2b, a2b = A("w2b", [128, 512], BF16), A("a2b", [128, 512], BF16)
        g2b, g2b2 = A("g2b", [128, 512], BF16), A("g2b2", [128, 512], BF16)
        T = {}

        def tk(n):
            if n not in T:
                T[n] = Tk(n)
            return T[n]
        tcn = C.t_const
        xt_tk = [Tk(), Tk()]
        eps_ap = cconst(C, EPS)[:, 0:1]
        gneps_ap = cconst(C, 64e-5)[:, 0:1]
        with ExitStack() as es3:
            mub = es3.enter_context(C.sb("rw_mub", [128, RWN], F32))
            omm = es3.enter_context(C.sb("rw_omm", [128, RWN], F32))
            stg = [es3.enter_context(C.sb("rw_stg%d" % i, [128, RWN], F32)) for i in range(2)]
            stg_tk = [Tk(), Tk()]
            O.dma(mub[:], C.w["rwkv_mu"][l:l + 1, :].partition_broadcast(128), (), (tk("mub"),))
            O.ts("dve", omm[:], mub[:], -1.0, ALU.mult, (tk("mub"),), (tk("mub"),), s2=1.0, op1=ALU.add)
            for c in range(8):
                sb = c % 2
                O.dma(stg[sb][:], C.w["mix_w_in"][l][c * 128:(c + 1) * 128, RW0:RW0 + RWN], (), (stg_tk[sb],))
                O.tt("dve", wa[:, c, :], stg[sb][:], omm[:], ALU.mult, (stg_tk[sb], tk("mub")), (tk("wa"),))
                O.tt("pool", wb[:, c, :], stg[sb][:], mub[:], ALU.mult, (stg_tk[sb], tk("mub")), (tk("wa"),))
            for dst in (w2b, a2b, g2b2):
                O.memset("pool", dst[:], 0.0, (tk("lw2"),))
            for (dst, nm, r0, r1) in ((w2b, "rwkv_w2", 0, 64), (a2b, "rwkv_a2", 0, 64), (g2b, "rwkv_g2", 0, 128), (g2b2, "rwkv_g2", 128, 160)):
                sb = 0
                O.dma(stg[sb][0:r1 - r0, 0:512], C.w[nm][l][r0:r1, :], (), (stg_tk[sb],))
                O.P.add("dve", (lambda dst=dst, n=r1 - r0: lambda e: e.tensor_copy(out=dst[0:n, :], in_=stg[0][0:n, 0:512]))(), (stg_tk[sb],), (tk("lw2"),))
            P.barrier()
        gbc = A("gbc", [128, D], F32)
        xt = [A("xt%d" % i, [128, D], F32) for i in range(2)]
        hTx = A("hTx", [128, 8, 129], BF16)
        hT = A("hT", [128, 8, 128], BF16)
        bc = {k: A("bc_" + k, [128, 512], F32) for k in ("w0", "a0", "kk", "ka", "rk", "lnw", "lnb")}
        tw, al = A("tw", [128, 128], BF16), A("al", [128, 128], BF16)
        sg1, sg2 = A("sg1", [128, 128], BF16), A("sg2", [128, 128], BF16)
        BTm, BOm = A("BTm", [128, 128], F32), A("BOm", [128, 128], F32)
        Msu, Miu, Msl = A("Msu", [128, 8, 64], F32), A("Miu", [128, 8, 64], F32), A("Msl", [128, 8, 64], F32)
        W = {k: A("w_" + k, [128, 512], F32) for k in (
            "sw", "a", "kkn", "kmod", "b", "lc", "t1", "At", "Bt", "Kt", "Rt", "Gt", "Bh", "Kh", "Vt", "g",
            "AtT", "BtT", "KtT", "RtT", "GtT", "Q0", "P0", "Q1", "P1", "Nak", "Nrb", "Nrk", "X", "AKV", "RKV", "W0", "U", "Y", "yc", "rr", "kr")}
        H = A("H", [128, 256], F32)
        sm = {k: A("s_" + k, [128, 8], F32) for k in ("n1", "n2", "rks", "m1", "m2", "m3")}
        outb = A("outb", [128, 512], BF16)
        scr = make_norm_scratch(C, es, "rw")
        bcast_load(C, gbc[:], tk("gbc"), C.w["mix_norm"][l:l + 1, :], D)
        for k, nm in (("w0", "rwkv_w0"), ("a0", "rwkv_a0"), ("kk", "rwkv_k_k"), ("ka", "rwkv_k_a"), ("rk", "rwkv_r_k"),
                      ("lnw", "rwkv_ln_w"), ("lnb", "rwkv_ln_b")):
            O.dma(bc[k][:], C.w[nm][l:l + 1, :].partition_broadcast(128), (), (tk("bc"),))
        O.ts("pool", BTm[:], C.trif[:], -EW, ALU.mult, (tcn,), (tk("msk"),))
        O.memset("pool", BTm[0:64, 64:128], 0.0, (tk("msk"),))
        O.memset("pool", BOm[:], 0.0, (tk("msk"),))
        O.memset("pool", BOm[0:64, 0:64], -EW, (tk("msk"),))
        O.memset("pool", BOm[64:128, 64:128], -EW, (tk("msk"),))
        for M_, pat, cm, op in ((Msu, [[0, 8], [1, 64]], -1, ALU.is_gt), (Miu, [[0, 8], [1, 64]], -1, ALU.is_ge),
                                (Msl, [[0, 8], [-1, 64]], 1, ALU.is_gt)):
            O.memset("pool", M_[:], 1.0, (tk("msk"),))
            for hf in range(2):
                P.add("pool", (lambda M_=M_, pat=pat, cm=cm, op=op, hf=hf: lambda e: e.affine_select(
                    out=M_[hf * 64:(hf + 1) * 64], in_=M_[hf * 64:(hf + 1) * 64], pattern=pat, compare_op=op, fill=0.0, base=0,
                    channel_multiplier=cm))(), (tk("msk"),), (tk("msk"),))
        O.memset("pool", H[:], 0.0, (tk("H"),))
        for t_ in (tw, al, sg2):
            O.memset("pool", t_[:], 0.0, (tk("lora"),))
        O.memset("pool", hTx[:, :, 0:1], 0.0, (tk("hTx"),))
        msk = tk("msk")
        f512 = lambda t: t[:]
        v8 = lambda ap: ap.rearrange("p (g d) -> p g d", d=64)
        b8 = lambda t: t[:, 0:8].unsqueeze(2).broadcast_to([128, 8, 64])
        bki = [1]

        def nb():
            bki[0] = bki[0] % 7 + 1
            return bki[0]

        CROSS = ("Bh", "Kh", "Vt", "g", "AtT", "RtT", "GtT", "X", "Nrb", "AKV", "RKV")
        W2 = {k: A("w2_" + k, [128, 512], F32) for k in CROSS}
        rks2 = A("s_rks2", [128, 8], F32)

        def Wp(nm, par):
            return W2[nm] if (par == 1 and nm in W2) else W[nm]

        XTp = lambda nm, h, c2, par: Wp(nm, par)[(h % 2) * 64:(h % 2) * 64 + 64, (h // 2) * 128 + c2 * 64:(h // 2) * 128 + c2 * 64 + 64]
        pv = lambda ap, par_: ap.rearrange("p (q two d) -> p q two d", two=2, d=64)[:, :, par_, :]
        bct = tk("bc")

        def stageA(i):
            par = i % 2
            j = i % 2
            wk, tT, am = tk("a_wk"), tk("a_tT"), tk("a_am")
            xwk, xtT, xam = tk("x_wk%d" % par), tk("x_tT%d" % par), tk("x_am%d" % par)
            rks = rks2 if par else sm["rks"]
            O.dma(xt[j][:], src[i * 128:(i + 1) * 128, :], (C.xtk[i],), (xt_tk[j],))
            norm_transpose(C, xt[j][:], xt_tk[j], gbc[:], tk("gbc"), hT, tk("hT"), 0, scr, 0)
            P.add("pool", lambda e: e.tensor_copy(out=hTx[:, :, 1:129], in_=hT[:]), (tk("hT"),), (tk("hTx"),))
            yield
            for dst_, dtk_, c0 in ((W["rr"], wk, 0), (W["kr"], wk, 512), (Wp("Vt", par), xwk, 1024)):
                bank = nb()
                n = 0
                for c in range(8):
                    for (w_, lo) in ((wa, 1), (wb, 0)):
                        O.mm(C.banks[bank][:, :], hTx[:, c, lo:lo + 128], w_[:, c, c0:c0 + 512], n == 0, n == 15, (tk("hTx"), tk("wa")), (C.bk[bank],))
                        n += 1
                O.act(dst_[:], C.banks[bank][:, :], AF.Copy, (C.bk[bank],), (dtk_,))
                yield
            b4 = nb()
            for (r0, r1, col, c0) in ((0, 64, 0, 1536), (0, 64, 128, 1600), (0, 128, 256, 1664), (0, 32, 384, 1792)):
                n = 0
                for c in range(8):
                    for (w_, lo) in ((wa, 1), (wb, 0)):
                        O.mm(C.banks[b4][r0:r1, col:col + 128], w_[:, c, c0:c0 + (r1 - r0)], hTx[:, c, lo:lo + 128], n == 0, n == 15,
                             (tk("hTx"), tk("wa")), (C.bk[b4],))
                        n += 1
            P.add("pool", lambda e: e.tensor_copy(out=hTx[:, :, 0:1], in_=hT[:, :, 127:128]), (tk("hT"), tk("hTx")), (tk("hTx"),))
            O.act(tw[0:64, :], C.banks[b4][0:64, 0:128], AF.Tanh, (C.bk[b4],), (tk("lora"),))
            O.act(al[0:64, :], C.banks[b4][0:64, 128:256], AF.Copy, (C.bk[b4],), (tk("lora"),))
            O.act(sg1[:], C.banks[b4][0:128, 256:384], AF.Sigmoid, (C.bk[b4],), (tk("lora"),))
            O.act(sg2[0:32, :], C.banks[b4][0:32, 384:512], AF.Sigmoid, (C.bk[b4],), (tk("lora"),))
            yield
            b5, b6, b7 = nb(), nb(), nb()
            O.mm(C.banks[b5][:, :], tw[:], w2b[:], True, True, (tk("lora"), tk("lw2")), (C.bk[b5],))
            O.mm(C.banks[b6][:, :], al[:], a2b[:], True, True, (tk("lora"), tk("lw2")), (C.bk[b6],))
            O.mm(C.banks[b7][:, :], sg1[:], g2b[:], True, False, (tk("lora"), tk("lw2")), (C.bk[b7],))
            O.mm(C.banks[b7][:, :], sg2[:], g2b2[:], False, True, (tk("lora"), tk("lw2")), (C.bk[b7],))
            O.tt("dve", W["t1"][:], C.banks[b5][:, :], bc["w0"][:], ALU.add, (C.bk[b5], bct), (wk,))
            O.act(W["sw"][:], W["t1"][:], AF.Sigmoid, (wk,), (wk,))
            O.tt("dve", W["t1"][:], C.banks[b6][:, :], bc["a0"][:], ALU.add, (C.bk[b6], bct), (wk,))
            O.act(W["a"][:], W["t1"][:], AF.Sigmoid, (wk,), (wk,))
            O.act(Wp("g", par)[:], C.banks[b7][:, :], AF.Copy, (C.bk[b7],), (xwk,))
            yield
            b5, b6 = nb(), nb()
            O.mm(C.banks[b5][:, :], BTm[:], W["sw"][:], True, True, (wk, msk), (C.bk[b5],))
            O.mm(C.banks[b6][:, :], BOm[:], W["sw"][:], True, True, (wk, msk), (C.bk[b6],))
            O.act(W["lc"][:], C.banks[b5][:, :], AF.Copy, (C.bk[b5],), (wk,))
            Bh, Kh = Wp("Bh", par), Wp("Kh", par)
            O.tt("dve", Bh[:], C.banks[b6][:, :], W["lc"][:], ALU.subtract, (C.bk[b6], wk), (xwk,))
            O.act(W["Gt"][:], C.banks[b6][:, :], AF.Exp, (C.bk[b6],), (wk,))
            yield
            O.tt("dve", W["kkn"][:], W["kr"][:], bc["kk"][:], ALU.mult, (wk, bct), (wk,))
            O.act(W["t1"][:], W["kkn"][:], AF.Square, (wk,), (wk,))
            O.red(sm["n1"][:], v8(W["t1"][:]), (wk,), (wk,))
            O.act(sm["n2"][:], sm["n1"][:], AF.Sqrt, (wk,), (wk,))
            O.ts("dve", sm["n2"][:], sm["n2"][:], 1e-12, ALU.max, (wk,), (wk,))
            O.rcp(sm["n1"][:], sm["n2"][:], (wk,), (wk,))
            O.tt("dve", v8(W["kkn"][:]), v8(W["kkn"][:]), b8(sm["n1"]), ALU.mult, (wk,), (wk,))
            yield
            O.stt(W["t1"][:], W["a"][:], -1.0, bc["ka"][:], ALU.add, ALU.mult, (wk, bct), (wk,))
            O.stt(W["kmod"][:], W["t1"][:], 1.0, W["kr"][:], ALU.add, ALU.mult, (wk,), (wk,))
            O.tt("dve", W["b"][:], W["kkn"][:], W["a"][:], ALU.mult, (wk,), (wk,))
            O.tt("dve", W["t1"][:], W["rr"][:], bc["rk"][:], ALU.mult, (wk, bct), (wk,))
            O.tt("dve", W["t1"][:], W["t1"][:], W["kmod"][:], ALU.mult, (wk,), (wk,))
            O.red(rks[:], v8(W["t1"][:]), (wk,), (xwk,))
            yield
            O.stt(W["At"][:], W["sw"][:], EW, W["lc"][:], ALU.mult, ALU.add, (wk,), (wk,))
            O.act(W["At"][:], W["At"][:], AF.Exp, (wk,), (wk,))
            O.stt(W["At"][:], W["kkn"][:], -1.0, W["At"][:], ALU.mult, ALU.mult, (wk,), (wk,))
            O.act(W["Bt"][:], W["lc"][:], AF.Exp, (wk,), (wk,), scale=-1.0)
            O.tt("dve", W["Kt"][:], W["kmod"][:], W["Bt"][:], ALU.mult, (wk,), (wk,))
            O.tt("dve", W["Bt"][:], W["b"][:], W["Bt"][:], ALU.mult, (wk,), (wk,))
            yield
            O.act(W["Rt"][:], W["lc"][:], AF.Exp, (wk,), (wk,))
            O.tt("dve", W["Rt"][:], W["rr"][:], W["Rt"][:], ALU.mult, (wk,), (wk,))
            O.act(Bh[:], Bh[:], AF.Exp, (xwk,), (xwk,))
            O.tt("dve", Kh[:], W["kmod"][:], Bh[:], ALU.mult, (wk, xwk), (xwk,))
            O.tt("dve", Bh[:], W["b"][:], Bh[:], ALU.mult, (wk, xwk), (xwk,))
            yield
            for nm in ("At", "Bt", "Kt", "Rt", "Gt"):
                bank = nb()
                for blk_ in range(4):
                    O.tr(C.banks[bank][:, blk_ * 128:(blk_ + 1) * 128], W[nm][:, blk_ * 128:(blk_ + 1) * 128], C.identf[:], (wk, tcn), (C.bk[bank],))
                O.act(Wp(nm + "T", par)[:], C.banks[bank][:, :], AF.Copy, (C.bk[bank],), (xtT if nm in ("At", "Rt", "Gt") else tT,))
                yield

            def mm_d(lnm, rnm):
                bb = [nb(), nb()]
                for c2 in (0, 1):
                    for h in range(8):
                        par_ = h % 2
                        O.mm(C.banks[bb[par_]][c2 * 64:(c2 + 1) * 64, h * 64:(h + 1) * 64], XTp(lnm, h, c2, par), XTp(rnm, h, c2, par), True, True,
                             (tT, xtT), (C.bk[bb[par_]],))
                return bb

            def mm_t(l_ap, r_ap, rd):
                b_ = nb()
                for h in range(8):
                    hs = slice(h * 64, (h + 1) * 64)
                    for c2 in (0, 1):
                        ps = slice(c2 * 64, (c2 + 1) * 64)
                        O.mm(C.banks[b_][ps, hs], l_ap[ps, hs], r_ap[ps, hs], True, True, rd, (C.bk[b_],))
                return b_

            for (dst, lnm, rnm, msk_) in (("Q0", "BtT", "AtT", Msu), ("P0", "AtT", "BtT", Msl), ("Nak", "KtT", "AtT", Msu),
                                          ("Nrb", "BtT", "RtT", Miu), ("Nrk", "KtT", "RtT", Miu)):
                bb = mm_d(lnm, rnm)
                mflat = msk_[:].rearrange("p g d -> p (g d)")
                for par_ in range(2):
                    O.tt("dve", pv(Wp(dst, par)[:], par_), pv(C.banks[bb[par_]][:, :], par_), pv(mflat, par_), ALU.mult, (C.bk[bb[par_]], msk),
                         (xam if dst == "Nrb" else am,))
                yield
            for (dst, nm) in (("AKV", "Nak"), ("RKV", "Nrk")):
                b_ = mm_t(W[nm], Wp("Vt", par), (am, xwk))
                O.act(Wp(dst, par)[:], C.banks[b_][:, :], AF.Copy, (C.bk[b_],), (xam,))
                yield
            X = Wp("X", par)
            O.tt("dve", X[:], Miu[:].rearrange("p g d -> p (g d)"), Msu[:].rearrange("p g d -> p (g d)"), ALU.subtract, (msk,), (xam,))
            O.tt("dve", X[:], X[:], W["Q0"][:], ALU.add, (am, xam), (xam,))
            Pc, Qc, Pn, Qn = "P0", "Q0", "P1", "Q1"
            for it in range(5):
                bP = mm_t(W[Qc], W[Pc], (am,))
                if it < 4:
                    bQ = mm_t(W[Pc], W[Qc], (am,))
                O.act(W[Pn][:], C.banks[bP][:, :], AF.Copy, (C.bk[bP],), (am,))
                if it < 4:
                    P.add("dve", (lambda Qn=Qn, bQ=bQ: lambda e: e.tensor_copy(out=W[Qn][:], in_=C.banks[bQ][:, :]))(), (C.bk[bQ],), (am,))
                yield
                bX = mm_t(W[Pn], X, (am, xam))
                O.tt("dve", X[:], C.banks[bX][:, :], X[:], ALU.add, (C.bk[bX], xam), (xam,))
                Pc, Qc, Pn, Qn = Pn, Qn, Pc, Qc
                yield

        def stageB(i):
            par = i % 2
            xwk, xtT, xam = tk("x_wk%d" % par), tk("x_tT%d" % par), tk("x_am%d" % par)
            rks = rks2 if par else sm["rks"]
            ch, Ht, yt, ot = tk("b_ch"), tk("H"), tk("b_Y"), tk("b_ot")
            X, Nrb, AKV, RKV = Wp("X", par), Wp("Nrb", par), Wp("AKV", par), Wp("RKV", par)
            Bh, Kh, Vt, g_ = Wp("Bh", par), Wp("Kh", par), Wp("Vt", par), Wp("g", par)
            for c2 in range(2):
                ps = slice(c2 * 64, (c2 + 1) * 64)
                bW, bY1 = [nb(), nb()], [nb(), nb()]
                for (bb, nm) in ((bW, "AtT"), (bY1, "RtT")):
                    for h in range(8):
                        par_ = h % 2
                        pr = slice((h % 2) * 64, (h % 2) * 64 + 64)
                        O.mm(C.banks[bb[par_]][ps, h * 64:(h + 1) * 64], XTp(nm, h, c2, par), H[pr, (h // 2) * 64:(h // 2) * 64 + 64], True, True,
                             (xtT, Ht), (C.bk[bb[par_]],))
                for par_ in range(2):
                    O.tt("dve", pv(W["W0"][ps, :], par_), pv(C.banks[bW[par_]][ps, :], par_), pv(AKV[ps, :], par_), ALU.add,
                         (C.bk[bW[par_]], xam), (ch,))
                for par_ in range(2):
                    O.tt("dve", pv(W["Y"][ps, :], par_), pv(C.banks[bY1[par_]][ps, :], par_), pv(RKV[ps, :], par_), ALU.add,
                         (C.bk[bY1[par_]], xam), (yt,))
                yield
                bU = nb()
                for h in range(8):
                    hs = slice(h * 64, (h + 1) * 64)
                    O.mm(C.banks[bU][ps, hs], X[ps, hs], W["W0"][ps, hs], True, True, (xam, ch), (C.bk[bU],))
                O.act(W["U"][ps, :], C.banks[bU][ps, :], AF.Copy, (C.bk[bU],), (ch,))
                yield
                bY2, bH = nb(), nb()
                for h in range(8):
                    hs = slice(h * 64, (h + 1) * 64)
                    O.mm(C.banks[bY2][ps, hs], Nrb[ps, hs], W["U"][ps, hs], True, True, (xam, ch), (C.bk[bY2],))
                for h in range(8):
                    hs = slice(h * 64, (h + 1) * 64)
                    pr = slice((h % 2) * 64, (h % 2) * 64 + 64)
                    ho = C.banks[bH][pr, (h // 2) * 64:(h // 2) * 64 + 64]
                    O.mm(ho, Bh[ps, hs], W["U"][ps, hs], True, False, (xwk, ch), (C.bk[bH],))
                    O.mm(ho, Kh[ps, hs], Vt[ps, hs], False, True, (xwk,), (C.bk[bH],))
                O.tt("dve", W["Y"][ps, :], C.banks[bY2][ps, :], W["Y"][ps, :], ALU.add, (C.bk[bY2], yt), (yt,))
                GtT = Wp("GtT", par)
                for pair in range(4):
                    cs = slice(pair * 64, (pair + 1) * 64)
                    O.stt(H[:, cs], H[:, cs], GtT[:, pair * 128 + c2 * 64:pair * 128 + c2 * 64 + 1], C.banks[bH][:, cs], ALU.mult, ALU.add,
                          (Ht, xtT, C.bk[bH]), (Ht,))
                yield
            tmp = W["W0"]
            O.red(sm["m1"][:], v8(W["Y"][:]), (yt,), (ot,))
            O.ts("dve", sm["m1"][:], sm["m1"][:], -1.0 / 64, ALU.mult, (ot,), (ot,))
            O.tt("dve", v8(W["yc"][:]), v8(W["Y"][:]), b8(sm["m1"]), ALU.add, (yt, ot), (ot,))
            O.act(tmp[:], W["yc"][:], AF.Square, (ot, ch), (ch,))
            O.red(sm["m2"][:], v8(tmp[:]), (ch,), (ot,))
            O.act(sm["m3"][:], sm["m2"][:], AF.Sqrt, (ot, tcn), (ot,), scale=1.0 / 64, bias=gneps_ap)
            O.rcp(sm["m2"][:], sm["m3"][:], (ot,), (ot,))
            yield
            O.tt("dve", v8(W["yc"][:]), v8(W["yc"][:]), b8(sm["m2"]), ALU.mult, (ot,), (ot,))
            O.tt("dve", W["yc"][:], W["yc"][:], bc["lnw"][:], ALU.mult, (ot, bct), (ot,))
            O.tt("dve", W["yc"][:], W["yc"][:], bc["lnb"][:], ALU.add, (ot, bct), (ot,))
            O.tt("dve", v8(tmp[:]), v8(Vt[:]), rks[:, 0:8].unsqueeze(2).broadcast_to([128, 8, 64]), ALU.mult, (xwk, ch), (ch,))
            O.tt("dve", W["yc"][:], W["yc"][:], tmp[:], ALU.add, (ot, ch), (ot,))
            O.tt("dve", outb[:], W["yc"][:], g_[:], ALU.mult, (ot, xwk), (tk("outb"),))
            O.dma(yr[i * 128:(i + 1) * 128, :], outb[:], (tk("outb"),), (yr_tk,))
            yield

        ntl = C.cfg.get("rw_tiles", NT)
        ratio = C.cfg.get("rw_ratio", 2)
        EARLY = C.cfg.get("rw_early", 7)
        START = C.cfg.get("rw_start", 22)

        def step(g):
            try:
                next(g)
                return True
            except StopIteration:
                return False

        gA_next, early_done = None, 0
        for i in range(ntl + 1):
            if i < ntl:
                if gA_next is not None:
                    gA, done_A = gA_next, early_done
                else:
                    gA, done_A = stageA(i), 0
            else:
                gA, done_A = None, 0
            gA_next, early_done = (stageA(i + 1), 0) if (i + 1 < ntl and EARLY > 0) else (None, 0)
            gB = stageB(i - 1) if i >= 1 else None
            while gA is not None or gB is not None:
                if gB is not None and not step(gB):
                    gB = None
                for _ in range(ratio):
                    if gA is not None:
                        if step(gA):
                            done_A += 1
                        else:
                            gA = None
                if gA is not None and gA_next is not None and done_A >= START and early_done < EARLY:
                    step(gA_next)
                    early_done += 1
        P.barrier()


def xattn_phase(C, l, src):
    nc, P = C.nc, C.P
    O = Ops(C)
    with ExitStack() as es:
        A = lambda name, shape, dt: es.enter_context(C.sb("xa_" + name, shape, dt))
        wq = A("wq", [128, 8, 512], BF16)
        wkv = A("wkv", [128, 8, 1024], BF16)
        wo = A("wo", [128, 4, D], BF16)
        stg = [A("stg%d" % i, [128, 1024], F32) for i in range(2)]
        gbc = A("gbc", [128, D], F32)
        gmb = A("gmb", [128, D], F32)
        gq = A("gq", [128, 512], F32)
        gk = A("gk", [128, 512], F32)
        xt = [A("xt%d" % i, [128, D], F32) for i in range(2)]
        hT = A("hT", [128, 8, 128], BF16)
        kT = A("kT", [128, 4, 256], BF16)
        vaug = A("vaug", [128, 2, 4, 129], BF16)
        sq = A("sq", [128, 512], F32)
        qn = A("qn", [128, 512], F32)
        qb = A("qb", [128, 512], BF16)
        qT = A("qT", [128, 4, 128], BF16)
        pt = [A("pt%d" % i, [128, 512], BF16) for i in range(2)]
        ob = A("ob", [128, 512], BF16)
        oT = A("oT", [128, 4, 128], BF16)
        st = {k: A("s_" + k, [128, 4], F32) for k in ("ss", "sd", "rs", "rec")}
        scr = make_norm_scratch(C, es, "xa")
        T = {}

        def tk(n):
            if n not in T:
                T[n] = Tk(n)
            return T[n]
        tcn = C.t_const
        stg_tk = [Tk(), Tk()]
        xt_tk = [Tk(), Tk()]
        eps_ap = cconst(C, EPS)[:, 0:1]
        zero_ap = cconst(C, 0.0)[:, 0:1]
        load_cast(C, wq, tk("wq"), C.w["xattn_w_q"][l], D, 512, stg, stg_tk, piece=512)
        load_cast(C, wkv, tk("wkv"), C.w["xattn_w_kv"][l], D, 1024, stg, stg_tk, piece=1024)
        load_cast(C, wo, tk("wo"), C.w["xattn_w_out"][l], 512, D, stg, stg_tk, piece=1024)
        bcast_load(C, gbc[:], tk("g"), C.w["xattn_norm"][l:l + 1, :], D)
        bcast_load(C, gmb[:], tk("g"), C.w["xattn_mem_norm"][l:l + 1, :], D)
        O.dma(gq[:].rearrange("p (g d) -> p g d", d=128), C.w["xattn_q_gain"][l:l + 1, :].partition_broadcast(128).broadcast_to([128, 4, 128]), (), (tk("g"),))
        O.dma(gk[:].rearrange("p (g d) -> p g d", d=128), C.w["xattn_k_gain"][l:l + 1, :].partition_broadcast(128).broadcast_to([128, 4, 128]), (), (tk("g"),))
        O.ts("dve", gq[:], gq[:], float(128 ** -0.5), ALU.mult, (tk("g"),), (tk("g"),))
        O.memset("pool", vaug[:, :, :, 128:129], 1.0, (tk("vaug"),))
        v4 = lambda ap: ap.rearrange("p (g d) -> p g d", d=128)
        b4 = lambda t: t[:, 0:4].unsqueeze(2).broadcast_to([128, 4, 128])

        def gnorm(src_psum, src_tk, gains, dst):
            O.act(sq[:], src_psum, AF.Square, (src_tk,), (tk("sq"),))
            O.red(st["ss"][:], v4(sq[:]), (tk("sq"),), (tk("st"),))
            O.act(st["sd"][:], st["ss"][:], AF.Sqrt, (tk("st"), tcn), (tk("st"),), scale=1.0 / 128, bias=eps_ap)
            O.rcp(st["rs"][:], st["sd"][:], (tk("st"),), (tk("st"),))
            O.tt("dve", v4(qn[:]), v4(src_psum), b4(st["rs"]), ALU.mult, (src_tk, tk("st")), (tk("qn"),))
            O.tt("dve", dst, qn[:], gains[:], ALU.mult, (tk("qn"), tk("g")), (tk("qb"),))

        for mt in range(2):
            O.dma(xt[mt][:], C.mem[mt * 128:(mt + 1) * 128, :], (), (xt_tk[mt],))
            norm_transpose(C, xt[mt][:], xt_tk[mt], gmb[:], tk("g"), hT, tk("hT"), 0, scr, 0)
            for half in range(2):
                for c in range(8):
                    O.mm(C.banks[1 + half][:, :], hT[:, c, :], wkv[:, c, half * 512:(half + 1) * 512], c == 0, c == 7, (tk("hT"), tk("wkv")), (C.bk[1 + half],))
            gnorm(C.banks[1][:, :], C.bk[1], gk, qb[:])
            O.act(vaug[:, mt, :, 0:128], v4(C.banks[2][:, :]), AF.Copy, (C.bk[2],), (tk("vaug"),))
            pb = C.banks[3].bitcast(BF16)
            for h in range(4):
                O.tr(pb[:, h * 128:(h + 1) * 128], qb[:, h * 128:(h + 1) * 128], C.ident[:], (tk("qb"), tcn), (C.bk[3],))
            O.act(kT[:, :, mt * 128:(mt + 1) * 128], pb[:, 0:512].rearrange("p (h t) -> p h t", h=4), AF.Copy, (C.bk[3],), (tk("kT"),))
        P.barrier()
        dup = {}
        for par in range(2):
            d_ = {}
            d_["hT"] = A("hT_%d" % par, [128, 8, 128], BF16)
            d_["sq"] = A("sq_%d" % par, [128, 512], F32)
            d_["qn"] = A("qn_%d" % par, [128, 512], F32)
            d_["qb"] = A("qb_%d" % par, [128, 512], BF16)
            d_["qT"] = A("qT_%d" % par, [128, 4, 128], BF16)
            d_["pt"] = [A("pt%d_%d" % (k_, par), [128, 512], BF16) for k_ in range(2)]
            d_["ob"] = A("ob_%d" % par, [128, 512], BF16)
            d_["oT"] = A("oT_%d" % par, [128, 4, 128], BF16)
            d_["st"] = {k_: A("s%s_%d" % (k_, par), [128, 4], F32) for k_ in ("ss", "sd", "rs", "rec")}
            d_["scr"] = make_norm_scratch(C, es, "xa%d" % par)
            dup[par] = d_

        def tile_gen(i):
            par = i % 2
            d_ = dup[par]
            B = 4 * par
            hT_, sq_, qn_, qb_, qT_, pt_, ob_, oT_, st_ = d_["hT"], d_["sq"], d_["qn"], d_["qb"], d_["qT"], d_["pt"], d_["ob"], d_["oT"], d_["st"]
            t = lambda n: tk("%s_p%d" % (n, par))
            j = par
            O.dma(xt[j][:], src[i * 128:(i + 1) * 128, :], (C.xtk[i],), (xt_tk[j],))
            norm_transpose(C, xt[j][:], xt_tk[j], gbc[:], tk("g"), hT_, t("hT"), 0, d_["scr"], B + 0)
            yield
            for c in range(8):
                O.mm(C.banks[B + 1][:, :], hT_[:, c, :], wq[:, c, :], c == 0, c == 7, (t("hT"), tk("wq")), (C.bk[B + 1],))
            yield
            O.act(sq_[:], C.banks[B + 1][:, :], AF.Square, (C.bk[B + 1],), (t("sq"),))
            O.red(st_["ss"][:], v4(sq_[:]), (t("sq"),), (t("st"),))
            yield
            O.act(st_["sd"][:], st_["ss"][:], AF.Sqrt, (t("st"), tcn), (t("st"),), scale=1.0 / 128, bias=eps_ap)
            O.rcp(st_["rs"][:], st_["sd"][:], (t("st"),), (t("st"),))
            yield
            O.tt("dve", v4(qn_[:]), v4(C.banks[B + 1][:, :]), b4(st_["rs"]), ALU.mult, (C.bk[B + 1], t("st")), (t("qn"),))
            O.tt("dve", qb_[:], qn_[:], gq[:], ALU.mult, (t("qn"), tk("g")), (t("qb"),))
            yield
            pb = C.banks[B + 2].bitcast(BF16)
            for h in range(4):
                O.tr(pb[:, h * 128:(h + 1) * 128], qb_[:, h * 128:(h + 1) * 128], C.ident[:], (t("qb"), tcn), (C.bk[B + 2],))
            O.act(qT_[:], pb[:, 0:512].rearrange("p (h t) -> p h t", h=4), AF.Copy, (C.bk[B + 2],), (t("qT"),))
            yield
            for mt in range(2):
                bs = B + 1 + mt
                for h in range(4):
                    O.mm(C.banks[bs][:, h * 128:(h + 1) * 128], kT[:, h, mt * 128:(mt + 1) * 128], qT_[:, h, :], True, True, (tk("kT"), t("qT")), (C.bk[bs],))
                P.add("act", (lambda mt=mt, bs=bs: lambda e: e.activation(out=pt_[mt][:], in_=C.banks[bs][:, :], func=AF.Exp, bias=zero_ap))(),
                      (C.bk[bs], tcn), (t("pt%d" % mt),))
                yield
            oslot = lambda h: (B + 3, h * 129) if h < 3 else (B + 0, 0)
            for h in range(4):
                bank, ocol = oslot(h)
                for mt in range(2):
                    O.mm(C.banks[bank][:, ocol:ocol + 129], pt_[mt][:, h * 128:(h + 1) * 128], vaug[:, mt, h, :], mt == 0, mt == 1,
                         (t("pt%d" % mt), tk("vaug")), (C.bk[bank],))
            yield
            for h in range(4):
                bank, ocol = oslot(h)
                O.rcp(st_["rec"][:, h:h + 1], C.banks[bank][:, ocol + 128:ocol + 129], (C.bk[bank],), (t("rec"),))
                O.ts("dve", ob_[:, h * 128:(h + 1) * 128], C.banks[bank][:, ocol:ocol + 128], st_["rec"][:, h:h + 1], ALU.mult, (C.bk[bank], t("rec")), (t("ob"),))
                if h % 2 == 1:
                    yield
            pb = C.banks[B + 1].bitcast(BF16)
            for h in range(4):
                O.tr(pb[:, h * 128:(h + 1) * 128], ob_[:, h * 128:(h + 1) * 128], C.ident[:], (t("ob"), tcn), (C.bk[B + 1],))
            O.act(oT_[:], pb[:, 0:512].rearrange("p (h t) -> p h t", h=4), AF.Copy, (C.bk[B + 1],), (t("oT"),))
            yield
            for half in range(2):
                bank = B + 2 + half
                for c in range(4):
                    O.mm(C.banks[bank][:, :], oT_[:, c, :], wo[:, c, half * 512:(half + 1) * 512], c == 0, c == 3, (t("oT"), tk("wo")), (C.bk[bank],))
                O.tt("dve", xt[j][:, half * 512:(half + 1) * 512], C.banks[bank][:, :], xt[j][:, half * 512:(half + 1) * 512], ALU.add,
                     (C.bk[bank], xt_tk[j]), (xt_tk[j],))
                yield
            O.dma(C.out[i * 128:(i + 1) * 128, :], xt[j][:], (xt_tk[j],), (C.xtk[i],))
            yield

        for m in range(NT // 2):
            gens = [tile_gen(2 * m), tile_gen(2 * m + 1)]
            while gens:
                for g_ in list(gens):
                    try:
                        next(g_)
                    except StopIteration:
                        gens.remove(g_)
        P.barrier()


_NC_CACHE = {}

LAUNCHES = (
    ("ffn1", {"plan": (("ffn1", 0),), "LW": 1}),
    ("rwkv", {"plan": (("rwkv", 0),), "LW": 1, "yr_kind": "ExternalOutput"}),
    ("attn", {"plan": (("attn", 0),), "LW": 1, "yr_kind": "ExternalInput"}),
    ("xattn", {"plan": (("xattn", 0),), "LW": 1}),
    ("ffn2", {"plan": (("ffn2", 0),), "LW": 1}),
)


def get_nc(name, cfg):
    if name not in _NC_CACHE:
        nc = bass.Bass("TRN2", target_bir_lowering=False)
        build(nc, dict(cfg))
        _NC_CACHE[name] = nc
    return _NC_CACHE[name]


def make_in_maps(inputs, cores, l=None, extra=None):
    maps = []
    for b in cores:
        m = {}
        for k, v in inputs.items():
            v = np.asarray(v)
            if k in ("x", "mem"):
                m[k] = np.ascontiguousarray(v[b])
            elif k == "positions":
                m[k] = np.ascontiguousarray(v[b:b + 1]).astype(np.int32)
            else:
                if k == "rwkv_r_k":
                    v = v.reshape(L, 512)
                if l is not None:
                    v = v[l:l + 1]
                m[k] = np.ascontiguousarray(v)
        if extra:
            for k, v in extra.items():
                m[k] = np.ascontiguousarray(v[b])
        maps.append(m)
    return maps


FUSED = True


def kernel(**inputs):
    cores = list(range(8))
    inp = dict(inputs)
    if FUSED:
        nc = get_nc("fused", {})
        res = run_bass_kernel_spmd(nc, make_in_maps(inp, cores), core_ids=cores)
        return np.stack([np.asarray(r["out"]) for r in res.results], axis=0).astype(np.float32)
    x = np.asarray(inp["x"])
    for l in range(L):
        yr = None
        for name, cfg in LAUNCHES:
            nc = get_nc(name, cfg)
            inp["x"] = x
            extra = {"yr": yr} if name == "attn" else None
            res = run_bass_kernel_spmd(nc, make_in_maps(inp, cores, l=l, extra=extra), core_ids=cores)
            if name == "rwkv":
                yr = np.stack([np.asarray(r["yr"]) for r in res.results], axis=0)
            else:
                x = np.stack([np.asarray(r["out"]) for r in res.results], axis=0).astype(np.float32)
    return x
```
